# Optimizing a Trainium2 kernel written in Bass

```python
import math
import jax, jax.numpy as jnp
from jax import lax
import numpy as np

D_MODEL = 1024
BATCH = 8
SEQ = 2048
DEPTH = 2
DEC_BATCH = 128
DEC_SEQ = 4
PAST_LEN = 16384
PAGE_SIZE = 128

N_MIXERS = 2
N_SSM_LAYERS = (DEPTH + 1) // 2
N_CONV_LAYERS = DEPTH // 2
SSM_GROUP = 16
SSM_GROUPS = D_MODEL // SSM_GROUP
SSM_STATE = 64
SSM_DT_MIN = 1e-3
SSM_DT_MAX = 1e-1
CONV_WIDTH = 3
MEM_TOKENS = 256
MEM_HEADS = 4
MEM_HEAD_DIM = D_MODEL // MEM_HEADS
PEER_HEADS = 8
PEER_KEYS = 128
PEER_EXPERTS = PEER_KEYS * PEER_KEYS
PEER_TOPK = 16
PEER_QUERY_DIM = 256
PEER_HALF = PEER_QUERY_DIM // 2
PEER_BLOCK = 256
RMS_EPS = 1e-6

kernel_name = 'hybrid_s5_shortconv_peer_memxattn_step'


def rmsnorm(x, g):
    xf = x.astype(jnp.float32)
    r = lax.rsqrt(jnp.mean(xf * xf, axis=-1, keepdims=True) + RMS_EPS)
    return (xf * r).astype(x.dtype) * g


def _cmul_combine(e1, e2):
    ar1, ai1, br1, bi1 = e1
    ar2, ai2, br2, bi2 = e2
    return (ar1 * ar2 - ai1 * ai2,
            ar1 * ai2 + ai1 * ar2,
            ar2 * br1 - ai2 * bi1 + br2,
            ar2 * bi1 + ai2 * br1 + bi2)


def s5_mixer(h, s0_re, s0_im, a_re, a_im, log_dt, b_re, b_im, c_re, c_im, d_skip, w_glu):
    f32 = jnp.float32
    bsz, s, _ = h.shape
    a_re = a_re.astype(f32)
    a_im = a_im.astype(f32)
    dt = jnp.exp(log_dt.astype(f32))[:, None]
    mag = jnp.exp(a_re * dt)
    ang = a_im * dt
    lb_re = mag * jnp.cos(ang)
    lb_im = mag * jnp.sin(ang)
    den = a_re * a_re + a_im * a_im
    f_re = ((lb_re - 1.0) * a_re + lb_im * a_im) / den
    f_im = (lb_im * a_re - (lb_re - 1.0) * a_im) / den
    b_re = b_re.astype(f32)
    b_im = b_im.astype(f32)
    bb_re = f_re[..., None] * b_re - f_im[..., None] * b_im
    bb_im = f_re[..., None] * b_im + f_im[..., None] * b_re
    u = h.astype(f32).reshape(bsz, s, SSM_GROUPS, SSM_GROUP)
    bu_re = jnp.einsum('bsgc,gpc->bsgp', u, bb_re)
    bu_im = jnp.einsum('bsgc,gpc->bsgp', u, bb_im)
    s0_re = s0_re.astype(f32)
    s0_im = s0_im.astype(f32)
    bu_re = bu_re.at[:, 0].add(lb_re * s0_re - lb_im * s0_im)
    bu_im = bu_im.at[:, 0].add(lb_re * s0_im + lb_im * s0_re)
    a_full_re = jnp.broadcast_to(lb_re, bu_re.shape)
    a_full_im = jnp.broadcast_to(lb_im, bu_im.shape)
    _, _, st_re, st_im = lax.associative_scan(
        _cmul_combine, (a_full_re, a_full_im, bu_re, bu_im), axis=1)
    c_re = c_re.astype(f32)
    c_im = c_im.astype(f32)
    y = (jnp.einsum('bsgp,gcp->bsgc', st_re, c_re)
         - jnp.einsum('bsgp,gcp->bsgc', st_im, c_im)).reshape(bsz, s, D_MODEL)
    y = y + d_skip.astype(f32) * h.astype(f32)
    z = jax.nn.gelu(y, approximate=False).astype(h.dtype)
    g_a, g_b = jnp.split(z @ w_glu, 2, axis=-1)
    return g_a * jax.nn.sigmoid(g_b), st_re[:, -1], st_im[:, -1]


def short_conv_mixer(h, buf, w_in, w_conv, w_out):
    s = h.shape[1]
    b_gate, c_gate, hv = jnp.split(h @ w_in, 3, axis=-1)
    v = c_gate * hv
    vp = jnp.concatenate([buf.astype(v.dtype), v], axis=1)
    conv = w_conv[0] * vp[:, 0:s]
    for k in range(1, CONV_WIDTH):
        conv = conv + w_conv[k] * vp[:, k:k + s]
    out = (b_gate * conv) @ w_out
    return out, vp[:, -(CONV_WIDTH - 1):]


def memory_kv(mem, w_k, w_v):
    bsz = mem.shape[0]
    k = jnp.einsum('bmd,ldf->lbmf', mem, w_k).reshape(DEPTH, bsz, MEM_TOKENS, MEM_HEADS, MEM_HEAD_DIM)
    v = jnp.einsum('bmd,ldf->lbmf', mem, w_v).reshape(DEPTH, bsz, MEM_TOKENS, MEM_HEADS, MEM_HEAD_DIM)
    return k, v


def memory_attention(h, mem_k, mem_v, w_q, w_o):
    bsz, s, _ = h.shape
    q = (h @ w_q).reshape(bsz, s, MEM_HEADS, MEM_HEAD_DIM)
    sc = jnp.einsum('bshd,bmhd->bhsm', q, mem_k).astype(jnp.float32) * (MEM_HEAD_DIM ** -0.5)
    p = jax.nn.softmax(sc, axis=-1).astype(h.dtype)
    o = jnp.einsum('bhsm,bmhd->bshd', p, mem_v).reshape(bsz, s, D_MODEL)
    return o @ w_o


def peer_ffn(h, w_query, key1, key2, u_tab, v_tab):
    bsz, s, d = h.shape
    t = bsz * s
    nb = -(-t // PEER_BLOCK)
    flat = jnp.pad(h.reshape(t, d), ((0, nb * PEER_BLOCK - t), (0, 0)))

    def block(hb):
        q = (hb @ w_query).reshape(PEER_BLOCK, PEER_HEADS, 2, PEER_HALF)
        s1 = jnp.einsum('thk,nk->thn', q[:, :, 0], key1)
        s2 = jnp.einsum('thk,nk->thn', q[:, :, 1], key2)
        v1, i1 = lax.top_k(s1, PEER_TOPK)
        v2, i2 = lax.top_k(s2, PEER_TOPK)
        cand = (v1[..., :, None] + v2[..., None, :]).reshape(PEER_BLOCK, PEER_HEADS, PEER_TOPK * PEER_TOPK)
        sc, ci = lax.top_k(cand, PEER_TOPK)
        e = (jnp.take_along_axis(i1, ci // PEER_TOPK, axis=-1) * PEER_KEYS
             + jnp.take_along_axis(i2, ci % PEER_TOPK, axis=-1))
        g = jax.nn.softmax(sc.astype(jnp.float32), axis=-1).astype(hb.dtype)
        act = jax.nn.gelu(jnp.einsum('thkd,td->thk', u_tab[e], hb), approximate=False)
        return jnp.einsum('thk,thkd->td', g * act, v_tab[e])

    out = lax.map(block, flat.reshape(nb, PEER_BLOCK, d))
    return out.reshape(nb * PEER_BLOCK, d)[:t].reshape(bsz, s, d)


def trunk(x, ssm_re0, ssm_im0, conv0, mem_k, mem_v, weights):
    (norm_mix, norm_mem, norm_ffn, norm_final,
     ssm_a_re, ssm_a_im, ssm_log_dt, ssm_b_re, ssm_b_im, ssm_c_re, ssm_c_im, ssm_d, ssm_w_glu,
     conv_w_in, conv_w, conv_w_out,
     mem_w_q, mem_w_o,
     peer_w_query, peer_key1, peer_key2, peer_u, peer_v) = weights
    ssm_re_out, ssm_im_out, conv_out = [], [], []
    for i in range(DEPTH):
        j = i // N_MIXERS
        h = rmsnorm(x, norm_mix[i])
        if i % N_MIXERS == 0:
            out, sr, si = s5_mixer(h, ssm_re0[j], ssm_im0[j], ssm_a_re[j], ssm_a_im[j], ssm_log_dt[j],
                                   ssm_b_re[j], ssm_b_im[j], ssm_c_re[j], ssm_c_im[j], ssm_d[j], ssm_w_glu[j])
            ssm_re_out.append(sr)
            ssm_im_out.append(si)
        else:
            out, cb = short_conv_mixer(h, conv0[j], conv_w_in[j], conv_w[j], conv_w_out[j])
            conv_out.append(cb)
        x = x + out
        x = x + memory_attention(rmsnorm(x, norm_mem[i]), mem_k[i], mem_v[i], mem_w_q[i], mem_w_o[i])
        x = x + peer_ffn(rmsnorm(x, norm_ffn[i]), peer_w_query[i], peer_key1[i], peer_key2[i],
                         peer_u[i], peer_v[i])
    y = rmsnorm(x, norm_final)
    return y, jnp.stack(ssm_re_out), jnp.stack(ssm_im_out), jnp.stack(conv_out)


def setup_inputs(seed: int = 0) -> dict:
    key = jax.random.key(seed)
    ks = jax.random.split(key, 40)
    f32 = jnp.float32
    nrm = lambda i, shape, scale: jax.random.normal(ks[i], shape, f32) * scale
    inp = {}
    inp['x_prompt'] = nrm(0, (BATCH, SEQ, D_MODEL), 1.0)
    inp['x_sample'] = nrm(1, (DEC_BATCH, DEC_SEQ, D_MODEL), 1.0)
    inp['mem_prompt'] = nrm(2, (BATCH, MEM_TOKENS, D_MODEL), 1.0)
    inp['state_ssm_re'] = nrm(3, (N_SSM_LAYERS, DEC_BATCH, SSM_GROUPS, SSM_STATE), 0.1)
    inp['state_ssm_im'] = nrm(4, (N_SSM_LAYERS, DEC_BATCH, SSM_GROUPS, SSM_STATE), 0.1)
    inp['state_conv'] = nrm(5, (N_CONV_LAYERS, DEC_BATCH, CONV_WIDTH - 1, D_MODEL), 1.0)
    inp['cache_mem_k'] = nrm(6, (DEPTH, DEC_BATCH, MEM_TOKENS, MEM_HEADS, MEM_HEAD_DIM), 1.0)
    inp['cache_mem_v'] = nrm(7, (DEPTH, DEC_BATCH, MEM_TOKENS, MEM_HEADS, MEM_HEAD_DIM), 1.0)
    inp['norm_mix'] = 1.0 + nrm(8, (DEPTH, D_MODEL), 0.01)
    inp['norm_mem'] = 1.0 + nrm(9, (DEPTH, D_MODEL), 0.01)
    inp['norm_ffn'] = 1.0 + nrm(10, (DEPTH, D_MODEL), 0.01)
    inp['norm_final'] = 1.0 + nrm(11, (D_MODEL,), 0.01)
    inp['ssm_a_re'] = -0.5 + nrm(12, (N_SSM_LAYERS, SSM_GROUPS, SSM_STATE), 0.01)
    inp['ssm_a_im'] = jnp.pi * jnp.arange(SSM_STATE, dtype=f32) + nrm(13, (N_SSM_LAYERS, SSM_GROUPS, SSM_STATE), 0.01)
    inp['ssm_log_dt'] = jax.random.uniform(ks[14], (N_SSM_LAYERS, SSM_GROUPS), f32,
                                           math.log(SSM_DT_MIN), math.log(SSM_DT_MAX))
    inp['ssm_b_re'] = nrm(15, (N_SSM_LAYERS, SSM_GROUPS, SSM_STATE, SSM_GROUP), (2 * SSM_GROUP) ** -0.5)
    inp['ssm_b_im'] = nrm(16, (N_SSM_LAYERS, SSM_GROUPS, SSM_STATE, SSM_GROUP), (2 * SSM_GROUP) ** -0.5)
    inp['ssm_c_re'] = nrm(17, (N_SSM_LAYERS, SSM_GROUPS, SSM_GROUP, SSM_STATE), SSM_STATE ** -0.5)
    inp['ssm_c_im'] = nrm(18, (N_SSM_LAYERS, SSM_GROUPS, SSM_GROUP, SSM_STATE), SSM_STATE ** -0.5)
    inp['ssm_d'] = nrm(19, (N_SSM_LAYERS, D_MODEL), 1.0)
    inp['ssm_w_glu'] = nrm(20, (N_SSM_LAYERS, D_MODEL, 2 * D_MODEL), D_MODEL ** -0.5)
    inp['conv_w_in'] = nrm(21, (N_CONV_LAYERS, D_MODEL, 3 * D_MODEL), D_MODEL ** -0.5)
    inp['conv_w'] = nrm(22, (N_CONV_LAYERS, CONV_WIDTH, D_MODEL), CONV_WIDTH ** -0.5)
    inp['conv_w_out'] = nrm(23, (N_CONV_LAYERS, D_MODEL, D_MODEL), D_MODEL ** -0.5)
    inp['mem_w_q'] = nrm(24, (DEPTH, D_MODEL, D_MODEL), D_MODEL ** -0.5)
    inp['mem_w_k'] = nrm(25, (DEPTH, D_MODEL, D_MODEL), D_MODEL ** -0.5)
    inp['mem_w_v'] = nrm(26, (DEPTH, D_MODEL, D_MODEL), D_MODEL ** -0.5)
    inp['mem_w_o'] = nrm(27, (DEPTH, D_MODEL, D_MODEL), D_MODEL ** -0.5)
    inp['peer_w_query'] = nrm(28, (DEPTH, D_MODEL, PEER_HEADS * PEER_QUERY_DIM), D_MODEL ** -0.5)
    inp['peer_key1'] = nrm(29, (DEPTH, PEER_KEYS, PEER_HALF), PEER_HALF ** -0.5)
    inp['peer_key2'] = nrm(30, (DEPTH, PEER_KEYS, PEER_HALF), PEER_HALF ** -0.5)
    inp['peer_u'] = nrm(31, (DEPTH, PEER_EXPERTS, D_MODEL), D_MODEL ** -0.5)
    inp['peer_v'] = nrm(32, (DEPTH, PEER_EXPERTS, D_MODEL), (PEER_HEADS * PEER_TOPK) ** -0.5)
    return inp


def reference(x_prompt, x_sample, mem_prompt, state_ssm_re, state_ssm_im, state_conv,
              cache_mem_k, cache_mem_v,
              norm_mix, norm_mem, norm_ffn, norm_final,
              ssm_a_re, ssm_a_im, ssm_log_dt, ssm_b_re, ssm_b_im, ssm_c_re, ssm_c_im, ssm_d, ssm_w_glu,
              conv_w_in, conv_w, conv_w_out,
              mem_w_q, mem_w_k, mem_w_v, mem_w_o,
              peer_w_query, peer_key1, peer_key2, peer_u, peer_v):
    weights = (norm_mix, norm_mem, norm_ffn, norm_final,
               ssm_a_re, ssm_a_im, ssm_log_dt, ssm_b_re, ssm_b_im, ssm_c_re, ssm_c_im, ssm_d, ssm_w_glu,
               conv_w_in, conv_w, conv_w_out,
               mem_w_q, mem_w_o,
               peer_w_query, peer_key1, peer_key2, peer_u, peer_v)
    bsz = x_prompt.shape[0]
    zero_re = jnp.zeros((N_SSM_LAYERS, bsz, SSM_GROUPS, SSM_STATE), jnp.float32)
    zero_im = jnp.zeros((N_SSM_LAYERS, bsz, SSM_GROUPS, SSM_STATE), jnp.float32)
    zero_conv = jnp.zeros((N_CONV_LAYERS, bsz, CONV_WIDTH - 1, D_MODEL), x_prompt.dtype)
    mem_k_prompt, mem_v_prompt = memory_kv(mem_prompt, mem_w_k, mem_w_v)
    y_prompt, ssm_re_prompt, ssm_im_prompt, conv_prompt = trunk(
        x_prompt, zero_re, zero_im, zero_conv, mem_k_prompt, mem_v_prompt, weights)
    y_sample, ssm_re_sample, ssm_im_sample, conv_sample = trunk(
        x_sample, state_ssm_re, state_ssm_im, state_conv, cache_mem_k, cache_mem_v, weights)
    return (y_prompt, y_sample, ssm_re_prompt, ssm_im_prompt, conv_prompt, mem_k_prompt, mem_v_prompt,
            ssm_re_sample, ssm_im_sample, conv_sample)
```

```python
from contextlib import ExitStack
import numpy as np
import concourse.bass as bass
import concourse.mybir as mybir
from concourse.bass_utils import run_bass_kernel_spmd

F32 = mybir.dt.float32
BF16 = mybir.dt.bfloat16
U32 = mybir.dt.uint32
ALU = mybir.AluOpType
AF = mybir.ActivationFunctionType
AX = mybir.AxisListType

NCORES = 8
import os as _os2
NOOUT = _os2.environ.get("NOOUT")
NT = 17
TC = NT * 128
MAGIC = 12582912.0
SC2PI = 6.283185
INV2PI = 0.15915494309189535
EPS = 1e-6


class Buf:
    __slots__ = ("w", "r")

    def __init__(self):
        self.w = None
        self.r = []


class K:
    def __init__(self, nc, stack, n_dma_sems=32):
        self.nc = nc
        self.eng = {"pe": nc.tensor, "dve": nc.vector, "act": nc.scalar,
                    "pool": nc.gpsimd, "sp": nc.sync}
        self.sem = {}
        self.cnt = {}
        for e in self.eng:
            self.sem[e] = stack.enter_context(nc.semaphore("s_" + e))
            self.cnt[e] = 0
        self.seen = {e: {} for e in self.eng}
        self.dsem = []
        for i in range(n_dma_sems):
            key = "d%d" % i
            self.sem[key] = stack.enter_context(nc.semaphore("s_" + key))
            self.cnt[key] = 0
            self.dsem.append(key)
        self.dnext = 0
        self.bufs = {}
        self.ninstr = 0
        self.halt = False

    def buf(self, name):
        b = self.bufs.get(name)
        if b is None:
            b = Buf()
            self.bufs[name] = b
        return b

    def _deps(self, reads, writes):
        deps = {}

        def add(ev):
            if ev is not None and deps.get(ev[0], 0) < ev[1]:
                deps[ev[0]] = ev[1]
        for b in reads:
            add(self.buf(b).w)
        for b in writes:
            bb = self.buf(b)
            add(bb.w)
            for ev in bb.r:
                add(ev)
        return deps

    def _wait(self, e, deps, skip_self=False):
        eng = self.eng[e]
        for kk, v in deps.items():
            if kk == e and skip_self:
                continue
            if self.seen[e].get(kk, 0) >= v:
                continue
            eng.wait_ge(self.sem[kk], v)
            self.seen[e][kk] = v

    def _mark(self, ev, reads, writes):
        for b in reads:
            self.buf(b).r.append(ev)
        for b in writes:
            bb = self.buf(b)
            bb.w = ev
            bb.r = []

    def op(self, e, fn, reads=(), writes=()):
        if self.halt:
            return None
        deps = self._deps(reads, writes)
        self._wait(e, deps, skip_self=(e == "pe"))
        ins = fn(self.eng[e])
        self.cnt[e] += 1
        ins.then_inc(self.sem[e], 1)
        ev = (e, self.cnt[e])
        self._mark(ev, reads, writes)
        self.ninstr += 1
        return ev

    def dma(self, q, fn, reads=(), writes=()):
        if self.halt:
            return None
        deps = self._deps(reads, writes)
        self._wait(q, deps)
        dkey = self.dsem[self.dnext % len(self.dsem)]
        self.dnext += 1
        if self.cnt[dkey] > 0:
            self._wait(q, {dkey: self.cnt[dkey]})
        ins = fn(self.eng[q])
        self.cnt[dkey] += 16
        ins.then_inc(self.sem[dkey], 16)
        ev = (dkey, self.cnt[dkey])
        self._mark(ev, reads, writes)
        self.ninstr += 1
        return ev

    def barrier(self):
        if self.halt:
            return
        deps = {kk: v for kk, v in self.cnt.items() if v > 0}
        for e in self.eng:
            self._wait(e, deps)
        self.bufs = {}


class _Stop(Exception):
    pass


def build(stop=99):
    nc = bass.Bass("TRN2", target_bir_lowering=False)

    def din(name, shape, dtype=F32):
        return nc.dram_tensor(name, shape, dtype, kind="ExternalInput").ap()

    def dout(name, shape):
        return nc.dram_tensor(name, shape, F32, kind="ExternalOutput").ap()

    xp = din("xp", [2048, 1024])
    xs = din("xs", [64, 1024])
    memp = din("memp", [256, 1024])
    s0re = din("s0re", [16, 4096])
    s0im = din("s0im", [16, 4096])
    sconv = din("sconv", [32, 1024])
    ck = din("ck", [2, 16, 256, 1024])
    cv = din("cv", [2, 16, 256, 1024])
    norm_mix = din("norm_mix", [2, 1024])
    norm_mem = din("norm_mem", [2, 1024])
    norm_ffn = din("norm_ffn", [2, 1024])
    norm_final = din("norm_final", [1, 1024])
    a_re = din("ssm_a_re", [64, 64])
    a_im = din("ssm_a_im", [64, 64])
    log_dt = din("ssm_log_dt", [64])
    b_re = din("ssm_b_re", [64, 64, 16])
    b_im = din("ssm_b_im", [64, 64, 16])
    c_re = din("ssm_c_re", [64, 16, 64])
    c_im = din("ssm_c_im", [64, 16, 64])
    ssm_d = din("ssm_d", [8, 128])
    w_glu = din("ssm_w_glu", [1024, 2048])
    conv_w_in = din("conv_w_in", [1024, 3072])
    conv_w = din("conv_w", [24, 128])
    conv_w_out = din("conv_w_out", [1024, 1024])
    mem_w_q = din("mem_w_q", [2, 1024, 1024])
    mem_w_k = din("mem_w_k", [2, 1024, 1024])
    mem_w_v = din("mem_w_v", [2, 1024, 1024])
    mem_w_o = din("mem_w_o", [2, 1024, 1024])
    peer_wq = din("peer_w_query", [2, 1024, 2048])
    peer_k1 = din("peer_key1", [2, 128, 128])
    peer_k2 = din("peer_key2", [2, 128, 128])
    peer_u = din("peer_u", [32768 if stop >= 3 else 128, 1024])
    peer_v = din("peer_v", [32768 if stop >= 3 else 128, 1024])

    o_yp = dout("o_yp", [2048, 1024])
    o_ys = dout("o_ys", [64, 1024])
    o_sre_p = dout("o_sre_p", [32, 128])
    o_sim_p = dout("o_sim_p", [32, 128])
    o_conv_p = dout("o_conv_p", [2, 1024])
    o_mk = dout("o_mk", [512, 1024])
    o_mv = dout("o_mv", [512, 1024])
    o_sre_s = dout("o_sre_s", [16, 4096])
    o_sim_s = dout("o_sim_s", [16, 4096])
    o_conv_s = dout("o_conv_s", [32, 1024])

    with ExitStack() as st:
        k = K(nc, st)

        uid = [0]

        def T(stack, name, shape, dtype):
            uid[0] += 1
            return stack.enter_context(nc.sbuf_tensor("%s_%d" % (name, uid[0]), shape, dtype))

        def PS(stack, name, shape, dtype=F32):
            uid[0] += 1
            return stack.enter_context(nc.psum_tensor("%s_%d" % (name, uid[0]), shape, dtype))

        def TT(e, out, in0, in1, op, r, w):
            return k.op(e, lambda E: E.tensor_tensor(out=out, in0=in0, in1=in1, op=op), r, w)

        def TS(e, out, in0, s1, op0, r, w, s2=None, op1=None):
            if op1 is None:
                return k.op(e, lambda E: E.tensor_scalar(out=out, in0=in0, scalar1=s1, scalar2=None, op0=op0), r, w)
            return k.op(e, lambda E: E.tensor_scalar(out=out, in0=in0, scalar1=s1, scalar2=s2, op0=op0, op1=op1), r, w)

        def STT(e, out, in0, scalar, in1, op0, op1, r, w):
            return k.op(e, lambda E: E.scalar_tensor_tensor(out=out, in0=in0, scalar=scalar, in1=in1, op0=op0, op1=op1), r, w)

        def ACT(out, in_, func, r, w, **kw):
            return k.op("act", lambda E: E.activation(out=out, in_=in_, func=func, **kw), r, w)

        def CP(e, out, in_, r, w):
            if e == "act":
                return k.op("act", lambda E: E.activation(out=out, in_=in_, func=AF.Copy), r, w)
            return k.op(e, lambda E: E.tensor_copy(out=out, in_=in_), r, w)

        def MM(out, lhsT, rhs, start, stop, r, w):
            return k.op("pe", lambda E: E.matmul(out, lhsT=lhsT, rhs=rhs, start=start, stop=stop), r, w)

        def TR(out, in_, ident, r, w):
            return k.op("pe", lambda E: E.transpose(out=out, in_=in_, identity=ident), r, w)

        def MS(e, ap, val, w):
            return k.op(e, lambda E: E.memset(ap, val), (), w)

        def LD(out, in_, w, r=(), q="sp"):
            return k.dma(q, lambda E: E.dma_start(out=out, in_=in_), r, w)

        ldc_n = [0]

        def LDC(out, in_, w, r=()):
            F = out.shape[-1]
            for c0 in range(0, F, 1024):
                c1 = min(F, c0 + 1024)
                i = ldc_n[0] % 2
                ldc_n[0] += 1
                LD(wstg[i][:, 0:c1 - c0], in_[:, c0:c1], ["wstg%d" % i])
                CP("pool" if i == 0 else "act", out[:, c0:c1], wstg[i][:, 0:c1 - c0], ["wstg%d" % i], w)

        def ckp(v):
            if stop <= v and not k.halt:
                k.barrier()
                k.halt = True

        X = T(st, "X", [128, NT, 1024], F32)
        identf = T(st, "identf", [128, 128], F32)
        identb = T(st, "identb", [128, 128], BF16)
        gbc = T(st, "gbc", [128, 1024], F32)
        junk = T(st, "junk", [128, 1024], BF16)
        htok = [T(st, "htok%d" % i, [128, 1024], BF16) for i in range(2)]
        nsm = T(st, "nsm", [128, 8], F32)
        epsT = T(st, "epsT", [128, 1], F32)
        memT = T(st, "memT", [128, 8, 256], BF16)
        wstg = [T(st, "wstg%d" % i, [128, 1024], F32) for i in range(2)]

        MS("pool", identf[:], 1.0, ["identf"])
        k.op("pool", lambda E: E.affine_select(out=identf[:], in_=identf[:], pattern=[[-1, 128]],
                                               compare_op=ALU.is_equal, fill=0.0, base=0,
                                               channel_multiplier=1), ["identf"], ["identf"])
        CP("dve", identb[:], identf[:], ["identf"], ["identb"])
        MS("dve", epsT[:], EPS, ["epsT"])
        MS("pool", X[:, 16, :], 0.0, ["X16"])
        xpv = xp.rearrange("(t p) d -> p t d", p=128)
        for g4 in range(4):
            LD(X[:, 4 * g4:4 * g4 + 4, :], xpv[:, 4 * g4:4 * g4 + 4, :], ["X%d" % t for t in range(4 * g4, 4 * g4 + 4)])
        LD(X[0:64, 16, :], xs, ["X16"])

        def load_gain(vec_ap):
            LD(gbc[:], vec_ap.to_broadcast([128, 1024]), ["gbc"])

        def norm_tile(t, psn, hT_out, hT_names, col0):
            i = t % 2
            ht = htok[i]
            ACT(junk[:], X[:, t, :], AF.Square, ["X%d" % t], ["junk", "nss%d" % i], accum_out=nsm[:, 2 * i:2 * i + 1])
            ACT(nsm[:, 2 * i + 1:2 * i + 2], nsm[:, 2 * i:2 * i + 1], AF.Sqrt, ["nss%d" % i], ["nrs%d" % i],
                scale=1.0 / 1024.0, bias=epsT[:, 0:1])
            k.op("dve", lambda E: E.reciprocal(out=nsm[:, 4 + i:5 + i], in_=nsm[:, 2 * i + 1:2 * i + 2]),
                 ["nrs%d" % i], ["nri%d" % i])
            STT("dve", ht[:], X[:, t, :], nsm[:, 4 + i:5 + i], gbc[:], ALU.mult, ALU.mult,
                ["X%d" % t, "nri%d" % i, "gbc"], ["htok%d" % i])
            if hT_out is None:
                return
            for dt in range(8):
                TR(psn[:, dt * 128:(dt + 1) * 128], ht[:, dt * 128:(dt + 1) * 128], identb[:],
                   ["htok%d" % i, "identb"], ["psn"])
            CP("act", hT_out[:, :, col0:col0 + 128], psn[:].rearrange("p (a b) -> p a b", a=8),
               ["psn"], hT_names)

        try:
          with ExitStack() as ph:
            hT = T(ph, "hT", [128, 8, TC], BF16)
            hTn = ["hT%d" % d for d in range(8)]
            load_gain(norm_mix[0:1, :])
            with ExitStack() as phn:
                psn = PS(phn, "psn", [128, 1024], BF16)
                for t in range(NT):
                    norm_tile(t, psn, hT, hTn, t * 128)
                k.barrier()
            ckp(0.1)

            with ExitStack() as ph2:
                are = T(ph2, "are", [32, 128], F32)
                aim = T(ph2, "aim", [32, 128], F32)
                ldt = T(ph2, "ldt", [32, 2], F32)
                dtb = T(ph2, "dtb", [32, 128], F32)
                qa = [T(ph2, "qa%d" % i, [32, 128], F32) for i in range(10)]
                sp6 = T(ph2, "sp6", [128, 6, 32], F32)
                dsk = T(ph2, "dsk", [128, 8], F32)
                dsk8 = T(ph2, "dsk8", [8, 128], F32)
                pA = PS(ph2, "pA", [128, 1024], F32)
                pB = PS(ph2, "pB", [128, 1024], F32)
                pY = PS(ph2, "pY", [128, 1024], F32)
                LD(are[:], a_re.rearrange("(q two) p -> q (two p)", two=2), ["are"])
                LD(aim[:], a_im.rearrange("(q two) p -> q (two p)", two=2), ["aim"])
                LD(ldt[:], log_dt.rearrange("(q two) -> q two", two=2), ["ldt"])
                LD(dsk8[:], ssm_d, ["dsk8"])
                ACT(ldt[:], ldt[:], AF.Exp, ["ldt"], ["ldt"])
                CP("dve", dtb[:].rearrange("q (two p) -> q two p", two=2),
                   ldt[:, :].unsqueeze(2).to_broadcast([32, 2, 64]), ["ldt"], ["dtb"])
                mag, ang, t1, t2, sina, cosa, lbre, lbim, fre, fim = [x[:] for x in qa]
                N = ["qa%d" % i for i in range(10)]
                TT("dve", t1, are[:], dtb[:], ALU.mult, ["are", "dtb"], [N[2]])
                ACT(mag, t1, AF.Exp, [N[2]], [N[0]])
                TT("dve", ang, aim[:], dtb[:], ALU.mult, ["aim", "dtb"], [N[1]])
                TS("dve", ang, ang, INV2PI, ALU.mult, [N[1]], [N[1]])
                TS("dve", t1, ang, MAGIC, ALU.add, [N[1]], [N[2]], s2=MAGIC, op1=ALU.subtract)
                TT("dve", t1, ang, t1, ALU.subtract, [N[1], N[2]], [N[2]])
                ACT(sina, t1, AF.Sin, [N[2]], [N[4]], scale=SC2PI)
                TS("dve", t2, ang, 0.25, ALU.add, [N[1]], [N[3]])
                TS("dve", t1, t2, MAGIC, ALU.add, [N[3]], [N[2]], s2=MAGIC, op1=ALU.subtract)
                TT("dve", t1, t2, t1, ALU.subtract, [N[3], N[2]], [N[2]])
                ACT(cosa, t1, AF.Sin, [N[2]], [N[5]], scale=SC2PI)
                TT("dve", lbre, mag, cosa, ALU.mult, [N[0], N[5]], [N[6]])
                TT("dve", lbim, mag, sina, ALU.mult, [N[0], N[4]], [N[7]])
                TT("dve", t1, are[:], are[:], ALU.mult, ["are"], [N[2]])
                TT("dve", t2, aim[:], aim[:], ALU.mult, ["aim"], [N[3]])
                TT("dve", t1, t1, t2, ALU.add, [N[2], N[3]], [N[2]])
                k.op("dve", lambda E: E.reciprocal(out=sina, in_=t1), [N[2]], [N[4]])
                TS("dve", cosa, lbre, -1.0, ALU.add, [N[6]], [N[5]])
                TT("dve", t1, cosa, are[:], ALU.mult, [N[5], "are"], [N[2]])
                TT("dve", t2, lbim, aim[:], ALU.mult, [N[7], "aim"], [N[3]])
                TT("dve", t1, t1, t2, ALU.add, [N[2], N[3]], [N[2]])
                TT("dve", fre, t1, sina, ALU.mult, [N[2], N[4]], [N[8]])
                TT("dve", t1, lbim, are[:], ALU.mult, [N[7], "are"], [N[2]])
                TT("dve", t2, cosa, aim[:], ALU.mult, [N[5], "aim"], [N[3]])
                TT("dve", t1, t1, t2, ALU.subtract, [N[2], N[3]], [N[2]])
                TT("dve", fim, t1, sina, ALU.mult, [N[2], N[4]], [N[9]])
                for j, (src, nm) in enumerate([(fre, N[8]), (fim, N[9]), (mag, N[0]), (ang, N[1]), (lbre, N[6]), (lbim, N[7])]):
                    TR(pA[:, j * 32:(j + 1) * 32], src, identf[0:32, 0:32], [nm, "identf"], ["pA"])
                CP("act", sp6[:].rearrange("p a b -> p (a b)"), pA[:, 0:192], ["pA"], ["sp6"])
                TR(pA[:, 256:264], dsk8[:], identf[0:8, 0:8], ["dsk8", "identf"], ["pA"])
                CP("act", dsk[:], pA[:, 256:264], ["pA"], ["dsk"])
                F_RE, F_IM, MAG, FQ, LBRE, LBIM = range(6)
                ckp(0.2)

                LT_re = T(ph2, "LT_re", [128, 32, 128], BF16)
                LT_im = T(ph2, "LT_im", [128, 32, 128], BF16)
                Cre = T(ph2, "Cre", [128, 32, 64], BF16)
                Cimn = T(ph2, "Cimn", [128, 32, 64], BF16)
                MS("pool", Cre[:], 0.0, ["Cre"])
                MS("pool", Cimn[:], 0.0, ["Cimn"])
                s0T_re = T(ph2, "s0T_re", [128, 32, 16], F32)
                s0T_im = T(ph2, "s0T_im", [128, 32, 16], F32)
                with ExitStack() as ph3:
                    bre = T(ph3, "bre", [128, 32, 16], F32)
                    bim = T(ph3, "bim", [128, 32, 16], F32)
                    bt1 = T(ph3, "bt1", [128, 32, 16], F32)
                    bt2 = T(ph3, "bt2", [128, 32, 16], F32)
                    bbr = T(ph3, "bbr", [128, 32, 16], F32)
                    bbi = T(ph3, "bbi", [128, 32, 16], F32)
                    Pre = T(ph3, "Pre", [128, 32, 128], BF16)
                    Pim = Pre
                    psb = PS(ph3, "psb", [128, 1024], BF16)
                    LD(bre[:], b_re.rearrange("(q two) p c -> (two p) q c", two=2), ["bre"])
                    LD(bim[:], b_im.rearrange("(q two) p c -> (two p) q c", two=2), ["bim"])
                    frb = sp6[:, F_RE, :].unsqueeze(2).to_broadcast([128, 32, 16])
                    fib = sp6[:, F_IM, :].unsqueeze(2).to_broadcast([128, 32, 16])
                    TT("dve", bt1[:], bre[:], frb, ALU.mult, ["bre", "sp6"], ["bt1"])
                    TT("dve", bt2[:], bim[:], fib, ALU.mult, ["bim", "sp6"], ["bt2"])
                    TT("dve", bbr[:], bt1[:], bt2[:], ALU.subtract, ["bt1", "bt2"], ["bbr"])
                    TT("dve", bt1[:], bre[:], fib, ALU.mult, ["bre", "sp6"], ["bt1"])
                    TT("dve", bt2[:], bim[:], frb, ALU.mult, ["bim", "sp6"], ["bt2"])
                    TT("dve", bbi[:], bt1[:], bt2[:], ALU.add, ["bt1", "bt2"], ["bbi"])
                    MS("pool", Pre[:], 0.0, ["Pre"])
                    for (Pd, bb, nmP, nmb, LT, nmL) in ((Pre, bbr, "Pre", "bbr", LT_re, "LT_re"), (Pre, bbi, "Pre", "bbi", LT_im, "LT_im")):
                        Pv = Pd[:].rearrange("p (qa qb) n -> p qa qb n", qb=4)
                        bv = bb[:].rearrange("p (qa qb) c -> p qa qb c", qb=4)
                        for qb in range(4):
                            CP("dve", Pv[0:64, :, qb, 32 * qb:32 * qb + 16], bv[0:64, :, qb, :], [nmb], [nmP])
                            CP("dve", Pv[64:128, :, qb, 32 * qb + 16:32 * qb + 32], bv[64:128, :, qb, :], [nmb], [nmP])
                        for b8 in range(4):
                            for j in range(8):
                                q = b8 * 8 + j
                                TR(psb[:, j * 128:(j + 1) * 128], Pd[:, q, :], identb[:], [nmP, "identb"], ["psb"])
                            CP("act", LT[:, b8 * 8:(b8 + 1) * 8, :], psb[:].rearrange("p (a b) -> p a b", a=8), ["psb"], [nmL])
                    k.barrier()
                ckp(0.3)
                with ExitStack() as ph3:
                    Sre = T(ph3, "Sre", [32, 32, 128], F32)
                    MS("pool", Sre[:], 0.0, ["Sre"])
                    for (csrc, Cd, nmC, sc) in ((c_re, Cre, "Cre", 1.0), (c_im, Cimn, "Cimn", -1.0)):
                        cvw = csrc.rearrange("(q two) c p -> two c q p", two=2)
                        LD(Sre[0:16, :, 0:64], cvw[0], ["Sre"])
                        LD(Sre[16:32, :, 64:128], cvw[1], ["Sre"])
                        for q in range(32):
                            TR(pA[:, q * 32:(q + 1) * 32], Sre[:, q, :], identf[0:32, 0:32], ["Sre", "identf"], ["pA"])
                        ACT(Cd[:, :, 32:64], pA[:].rearrange("p (a b) -> p a b", a=32), AF.Copy, ["pA"], [nmC], scale=sc)
                    k.barrier()
                ckp(0.35)
                with ExitStack() as ph3:
                    s0sb = T(ph3, "s0sb", [16, 1024], F32)
                    for (src, dst, nmd) in ((s0re, s0T_re, "s0T_re"), (s0im, s0T_im, "s0T_im")):
                        for b8 in range(4):
                            LD(s0sb[:], src[:, b8 * 1024:(b8 + 1) * 1024], ["s0sb"])
                            for j in range(8):
                                TR(pA[:, j * 16:(j + 1) * 16], s0sb[:, j * 128:(j + 1) * 128], identf[0:16, 0:16],
                                   ["s0sb", "identf"], ["pA"])
                            CP("act", dst[:, b8 * 8:(b8 + 1) * 8, :].rearrange("p a b -> p (a b)"), pA[:, 0:128], ["pA"], [nmd])
                    k.barrier()

                ckp(0.4)
                CH = 256
                NCH = 2048 // CH
                iot = T(ph2, "iot", [128, CH], F32)
                csb = [T(ph2, "cs%d" % i, [128, CH], F32) for i in range(2)]
                snb = [T(ph2, "sn%d" % i, [128, CH], F32) for i in range(2)]
                x2 = T(ph2, "x2", [128, CH], F32)
                bpr = [T(ph2, "bpr%d" % i, [128, CH], F32) for i in range(2)]
                bpi = [T(ph2, "bpi%d" % i, [128, CH], F32) for i in range(2)]
                vr = [T(ph2, "vr%d" % i, [128, CH], F32) for i in range(2)]
                vi = [T(ph2, "vi%d" % i, [128, CH], F32) for i in range(2)]
                zr = [T(ph2, "zr%d" % i, [128, CH], BF16) for i in range(2)]
                zi = [T(ph2, "zi%d" % i, [128, CH], BF16) for i in range(2)]
                u1 = T(ph2, "u1", [128, CH], F32)
                u2 = T(ph2, "u2", [128, CH], F32)
                d1 = T(ph2, "d1", [128, CH], F32)
                d2 = T(ph2, "d2", [128, CH], F32)
                ypre = T(ph2, "ypre", [128, CH], F32)
                tiny = T(ph2, "tiny", [128, 16], F32)
                cinit = T(ph2, "cinit", [128, 2, 2], F32)
                fin_re = T(ph2, "fin_re", [128, 32], F32)
                fin_im = T(ph2, "fin_im", [128, 32], F32)
                fins_re = T(ph2, "fins_re", [128, 32, 16], F32)
                fins_im = T(ph2, "fins_im", [128, 32, 16], F32)
                m0 = T(ph2, "m0", [128, 16, 4], F32)
                decs = T(ph2, "decs", [128, 64], F32)
                sm = [T(ph2, "sm%d" % i, [128, 64], F32) for i in range(6)]
                zsb = [T(ph2, "zsb%d" % i, [128, 64], BF16) for i in range(2)]
                lam = T(ph2, "lam", [128, 2, 16], F32)

                k.op("pool", lambda E: E.iota(iot[:], pattern=[[1, CH]], base=0, channel_multiplier=0,
                                              allow_small_or_imprecise_dtypes=True), (), ["iot"])
                MS("dve", m0[:], 1.0, ["m0"])
                MS("dve", m0[:, :, 0:1], 0.0, ["m0"])

                import os as _os
                QSTOP = int(_os.environ.get("QSTOP", "99"))
                for q in range(32):
                    if q == QSTOP:
                        ckp(0.65)
                    dti = q // 4
                    qq = q % 4
                    pi = q % 2
                    cs, sn = csb[pi], snb[pi]
                    csn, snn = "cs%d" % pi, "sn%d" % pi
                    fq = sp6[:, FQ, q:q + 1]
                    magq = sp6[:, MAG, q:q + 1]
                    TS("pool", sn[:], iot[:], fq, ALU.mult, ["iot", "sp6"], [snn])
                    TS("pool", x2[:], sn[:], MAGIC, ALU.add, [snn], ["x2"], s2=MAGIC, op1=ALU.subtract)
                    TT("pool", sn[:], sn[:], x2[:], ALU.subtract, [snn, "x2"], [snn])
                    ACT(sn[:], sn[:], AF.Sin, [snn], [snn], scale=SC2PI)
                    TS("pool", cs[:], iot[:], fq, ALU.mult, ["iot", "sp6"], [csn], s2=0.25, op1=ALU.add)
                    TS("pool", x2[:], cs[:], MAGIC, ALU.add, [csn], ["x2"], s2=MAGIC, op1=ALU.subtract)
                    TT("pool", cs[:], cs[:], x2[:], ALU.subtract, [csn, "x2"], [csn])
                    ACT(cs[:], cs[:], AF.Sin, [csn], [csn], scale=SC2PI)

                    ckp(0.5)
                    for c in range(NCH):
                        bi = c % 2
                        cols = slice(c * CH, (c + 1) * CH)
                        MM(pA[:, bi * 512:bi * 512 + CH], LT_re[:, q, :], hT[:, dti, cols], True, True,
                           ["LT_re", hTn[dti]], ["pA%d" % bi])
                        MM(pB[:, bi * 512:bi * 512 + CH], LT_im[:, q, :], hT[:, dti, cols], True, True,
                           ["LT_im", hTn[dti]], ["pB%d" % bi])
                        par = pA[:, bi * 512:bi * 512 + CH]
                        pai = pB[:, bi * 512:bi * 512 + CH]
                        TT("dve", d1[:], par, cs[:], ALU.mult, ["pA%d" % bi, csn], ["d1"])
                        TT("dve", d2[:], pai, sn[:], ALU.mult, ["pB%d" % bi, snn], ["d2"])
                        TT("dve", bpr[bi][:], d1[:], d2[:], ALU.add, ["d1", "d2"], ["bpr%d" % bi])
                        TT("dve", d1[:], pai, cs[:], ALU.mult, ["pB%d" % bi, csn], ["d1"])
                        TT("dve", d2[:], par, sn[:], ALU.mult, ["pA%d" % bi, snn], ["d2"])
                        TT("dve", bpi[bi][:], d1[:], d2[:], ALU.subtract, ["d1", "d2"], ["bpi%d" % bi])
                        magb = magq.to_broadcast([128, CH])
                        if c == 0:
                            ir, ii = 0.0, 0.0
                            rd = []
                        else:
                            ir, ii = cinit[:, 0, 0:1], cinit[:, 0, 1:2]
                            rd = ["cinit"]
                        k.op("dve", lambda E: E.tensor_tensor_scan(out=vr[bi][:], data0=magb, data1=bpr[bi][:], initial=ir,
                                                                   op0=ALU.mult, op1=ALU.add),
                             ["sp6", "bpr%d" % bi] + rd, ["vr%d" % bi])
                        k.op("dve", lambda E: E.tensor_tensor_scan(out=vi[bi][:], data0=magb, data1=bpi[bi][:], initial=ii,
                                                                   op0=ALU.mult, op1=ALU.add),
                             ["sp6", "bpi%d" % bi] + rd, ["vi%d" % bi])
                        TT("pool", u1[:], vr[bi][:], cs[:], ALU.mult, ["vr%d" % bi, csn], ["u1"])
                        TT("pool", u2[:], vi[bi][:], sn[:], ALU.mult, ["vi%d" % bi, snn], ["u2"])
                        TT("pool", zr[bi][:], u1[:], u2[:], ALU.subtract, ["u1", "u2"], ["zr%d" % bi])
                        TT("pool", u1[:], vr[bi][:], sn[:], ALU.mult, ["vr%d" % bi, snn], ["u1"])
                        TT("pool", u2[:], vi[bi][:], cs[:], ALU.mult, ["vi%d" % bi, csn], ["u2"])
                        TT("pool", zi[bi][:], u1[:], u2[:], ALU.add, ["u1", "u2"], ["zi%d" % bi])
                        L = slice(CH - 1, CH)
                        TS("dve", tiny[:, 0:1], vi[bi][:, L], sn[:, L], ALU.mult, ["vi%d" % bi, snn], ["tiny"])
                        STT("dve", tiny[:, 1:2], vr[bi][:, L], cs[:, L], tiny[:, 0:1], ALU.mult, ALU.subtract,
                            ["vr%d" % bi, csn, "tiny"], ["tiny"])
                        TS("dve", tiny[:, 2:3], vi[bi][:, L], cs[:, L], ALU.mult, ["vi%d" % bi, csn], ["tiny"])
                        STT("dve", tiny[:, 3:4], vr[bi][:, L], sn[:, L], tiny[:, 2:3], ALU.mult, ALU.add,
                            ["vr%d" % bi, snn, "tiny"], ["tiny"])
                        if c < NCH - 1:
                            TS("dve", tiny[:, 4:5], tiny[:, 3:4], sn[:, 1:2], ALU.mult, ["tiny", snn], ["tiny"])
                            STT("dve", cinit[:, 0, 0:1], tiny[:, 1:2], cs[:, 1:2], tiny[:, 4:5], ALU.mult, ALU.subtract,
                                ["tiny", csn], ["cinit"])
                            TS("dve", tiny[:, 5:6], tiny[:, 3:4], cs[:, 1:2], ALU.mult, ["tiny", csn], ["tiny"])
                            STT("dve", cinit[:, 0, 1:2], tiny[:, 1:2], sn[:, 1:2], tiny[:, 5:6], ALU.mult, ALU.add,
                                ["tiny", snn], ["cinit"])
                        else:
                            CP("dve", fin_re[:, q:q + 1], tiny[:, 1:2], ["tiny"], ["fin_re"])
                            CP("dve", fin_im[:, q:q + 1], tiny[:, 3:4], ["tiny"], ["fin_im"])
                        yo = pY[32 * qq:32 * qq + 32, bi * 512:bi * 512 + CH]
                        if qq % 2 == 0:
                            ymm, csl = yo, slice(32, 64)
                        else:
                            ymm, csl = pY[32 * (qq - 1):32 * (qq + 1), bi * 512:bi * 512 + CH], slice(0, 64)
                        MM(ymm, Cre[:, q, csl], zr[bi][:], True, False, ["Cre", "zr%d" % bi], ["pY%d" % bi])
                        MM(ymm, Cimn[:, q, csl], zi[bi][:], False, True, ["Cimn", "zi%d" % bi], ["pY%d" % bi])
                        rows = slice(32 * qq, 32 * qq + 32)
                        STT("dve", ypre[rows, :], hT[rows, dti, cols], dsk[rows, dti:dti + 1], yo, ALU.mult, ALU.add,
                            [hTn[dti], "dsk", "pY%d" % bi], ["ypre"])
                        ACT(hT[rows, dti, cols], ypre[rows, :], AF.Gelu, ["ypre"], [hTn[dti]])

                    ckp(0.55)
                    scol = slice(2048, 2112)
                    MM(pA[:, 0:64], LT_re[:, q, :], hT[:, dti, scol], True, True, ["LT_re", hTn[dti]], ["pA0"])
                    MM(pB[:, 0:64], LT_im[:, q, :], hT[:, dti, scol], True, True, ["LT_im", hTn[dti]], ["pB0"])
                    v3 = lambda ap: ap.rearrange("p (s t) -> p s t", t=4)
                    cs4 = cs[:, 0:4].unsqueeze(1).to_broadcast([128, 16, 4])
                    sn4 = sn[:, 0:4].unsqueeze(1).to_broadcast([128, 16, 4])
                    par = v3(pA[:, 0:64])
                    pai = v3(pB[:, 0:64])
                    s0, s1, s2_, s3, s4, s5 = [v3(x[:]) for x in sm]
                    TT("dve", s0, par, cs4, ALU.mult, ["pA0", csn], ["sm0"])
                    TT("dve", s1, pai, sn4, ALU.mult, ["pB0", snn], ["sm1"])
                    TT("dve", s2_, s0, s1, ALU.add, ["sm0", "sm1"], ["sm2"])
                    TT("dve", s0, pai, cs4, ALU.mult, ["pB0", csn], ["sm0"])
                    TT("dve", s1, par, sn4, ALU.mult, ["pA0", snn], ["sm1"])
                    TT("dve", s3, s0, s1, ALU.subtract, ["sm0", "sm1"], ["sm3"])
                    lbr = sp6[:, LBRE, q:q + 1]
                    lbi = sp6[:, LBIM, q:q + 1]
                    TS("dve", lam[:, 0, :], s0T_im[:, q, :], lbi, ALU.mult, ["s0T_im", "sp6"], ["lam"])
                    STT("dve", lam[:, 0, :], s0T_re[:, q, :], lbr, lam[:, 0, :], ALU.mult, ALU.subtract,
                        ["s0T_re", "sp6", "lam"], ["lam"])
                    TS("dve", lam[:, 1, :], s0T_re[:, q, :], lbi, ALU.mult, ["s0T_re", "sp6"], ["lam"])
                    STT("dve", lam[:, 1, :], s0T_im[:, q, :], lbr, lam[:, 1, :], ALU.mult, ALU.add,
                        ["s0T_im", "sp6", "lam"], ["lam"])
                    TT("dve", s2_[:, :, 0:1], s2_[:, :, 0:1], lam[:, 0, :].unsqueeze(2), ALU.add, ["sm2", "lam"], ["sm2"])
                    TT("dve", s3[:, :, 0:1], s3[:, :, 0:1], lam[:, 1, :].unsqueeze(2), ALU.add, ["sm3", "lam"], ["sm3"])
                    TS("dve", decs[:], m0[:].rearrange("p s t -> p (s t)"), magq, ALU.mult, ["m0", "sp6"], ["decs"])
                    k.op("dve", lambda E: E.tensor_tensor_scan(out=sm[4][:], data0=decs[:], data1=sm[2][:], initial=0.0,
                                                               op0=ALU.mult, op1=ALU.add), ["decs", "sm2"], ["sm4"])
                    k.op("dve", lambda E: E.tensor_tensor_scan(out=sm[5][:], data0=decs[:], data1=sm[3][:], initial=0.0,
                                                               op0=ALU.mult, op1=ALU.add), ["decs", "sm3"], ["sm5"])
                    TT("dve", s0, s4, cs4, ALU.mult, ["sm4", csn], ["sm0"])
                    TT("dve", s1, s5, sn4, ALU.mult, ["sm5", snn], ["sm1"])
                    TT("dve", s2_, s0, s1, ALU.subtract, ["sm0", "sm1"], ["sm2"])
                    TT("dve", s0, s4, sn4, ALU.mult, ["sm4", snn], ["sm0"])
                    TT("dve", s1, s5, cs4, ALU.mult, ["sm5", csn], ["sm1"])
                    TT("dve", s3, s0, s1, ALU.add, ["sm0", "sm1"], ["sm3"])
                    CP("dve", zsb[0][:], sm[2][:], ["sm2"], ["zsb0"])
                    CP("dve", zsb[1][:], sm[3][:], ["sm3"], ["zsb1"])
                    CP("dve", fins_re[:, q, :].unsqueeze(2), s2_[:, :, 3:4], ["sm2"], ["fins_re"])
                    CP("dve", fins_im[:, q, :].unsqueeze(2), s3[:, :, 3:4], ["sm3"], ["fins_im"])
                    yo = pY[32 * qq:32 * qq + 32, 0:64]
                    if qq % 2 == 0:
                        ymm, csl = yo, slice(32, 64)
                    else:
                        ymm, csl = pY[32 * (qq - 1):32 * (qq + 1), 0:64], slice(0, 64)
                    MM(ymm, Cre[:, q, csl], zsb[0][:], True, False, ["Cre", "zsb0"], ["pY0"])
                    MM(ymm, Cimn[:, q, csl], zsb[1][:], False, True, ["Cimn", "zsb1"], ["pY0"])
                    rows = slice(32 * qq, 32 * qq + 32)
                    STT("dve", ypre[rows, 0:64], hT[rows, dti, scol], dsk[rows, dti:dti + 1], yo, ALU.mult, ALU.add,
                        [hTn[dti], "dsk", "pY0"], ["ypre"])
                    ACT(hT[rows, dti, scol], ypre[rows, 0:64], AF.Gelu, ["ypre"], [hTn[dti]])

                    ckp(0.6)
                ckp(0.7)
                stg = T(ph2, "stg", [32, 128], F32)
                stg2 = T(ph2, "stg2", [16, 1024], F32)
                for (fin, dst, nm) in ((fin_re, o_sre_p, "fin_re"), (fin_im, o_sim_p, "fin_im")):
                    TR(pA[0:32, 0:128], fin[:], identf[:], [nm, "identf"], ["pA0"])
                    CP("act", stg[:], pA[0:32, 0:128], ["pA0"], ["stg"])
                    LD(dst, stg[:], [], ["stg"])
                for (fin, dst, nm) in ((fins_re, o_sre_s, "fins_re"), (fins_im, o_sim_s, "fins_im")):
                    for b8 in range(4):
                        for j in range(8):
                            q = b8 * 8 + j
                            TR(pB[0:16, j * 128:(j + 1) * 128], fin[:, q, :], identf[:], [nm, "identf"], ["pB0"])
                        CP("act", stg2[:], pB[0:16, :], ["pB0"], ["stg2"])
                        LD(dst[:, b8 * 1024:(b8 + 1) * 1024], stg2[:], [], ["stg2"])
                k.barrier()

            ckp(0.8)
            with ExitStack() as ph2:
                wglu = T(ph2, "wglu", [128, 8, 2048], BF16)
                sig = T(ph2, "sig", [128, 1024], F32)
                gtmp = T(ph2, "gtmp", [128, 1024], F32)
                pg = [PS(ph2, "pg%d" % i, [128, 2048], F32) for i in range(2)]
                wv_ = w_glu.rearrange("(dt p) f -> p dt f", p=128)
                for dt in range(8):
                    LDC(wglu[:, dt, :], wv_[:, dt, :], ["wglu"])
                for t in range(NT):
                    pi = t % 2
                    for fc in range(4):
                        for dt in range(8):
                            MM(pg[pi][:, fc * 512:(fc + 1) * 512], hT[:, dt, t * 128:(t + 1) * 128],
                               wglu[:, dt, fc * 512:(fc + 1) * 512], dt == 0, dt == 7,
                               [hTn[dt], "wglu"], ["pg%d" % pi])
                    ACT(sig[:], pg[pi][:, 1024:2048], AF.Sigmoid, ["pg%d" % pi], ["sig"])
                    TT("dve", gtmp[:], pg[pi][:, 0:1024], sig[:], ALU.mult, ["pg%d" % pi, "sig"], ["gtmp"])
                    TT("dve", X[:, t, :], X[:, t, :], gtmp[:], ALU.add, ["X%d" % t, "gtmp"], ["X%d" % t])
                k.barrier()

        except _Stop:
            pass
        k.halt = False

        def attention(l):
            with ExitStack() as ph:
                psn = PS(ph, "psn", [128, 1024], BF16)
                pA = PS(ph, "pA", [128, 1024], F32)
                pB = PS(ph, "pB", [128, 2048], F32)
                psx = PS(ph, "psx", [128, 1024], BF16)
                KT = T(ph, "KT", [128, 8, 256], BF16)
                Vb = T(ph, "Vb", [128, 2, 1024], BF16)
                load_gain(norm_mem[l:l + 1, :])
                with ExitStack() as ph2:
                    wk = T(ph2, "wk", [128, 8, 1024], BF16)
                    wv = T(ph2, "wv", [128, 8, 1024], BF16)
                    kst = [T(ph2, "kst%d" % i, [128, 1024], F32) for i in range(2)]
                    wkv = mem_w_k[l].rearrange("(dt p) f -> p dt f", p=128)
                    wvv = mem_w_v[l].rearrange("(dt p) f -> p dt f", p=128)
                    for dt in range(8):
                        LDC(wk[:, dt, :], wkv[:, dt, :], ["wk"])
                        LDC(wv[:, dt, :], wvv[:, dt, :], ["wv"])
                    if l == 0:
                        memb = T(ph2, "memb", [128, 2, 1024], BF16)
                        for mt in range(2):
                            LDC(memb[:, mt, :], memp[mt * 128:(mt + 1) * 128, :], ["memb"])
                        for mt in range(2):
                            for dt in range(8):
                                TR(psx[:, dt * 128:(dt + 1) * 128], memb[:, mt, dt * 128:(dt + 1) * 128], identb[:],
                                   ["memb", "identb"], ["psx"])
                            CP("act", memT[:, :, mt * 128:(mt + 1) * 128], psx[:].rearrange("p (a b) -> p a b", a=8),
                               ["psx"], ["memT"])
                    ckp(1.25)
                    n = 0
                    for (w_, wn, dst, isv) in ((wk, "wk", o_mk, False), (wv, "wv", o_mv, True)):
                        if isv and _os2.environ.get("NOV"):
                            continue
                        for mt in range(2):
                            ks = kst[n % 2]
                            ksn = "kst%d" % (n % 2)
                            n += 1
                            for fc in range(2):
                                for dt in range(8):
                                    MM(pA[:, fc * 512:(fc + 1) * 512], memT[:, dt, mt * 128:(mt + 1) * 128],
                                       w_[:, dt, fc * 512:(fc + 1) * 512], dt == 0, dt == 7, ["memT", wn], ["pA"])
                            CP("act", ks[:], pA[:], ["pA"], [ksn])
                            if isv:
                                CP("dve", Vb[:, mt, :], ks[:], [ksn], ["Vb"])
                            if NOOUT is None:
                                LD(dst[l * 256 + mt * 128:l * 256 + (mt + 1) * 128, :], ks[:], [], [ksn])
                    ckp(1.3)
                    for ft in range(8):
                        for dt in range(8):
                            MM(pB[:, (ft % 4) * 512:(ft % 4) * 512 + 256], wk[:, dt, ft * 128:(ft + 1) * 128], memT[:, dt, :],
                               dt == 0, dt == 7, ["wk", "memT"], ["pB"])
                        CP("act", KT[:, ft, :], pB[:, (ft % 4) * 512:(ft % 4) * 512 + 256], ["pB"], ["KT"])
                    k.barrier()
                ckp(1.4)

                with ExitStack() as ph2:
                    wq = T(ph2, "wq", [128, 8, 1024], BF16)
                    wo = T(ph2, "wo", [128, 8, 1024], BF16)
                    hTg = T(ph2, "hTg", [128, 8, 512], BF16)
                    qT = T(ph2, "qT", [128, 8, 512], BF16)
                    Pf = T(ph2, "Pf", [128, 4, 256], F32)
                    Pn = T(ph2, "Pn", [128, 4, 256], BF16)
                    PT = T(ph2, "PT", [128, 8, 128], BF16)
                    oT = T(ph2, "oT", [128, 8, 128], BF16)
                    stat = T(ph2, "stat", [128, 16], F32)
                    Kc = T(ph2, "Kc", [128, 2, 1024], BF16)
                    Vc = [T(ph2, "Vc%d" % i, [128, 2, 1024], BF16) for i in range(2)]
                    KTc = [T(ph2, "KTc%d" % i, [128, 8, 256], BF16) for i in range(2)]
                    qTm = [T(ph2, "qTm%d" % i, [128, 8, 128], BF16) for i in range(2)]
                    wqv = mem_w_q[l].rearrange("(dt p) f -> p dt f", p=128)
                    wov = mem_w_o[l].rearrange("(dt p) f -> p dt f", p=128)
                    for dt in range(8):
                        LDC(wq[:, dt, :], wqv[:, dt, :], ["wq"])
                        LDC(wo[:, dt, :], wov[:, dt, :], ["wo"])

                    def softmax_pv_o(t, nheadbanks):
                        hs = 512 if nheadbanks == 4 else 256
                        sview = pB[:, 0:4 * hs].rearrange("p (h m) -> p h m", h=4)[:, :, 0:256]
                        k.op("dve", lambda E: E.tensor_reduce(out=stat[:, 0:4], in_=sview, axis=AX.X, op=ALU.max),
                             ["pB"], ["stat"])
                        TS("dve", stat[:, 4:8], stat[:, 0:4], -1.0, ALU.mult, ["stat"], ["stat"])
                        for h in range(4):
                            ACT(Pf[:, h, :], pB[:, h * hs:h * hs + 256], AF.Exp, ["pB", "stat"], ["Pf", "stat"],
                                bias=stat[:, 4 + h:5 + h], scale=1.0, accum_out=stat[:, 8 + h:9 + h])
                        k.op("dve", lambda E: E.reciprocal(out=stat[:, 12:16], in_=stat[:, 8:12]), ["stat"], ["stat"])
                        TT("dve", Pn[:], Pf[:], stat[:, 12:16].unsqueeze(2).to_broadcast([128, 4, 256]), ALU.mult,
                           ["Pf", "stat"], ["Pn"])
                        for h in range(4):
                            for mt in range(2):
                                TR(psx[:, (h * 2 + mt) * 128:(h * 2 + mt + 1) * 128], Pn[:, h, mt * 128:(mt + 1) * 128],
                                   identb[:], ["Pn", "identb"], ["psx"])
                        CP("act", PT[:], psx[:].rearrange("p (a b) -> p a b", a=8), ["psx"], ["PT"])

                    def oproj(t):
                        CP("act", oT[:], pA[:].rearrange("p (a b) -> p a b", a=8), ["pA"], ["oT"])
                        for fc in range(2):
                            for ft in range(8):
                                MM(pA[:, fc * 512:(fc + 1) * 512], oT[:, ft, :], wo[:, ft, fc * 512:(fc + 1) * 512],
                                   ft == 0, ft == 7, ["oT", "wo"], ["pA"])
                        TT("dve", X[:, t, :], X[:, t, :], pA[:], ALU.add, ["X%d" % t, "pA"], ["X%d" % t])

                    for g in range(4):
                        for j in range(4):
                            norm_tile(4 * g + j, psn, hTg, ["hTg"], j * 128)
                        for ft in range(8):
                            bo = 1024 + (ft % 2) * 512
                            for dt in range(8):
                                MM(pB[:, bo:bo + 512], wq[:, dt, ft * 128:(ft + 1) * 128], hTg[:, dt, :], dt == 0, dt == 7,
                                   ["wq", "hTg"], ["pBq%d" % (ft % 2)])
                            ACT(qT[:, ft, :], pB[:, bo:bo + 512], AF.Copy, ["pBq%d" % (ft % 2)], ["qT"], scale=1.0 / 16.0)
                        for j in range(4):
                            t = 4 * g + j
                            tc_ = slice(j * 128, (j + 1) * 128)
                            for h in range(4):
                                for jj in range(2):
                                    MM(pB[:, h * 256:(h + 1) * 256], qT[:, 2 * h + jj, tc_], KT[:, 2 * h + jj, :],
                                       jj == 0, jj == 1, ["qT", "KT"], ["pB"])
                            if j == 0 and g == 0:
                                ckp(1.45)
                            softmax_pv_o(t, 2)
                            if j == 0 and g == 0:
                                ckp(1.5)
                            for ft in range(8):
                                h = ft // 2
                                for mt in range(2):
                                    MM(pA[:, ft * 128:(ft + 1) * 128], Vb[:, mt, ft * 128:(ft + 1) * 128], PT[:, h * 2 + mt, :],
                                       mt == 0, mt == 1, ["Vb", "PT"], ["pA"])
                            oproj(t)
                            if j == 0 and g == 0:
                                ckp(1.6)
                    ckp(1.7)
                    t = 16
                    norm_tile(t, psn, hTg, ["hTg"], 0)
                    for ft in range(8):
                        for dt in range(8):
                            MM(pA[:, ft * 128:(ft + 1) * 128], wq[:, dt, ft * 128:(ft + 1) * 128], hTg[:, dt, 0:128],
                               dt == 0, dt == 7, ["wq", "hTg"], ["pA"])
                    ACT(qT[:, :, 0:128], pA[:].rearrange("p (a b) -> p a b", a=8), AF.Copy, ["pA"], ["qT"], scale=1.0 / 16.0)
                    for i in range(16):
                        pi = i % 2
                        for mt in range(2):
                            LDC(Kc[:, mt, :], ck[l, i, mt * 128:(mt + 1) * 128, :], ["Kc"])
                        for mt in range(2):
                            for ft in range(8):
                                TR(psx[:, ft * 128:(ft + 1) * 128], Kc[:, mt, ft * 128:(ft + 1) * 128], identb[:],
                                   ["Kc", "identb"], ["psx"])
                            CP("act", KTc[pi][:, :, mt * 128:(mt + 1) * 128], psx[:].rearrange("p (a b) -> p a b", a=8),
                               ["psx"], ["KTc%d" % pi])
                        MS("pool", qTm[pi][:], 0.0, ["qTm%d" % pi])
                        CP("pool", qTm[pi][:, :, 4 * i:4 * i + 4], qT[:, :, 4 * i:4 * i + 4], ["qT"], ["qTm%d" % pi])
                        for h in range(4):
                            for jj in range(2):
                                MM(pB[:, h * 512:h * 512 + 256], qTm[pi][:, 2 * h + jj, :], KTc[pi][:, 2 * h + jj, :],
                                   (i == 0 and jj == 0), (i == 15 and jj == 1), ["qTm%d" % pi, "KTc%d" % pi], ["pB"])
                    ckp(1.8)
                    softmax_pv_o(t, 4)
                    MS("dve", oT[:], 0.0, ["oT"])
                    for i in range(16):
                        pi = i % 2
                        for mt in range(2):
                            LDC(Vc[pi][:, mt, :], cv[l, i, mt * 128:(mt + 1) * 128, :], ["Vc%d" % pi])
                        for ft in range(8):
                            h = ft // 2
                            for mt in range(2):
                                MM(pA[:, ft * 128 + 4 * i:ft * 128 + 4 * i + 4], Vc[pi][:, mt, ft * 128:(ft + 1) * 128],
                                   PT[:, h * 2 + mt, 4 * i:4 * i + 4], mt == 0, mt == 1, ["Vc%d" % pi, "PT"], ["pA"])
                    ckp(1.9)
                    pAv = pA[:].rearrange("p (a b) -> p a b", a=8)
                    CP("act", oT[:, :, 0:64], pAv[:, :, 0:64], ["pA"], ["oT"])
                    for fc in range(2):
                        for ft in range(8):
                            MM(pA[:, fc * 512:(fc + 1) * 512], oT[:, ft, :], wo[:, ft, fc * 512:(fc + 1) * 512],
                               ft == 0, ft == 7, ["oT", "wo"], ["pA"])
                    TT("dve", X[0:64, t, :], X[0:64, t, :], pA[0:64, :], ALU.add, ["X16", "pA"], ["X16"])
                    k.barrier()

        def peer(l):
            NU = 3
            NV = 2
            with ExitStack() as ph:
                psn = PS(ph, "psn", [128, 1024], BF16)
                pQ = PS(ph, "pQ", [128, 2048], F32)
                pO = PS(ph, "pO", [128, 1024], F32)
                wqp = T(ph, "wqp", [128, 8, 2048], BF16)
                keyT = T(ph, "keyT", [128, 2, 128], BF16)
                kld = T(ph, "kld", [128, 128], F32)
                hTt = T(ph, "hTt", [128, 8, 128], BF16)
                qpT = T(ph, "qpT", [128, 16, 128], BF16)
                V12 = T(ph, "V12", [128, 16, 16], F32)
                I12u = T(ph, "I12u", [128, 16, 16], U32)
                I12f = T(ph, "I12f", [128, 16, 16], F32)
                work = T(ph, "work", [128, 128], F32)
                cand = T(ph, "cand", [128, 8, 256], F32)
                work2 = T(ph, "work2", [128, 256], F32)
                scv = T(ph, "scv", [128, 8, 16], F32)
                ciu = T(ph, "ciu", [128, 8, 16], U32)
                cff = T(ph, "cff", [128, 128], F32)
                caf = T(ph, "caf", [128, 128], F32)
                cbf = T(ph, "cbf", [128, 128], F32)
                io16 = T(ph, "io16", [128, 16], F32)
                eq = cand[:].rearrange("p h (k a) -> p (h k) a", a=16)
                isel = T(ph, "isel", [128, 2, 128], F32)
                ef = T(ph, "ef", [128, 128], F32)
                eu = T(ph, "eu", [128, 128], U32)
                gsm = T(ph, "gsm", [128, 8, 16], F32)
                gst = T(ph, "gst", [128, 16], F32)
                actv = T(ph, "actv", [128, 128], F32)
                wgt = T(ph, "wgt", [128, 128], F32)
                Ub = [T(ph, "Ub%d" % i, [128, 1024], F32) for i in range(NU)]
                Vg = [T(ph, "Vg%d" % i, [128, 1024], F32) for i in range(NV)]
                Vh = [T(ph, "Vh%d" % i, [128, 1024], BF16) for i in range(2)]
                dg = [T(ph, "dg%d" % i, [128, 128], BF16) for i in range(4)]
                load_gain(norm_ffn[l:l + 1, :])
                wv_ = peer_wq[l].rearrange("(dt p) f -> p dt f", p=128)
                for dt in range(8):
                    LDC(wqp[:, dt, :], wv_[:, dt, :], ["wqp"])
                for j, ksrc in enumerate((peer_k1, peer_k2)):
                    LD(kld[:], ksrc[l], ["kld"])
                    TR(pO[:, 0:128], kld[:], identf[:], ["kld", "identf"], ["pO"])
                    CP("act", keyT[:, j, :], pO[:, 0:128], ["pO"], ["keyT"])
                k.op("pool", lambda E: E.iota(io16[:], pattern=[[1, 16]], base=0, channel_multiplier=0,
                                              allow_small_or_imprecise_dtypes=True), (), ["io16"])
                for t in range(NT):
                    norm_tile(t, psn, hTt, ["hTt"], 0)
                    ht = htok[t % 2]
                    htn = "htok%d" % (t % 2)
                    for ft in range(16):
                        for dt in range(8):
                            MM(pQ[:, ft * 128:(ft + 1) * 128], wqp[:, dt, ft * 128:(ft + 1) * 128], hTt[:, dt, :],
                               dt == 0, dt == 7, ["wqp", "hTt"], ["pQ"])
                    CP("act", qpT[:], pQ[:].rearrange("p (a b) -> p a b", a=16), ["pQ"], ["qpT"])
                    for blk in range(16):
                        MM(pQ[:, blk * 128:(blk + 1) * 128], qpT[:, blk, :], keyT[:, blk % 2, :], True, True,
                           ["qpT", "keyT"], ["pQ"])
                    for blk in range(16):
                        sb = pQ[:, blk * 128:(blk + 1) * 128]
                        k.op("dve", lambda E: E.max(out=V12[:, blk, 0:8], in_=sb), ["pQ"], ["V12"])
                        k.op("dve", lambda E: E.max_index(out=I12u[:, blk, 0:8], in_max=V12[:, blk, 0:8], in_values=sb),
                             ["pQ", "V12"], ["I12u"])
                        k.op("dve", lambda E: E.match_replace(out=work[:], in_to_replace=V12[:, blk, 0:8], in_values=sb,
                                                              imm_value=-1e30), ["pQ", "V12"], ["work"])
                        k.op("dve", lambda E: E.max(out=V12[:, blk, 8:16], in_=work[:]), ["work"], ["V12"])
                        k.op("dve", lambda E: E.max_index(out=I12u[:, blk, 8:16], in_max=V12[:, blk, 8:16], in_values=work[:]),
                             ["work", "V12"], ["I12u"])
                    CP("dve", I12f[:], I12u[:], ["I12u"], ["I12f"])
                    V4 = V12[:].rearrange("p (h two) a -> p h two a", two=2)
                    I4 = I12f[:].rearrange("p (h two) a -> p h two a", two=2)
                    TT("dve", cand[:].rearrange("p h (a b) -> p h a b", a=16),
                       V4[:, :, 0, :].unsqueeze(3).to_broadcast([128, 8, 16, 16]),
                       V4[:, :, 1, :].unsqueeze(2).to_broadcast([128, 8, 16, 16]), ALU.add, ["V12"], ["cand"])
                    for h in range(8):
                        k.op("dve", lambda E: E.max(out=scv[:, h, 0:8], in_=cand[:, h, :]), ["cand"], ["scv"])
                        k.op("dve", lambda E: E.max_index(out=ciu[:, h, 0:8], in_max=scv[:, h, 0:8], in_values=cand[:, h, :]),
                             ["cand", "scv"], ["ciu"])
                        k.op("dve", lambda E: E.match_replace(out=work2[:], in_to_replace=scv[:, h, 0:8], in_values=cand[:, h, :],
                                                              imm_value=-1e30), ["cand", "scv"], ["work2"])
                        k.op("dve", lambda E: E.max(out=scv[:, h, 8:16], in_=work2[:]), ["work2"], ["scv"])
                        k.op("dve", lambda E: E.max_index(out=ciu[:, h, 8:16], in_max=scv[:, h, 8:16], in_values=work2[:]),
                             ["work2", "scv"], ["ciu"])
                    civ = ciu[:].rearrange("p h k -> p (h k)")
                    CP("dve", cff[:], civ, ["ciu"], ["cff"])
                    TS("dve", caf[:], cff[:], 1.0 / 16.0, ALU.mult, ["cff"], ["caf"], s2=-0.46875, op1=ALU.add)
                    TS("dve", caf[:], caf[:], MAGIC, ALU.add, ["caf"], ["caf"], s2=MAGIC, op1=ALU.subtract)
                    STT("dve", cbf[:], caf[:], -16.0, cff[:], ALU.mult, ALU.add, ["caf", "cff"], ["cbf"])
                    io_b = io16[:, :].unsqueeze(1).to_broadcast([128, 128, 16])
                    for half, cf, cfn in ((0, caf, "caf"), (1, cbf, "cbf")):
                        TT("dve", eq, io_b, cf[:, :].unsqueeze(2).to_broadcast([128, 128, 16]), ALU.is_equal,
                           ["io16", cfn], ["cand"])
                        eq4 = cand[:].rearrange("p h (k a) -> p h k a", a=16)
                        TT("dve", eq4, eq4, I4[:, :, half, :].unsqueeze(2).to_broadcast([128, 8, 16, 16]), ALU.mult,
                           ["cand", "I12f"], ["cand"])
                        k.op("dve", lambda E: E.tensor_reduce(out=isel[:, half, :], in_=eq, axis=AX.X, op=ALU.add),
                             ["cand"], ["isel"])
                    STT("dve", ef[:], isel[:, 0, :], 128.0, isel[:, 1, :], ALU.mult, ALU.add, ["isel"], ["ef"])
                    TS("dve", ef[:], ef[:], float(16384 * l), ALU.add, ["ef"], ["ef"])
                    CP("dve", eu[:], ef[:], ["ef"], ["eu"])
                    TT("dve", gsm[:], scv[:], scv[:, :, 0:1].to_broadcast([128, 8, 16]), ALU.subtract, ["scv"], ["gsm"])
                    ACT(gsm[:], gsm[:], AF.Exp, ["gsm"], ["gsm"])
                    k.op("dve", lambda E: E.tensor_reduce(out=gst[:, 0:8], in_=gsm[:], axis=AX.X, op=ALU.add), ["gsm"], ["gst"])
                    k.op("dve", lambda E: E.reciprocal(out=gst[:, 8:16], in_=gst[:, 0:8]), ["gst"], ["gst"])
                    TT("dve", gsm[:], gsm[:], gst[:, 8:16].unsqueeze(2).to_broadcast([128, 8, 16]), ALU.mult,
                       ["gsm", "gst"], ["gsm"])
                    for s in range(128):
                        ub = Ub[s % NU]
                        ubn = "Ub%d" % (s % NU)
                        k.dma("pool", lambda E: E.indirect_dma_start(
                            out=ub[:], out_offset=None, in_=peer_u,
                            in_offset=bass.IndirectOffsetOnAxis(ap=eu[:, s:s + 1], axis=0)), ["eu"], [ubn])
                        k.op("dve", lambda E: E.scalar_tensor_tensor(
                            out=junk[:], in0=ub[:], scalar=1.0, in1=ht[:], op0=ALU.mult, op1=ALU.mult,
                            accum_out=actv[:, s:s + 1]), [ubn, htn], ["junk", "actv"])
                    ACT(actv[:], actv[:], AF.Gelu, ["actv"], ["actv"])
                    TT("dve", wgt[:], actv[:], gsm[:].rearrange("p h k -> p (h k)"), ALU.mult, ["actv", "gsm"], ["wgt"])
                    for s in range(128):
                        vg = Vg[s % NV]
                        vgn = "Vg%d" % (s % NV)
                        dgi = dg[s % 4]
                        dgn = "dg%d" % (s % 4)
                        k.dma("pool", lambda E: E.indirect_dma_start(
                            out=vg[:], out_offset=None, in_=peer_v,
                            in_offset=bass.IndirectOffsetOnAxis(ap=eu[:, s:s + 1], axis=0)), ["eu"], [vgn])
                        TS("dve", dgi[:], identf[:], wgt[:, s:s + 1], ALU.mult, ["identf", "wgt"], [dgn])
                        vh = Vh[s % 2]
                        vhn = "Vh%d" % (s % 2)
                        CP("act", vh[:], vg[:], [vgn], [vhn])
                        for fc in range(2):
                            MM(pO[:, fc * 512:(fc + 1) * 512], dgi[:], vh[:, fc * 512:(fc + 1) * 512], s == 0, s == 127,
                               [dgn, vhn], ["pO"])
                    TT("dve", X[:, t, :], X[:, t, :], pO[:], ALU.add, ["X%d" % t, "pO"], ["X%d" % t])
                k.barrier()

        def conv_mixer():
            GW = 256
            with ExitStack() as ph:
                psn = PS(ph, "psn", [128, 1024], BF16)
                pC = [PS(ph, "pC%d" % i, [128, 512], F32) for i in range(3)]
                pX = PS(ph, "pX", [128, 1024], F32)
                win = T(ph, "win", [128, 8, 3072], BF16)
                wout = T(ph, "wout", [128, 8, 1024], BF16)
                cw24 = T(ph, "cw24", [24, 128], F32)
                cw = T(ph, "cw", [128, 24], F32)
                hTg = T(ph, "hTg", [128, 8, GW], BF16)
                uT = T(ph, "uT", [128, 8, GW], BF16)
                vp = [T(ph, "vp%d" % i, [128, GW + 2], F32) for i in range(2)]
                hvs = T(ph, "hvs", [128, GW], F32)
                acc = T(ph, "acc", [128, GW], F32)
                cbuf = T(ph, "cbuf", [128, 8, 2], F32)
                scb = T(ph, "scb", [128, 8, 32], F32)
                cso = T(ph, "cso", [128, 8, 32], F32)
                vps = T(ph, "vps", [128, 16, 6], F32)
                accs = T(ph, "accs", [128, 16, 4], F32)
                ostg = T(ph, "ostg", [32, 1024], F32)
                load_gain(norm_mix[1:2, :])
                wiv = conv_w_in.rearrange("(dt p) f -> p dt f", p=128)
                wov = conv_w_out.rearrange("(dt p) f -> p dt f", p=128)
                for dt in range(8):
                    LDC(win[:, dt, :], wiv[:, dt, :], ["win"])
                    LDC(wout[:, dt, :], wov[:, dt, :], ["wout"])
                LD(cw24[:], conv_w, ["cw24"])
                TR(pX[:, 0:24], cw24[:], identf[0:24, 0:24], ["cw24", "identf"], ["pX"])
                CP("act", cw[:], pX[:, 0:24], ["pX"], ["cw"])
                LD(ostg[:], sconv, ["ostg"])
                for ft in range(8):
                    TR(pX[:, 32 * ft:32 * ft + 32], ostg[:, ft * 128:(ft + 1) * 128], identf[0:32, 0:32],
                       ["ostg", "identf"], ["pX"])
                CP("act", scb[:].rearrange("p a b -> p (a b)"), pX[:, 0:256], ["pX"], ["scb"])
                MS("dve", cbuf[:], 0.0, ["cbuf"])

                def wout_tile(t, c0):
                    for fc in range(2):
                        for ft in range(8):
                            MM(pX[:, fc * 512:(fc + 1) * 512], uT[:, ft, c0:c0 + 128], wout[:, ft, fc * 512:(fc + 1) * 512],
                               ft == 0, ft == 7, ["uT", "wout"], ["pX"])
                    TT("dve", X[:, t, :], X[:, t, :], pX[:], ALU.add, ["X%d" % t, "pX"], ["X%d" % t])

                NG = 2048 // GW
                TPG = GW // 128
                for g in range(NG + 1):
                    sample = (g == NG)
                    ncol = 128 if sample else GW
                    if sample:
                        norm_tile(16, psn, hTg, ["hTg"], 0)
                    else:
                        for j in range(TPG):
                            norm_tile(TPG * g + j, psn, hTg, ["hTg"], j * 128)
                    for ft in range(8):
                        for j3, off in enumerate((0, 1024, 2048)):
                            for dt in range(8):
                                MM(pC[j3][:, 0:ncol], win[:, dt, off + ft * 128:off + (ft + 1) * 128], hTg[:, dt, 0:ncol],
                                   dt == 0, dt == 7, ["win", "hTg"], ["pC%d" % j3])
                        w0 = cw[:, 0 * 8 + ft:0 * 8 + ft + 1]
                        w1 = cw[:, 1 * 8 + ft:1 * 8 + ft + 1]
                        w2 = cw[:, 2 * 8 + ft:2 * 8 + ft + 1]
                        if not sample:
                            vpp = vp[ft % 2]
                            vpn = "vp%d" % (ft % 2)
                            CP("act", hvs[:], pC[2][:, 0:GW], ["pC2"], ["hvs"])
                            CP("dve", vpp[:, 0:2], cbuf[:, ft, :], ["cbuf"], [vpn])
                            TT("dve", vpp[:, 2:GW + 2], pC[1][:, 0:GW], hvs[:], ALU.mult, ["pC1", "hvs"], [vpn])
                            CP("dve", cbuf[:, ft, :], vpp[:, GW:GW + 2], [vpn], ["cbuf"])
                            TS("dve", acc[:], vpp[:, 0:GW], w0, ALU.mult, [vpn, "cw"], ["acc"])
                            STT("dve", acc[:], vpp[:, 1:GW + 1], w1, acc[:], ALU.mult, ALU.add, [vpn, "cw", "acc"], ["acc"])
                            STT("dve", acc[:], vpp[:, 2:GW + 2], w2, acc[:], ALU.mult, ALU.add, [vpn, "cw", "acc"], ["acc"])
                            TT("dve", uT[:, ft, :], pC[0][:, 0:GW], acc[:], ALU.mult, ["pC0", "acc"], ["uT"])
                        else:
                            CP("act", hvs[:, 0:64], pC[2][:, 0:64], ["pC2"], ["hvs"])
                            CP("dve", vps[:, :, 0:2], scb[:, ft, :].rearrange("p (s r) -> p s r", r=2), ["scb"], ["vps"])
                            TT("dve", vps[:, :, 2:6], pC[1][:, 0:64].rearrange("p (s t) -> p s t", t=4),
                               hvs[:, 0:64].rearrange("p (s t) -> p s t", t=4), ALU.mult, ["pC1", "hvs"], ["vps"])
                            CP("dve", cso[:, ft, :].rearrange("p (s r) -> p s r", r=2), vps[:, :, 4:6], ["vps"], ["cso"])
                            TS("dve", accs[:], vps[:, :, 0:4], w0, ALU.mult, ["vps", "cw"], ["accs"])
                            STT("dve", accs[:], vps[:, :, 1:5], w1, accs[:], ALU.mult, ALU.add, ["vps", "cw", "accs"], ["accs"])
                            STT("dve", accs[:], vps[:, :, 2:6], w2, accs[:], ALU.mult, ALU.add, ["vps", "cw", "accs"], ["accs"])
                            MS("dve", uT[:, ft, 0:128], 0.0, ["uT"])
                            TT("dve", uT[:, ft, 0:64].rearrange("p (s t) -> p s t", t=4),
                               pC[0][:, 0:64].rearrange("p (s t) -> p s t", t=4), accs[:], ALU.mult, ["pC0", "accs"], ["uT"])
                    if not sample:
                        for j in range(TPG):
                            wout_tile(TPG * g + j, j * 128)
                    else:
                        wout_tile(16, 0)
                    if g == NG - 1:
                        for ft in range(8):
                            TR(pX[0:2, ft * 128:(ft + 1) * 128], cbuf[:, ft, :], identf[:], ["cbuf", "identf"], ["pX"])
                        CP("act", ostg[0:2, :], pX[0:2, :], ["pX"], ["ostg"])
                        LD(o_conv_p, ostg[0:2, :], [], ["ostg"])
                for ft in range(8):
                    TR(pX[0:32, ft * 128:(ft + 1) * 128], cso[:, ft, :], identf[:], ["cso", "identf"], ["pX"])
                CP("act", ostg[:], pX[0:32, :], ["pX"], ["ostg"])
                LD(o_conv_s, ostg[:], [], ["ostg"])
                k.barrier()

        if stop >= 1.2:
            attention(0)
            k.halt = False
        if stop >= 3:
            peer(0)
        if stop >= 4:
            conv_mixer()
        if stop >= 5:
            attention(1)
        if stop >= 6:
            peer(1)

        with ExitStack() as ph:
            yo = [T(ph, "yo%d" % i, [128, 1024], F32) for i in range(2)]
            load_gain(norm_final[0:1, :])
            for t in range(NT):
                i = t % 2
                ACT(junk[:], X[:, t, :], AF.Square, ["X%d" % t], ["junk", "nss%d" % i], accum_out=nsm[:, 2 * i:2 * i + 1])
                ACT(nsm[:, 2 * i + 1:2 * i + 2], nsm[:, 2 * i:2 * i + 1], AF.Sqrt, ["nss%d" % i], ["nrs%d" % i],
                    scale=1.0 / 1024.0, bias=epsT[:, 0:1])
                k.op("dve", lambda E: E.reciprocal(out=nsm[:, 4 + i:5 + i], in_=nsm[:, 2 * i + 1:2 * i + 2]),
                     ["nrs%d" % i], ["nri%d" % i])
                STT("dve", yo[i][:], X[:, t, :], nsm[:, 4 + i:5 + i], gbc[:], ALU.mult, ALU.mult,
                    ["X%d" % t, "nri%d" % i, "gbc"], ["yo%d" % i])
                if t < 16:
                    LD(o_yp[t * 128:(t + 1) * 128, :], yo[i][:], [], ["yo%d" % i])
                else:
                    LD(o_ys, yo[i][0:64, :], [], ["yo%d" % i])
            k.barrier()
        print("ninstr", k.ninstr)
    return nc


_NC = None
_STOP = 99


def kernel(**inp):
    global _NC
    if _NC is None:
        _NC = build(_STOP)
    nc = _NC
    f = lambda a: np.ascontiguousarray(np.asarray(a, dtype=np.float32))
    common = {
        "norm_mix": f(inp["norm_mix"]), "norm_mem": f(inp["norm_mem"]), "norm_ffn": f(inp["norm_ffn"]),
        "norm_final": f(inp["norm_final"]).reshape(1, 1024),
        "ssm_a_re": f(inp["ssm_a_re"])[0], "ssm_a_im": f(inp["ssm_a_im"])[0], "ssm_log_dt": f(inp["ssm_log_dt"])[0],
        "ssm_b_re": f(inp["ssm_b_re"])[0], "ssm_b_im": f(inp["ssm_b_im"])[0],
        "ssm_c_re": f(inp["ssm_c_re"])[0], "ssm_c_im": f(inp["ssm_c_im"])[0],
        "ssm_d": f(inp["ssm_d"]).reshape(8, 128), "ssm_w_glu": f(inp["ssm_w_glu"])[0],
        "conv_w_in": f(inp["conv_w_in"])[0], "conv_w": f(inp["conv_w"]).reshape(24, 128),
        "conv_w_out": f(inp["conv_w_out"])[0],
        "mem_w_q": f(inp["mem_w_q"]), "mem_w_k": f(inp["mem_w_k"]), "mem_w_v": f(inp["mem_w_v"]),
        "mem_w_o": f(inp["mem_w_o"]), "peer_w_query": f(inp["peer_w_query"]),
        "peer_key1": f(inp["peer_key1"]), "peer_key2": f(inp["peer_key2"]),
        "peer_u": f(inp["peer_u"]).reshape(32768, 1024)[:(32768 if _STOP >= 3 else 128)],
        "peer_v": f(inp["peer_v"]).reshape(32768, 1024)[:(32768 if _STOP >= 3 else 128)],
    }
    x_prompt = f(inp["x_prompt"]); x_sample = f(inp["x_sample"]); mem_prompt = f(inp["mem_prompt"])
    sre = f(inp["state_ssm_re"]); sim = f(inp["state_ssm_im"]); sconv = f(inp["state_conv"])
    ckk = f(inp["cache_mem_k"]); cvv = f(inp["cache_mem_v"])
    in_maps = []
    for c in range(NCORES):
        s = slice(16 * c, 16 * c + 16)
        m = dict(common)
        m["xp"] = x_prompt[c]
        m["xs"] = np.ascontiguousarray(x_sample[s].reshape(64, 1024))
        m["memp"] = mem_prompt[c]
        m["s0re"] = np.ascontiguousarray(sre[0, s].reshape(16, 4096))
        m["s0im"] = np.ascontiguousarray(sim[0, s].reshape(16, 4096))
        m["sconv"] = np.ascontiguousarray(sconv[0, s].reshape(32, 1024))
        m["ck"] = np.ascontiguousarray(ckk[:, s].reshape(2, 16, 256, 1024))
        m["cv"] = np.ascontiguousarray(cvv[:, s].reshape(2, 16, 256, 1024))
        in_maps.append(m)
    res = run_bass_kernel_spmd(nc, in_maps, core_ids=list(range(NCORES)))
    R = res.results
    y_prompt = np.stack([R[c]["o_yp"] for c in range(NCORES)]).reshape(8, 2048, 1024)
    y_sample = np.concatenate([R[c]["o_ys"].reshape(16, 4, 1024) for c in range(NCORES)], axis=0)
    ssm_re_p = np.stack([R[c]["o_sre_p"].reshape(64, 64) for c in range(NCORES)])[None]
    ssm_im_p = np.stack([R[c]["o_sim_p"].reshape(64, 64) for c in range(NCORES)])[None]
    conv_p = np.stack([R[c]["o_conv_p"] for c in range(NCORES)])[None]
    mk = np.stack([R[c]["o_mk"].reshape(2, 256, 1024) for c in range(NCORES)], axis=1).reshape(2, 8, 256, 4, 256)
    mv = np.stack([R[c]["o_mv"].reshape(2, 256, 1024) for c in range(NCORES)], axis=1).reshape(2, 8, 256, 4, 256)
    ssm_re_s = np.concatenate([R[c]["o_sre_s"].reshape(16, 64, 64) for c in range(NCORES)], axis=0)[None]
    ssm_im_s = np.concatenate([R[c]["o_sim_s"].reshape(16, 64, 64) for c in range(NCORES)], axis=0)[None]
    conv_s = np.concatenate([R[c]["o_conv_s"].reshape(16, 2, 1024) for c in range(NCORES)], axis=0)[None]
    out = (y_prompt, y_sample, ssm_re_p, ssm_im_p, conv_p, mk, mv, ssm_re_s, ssm_im_s, conv_s)
    return tuple(np.ascontiguousarray(o, dtype=np.float32) for o in out)
```

```python
from contextlib import ExitStack
import numpy as np
import concourse.bass as bass
import concourse.mybir as mybir
from concourse.bass_utils import run_bass_kernel_spmd

F32 = mybir.dt.float32
BF16 = mybir.dt.bfloat16
U32 = mybir.dt.uint32
ALU = mybir.AluOpType
AF = mybir.ActivationFunctionType
AX = mybir.AxisListType

NCORES = 8
import os as _os2
NOOUT = _os2.environ.get("NOOUT")
NT = 17
TC = NT * 128
MAGIC = 12582912.0
SC2PI = 6.283185
INV2PI = 0.15915494309189535
EPS = 1e-6


class Buf:
    __slots__ = ("w", "r")

    def __init__(self):
        self.w = None
        self.r = []


class K:
    def __init__(self, nc, stack, n_dma_sems=32):
        self.nc = nc
        self.eng = {"pe": nc.tensor, "dve": nc.vector, "act": nc.scalar,
                    "pool": nc.gpsimd, "sp": nc.sync}
        self.sem = {}
        self.cnt = {}
        for e in self.eng:
            self.sem[e] = stack.enter_context(nc.semaphore("s_" + e))
            self.cnt[e] = 0
        self.seen = {e: {} for e in self.eng}
        self.dsem = []
        for i in range(n_dma_sems):
            key = "d%d" % i
            self.sem[key] = stack.enter_context(nc.semaphore("s_" + key))
            self.cnt[key] = 0
            self.dsem.append(key)
        self.dnext = 0
        self.bufs = {}
        self.ninstr = 0
        self.halt = False

    def buf(self, name):
        b = self.bufs.get(name)
        if b is None:
            b = Buf()
            self.bufs[name] = b
        return b

    def _deps(self, reads, writes):
        deps = {}

        def add(ev):
            if ev is not None and deps.get(ev[0], 0) < ev[1]:
                deps[ev[0]] = ev[1]
        for b in reads:
            add(self.buf(b).w)
        for b in writes:
            bb = self.buf(b)
            add(bb.w)
            for ev in bb.r:
                add(ev)
        return deps

    def _wait(self, e, deps, skip_self=False):
        eng = self.eng[e]
        for kk, v in deps.items():
            if kk == e and skip_self:
                continue
            if self.seen[e].get(kk, 0) >= v:
                continue
            eng.wait_ge(self.sem[kk], v)
            self.seen[e][kk] = v

    def _mark(self, ev, reads, writes):
        for b in reads:
            self.buf(b).r.append(ev)
        for b in writes:
            bb = self.buf(b)
            bb.w = ev
            bb.r = []

    def op(self, e, fn, reads=(), writes=()):
        if self.halt:
            return None
        deps = self._deps(reads, writes)
        self._wait(e, deps, skip_self=(e == "pe"))
        ins = fn(self.eng[e])
        self.cnt[e] += 1
        ins.then_inc(self.sem[e], 1)
        ev = (e, self.cnt[e])
        self._mark(ev, reads, writes)
        self.ninstr += 1
        return ev

    def dma(self, q, fn, reads=(), writes=()):
        if self.halt:
            return None
        deps = self._deps(reads, writes)
        self._wait(q, deps)
        dkey = self.dsem[self.dnext % len(self.dsem)]
        self.dnext += 1
        if self.cnt[dkey] > 0:
            self._wait(q, {dkey: self.cnt[dkey]})
        ins = fn(self.eng[q])
        self.cnt[dkey] += 16
        ins.then_inc(self.sem[dkey], 16)
        ev = (dkey, self.cnt[dkey])
        self._mark(ev, reads, writes)
        self.ninstr += 1
        return ev

    def barrier(self):
        if self.halt:
            return
        deps = {kk: v for kk, v in self.cnt.items() if v > 0}
        for e in self.eng:
            self._wait(e, deps)
        self.bufs = {}


class _Stop(Exception):
    pass


def build(stop=99):
    nc = bass.Bass("TRN2", target_bir_lowering=False)

    def din(name, shape, dtype=F32):
        return nc.dram_tensor(name, shape, dtype, kind="ExternalInput").ap()

    def dout(name, shape):
        return nc.dram_tensor(name, shape, F32, kind="ExternalOutput").ap()

    xp = din("xp", [2048, 1024])
    xs = din("xs", [64, 1024])
    memp = din("memp", [256, 1024])
    s0re = din("s0re", [16, 4096])
    s0im = din("s0im", [16, 4096])
    sconv = din("sconv", [32, 1024])
    ck = din("ck", [2, 16, 256, 1024])
    cv = din("cv", [2, 16, 256, 1024])
    norm_mix = din("norm_mix", [2, 1024])
    norm_mem = din("norm_mem", [2, 1024])
    norm_ffn = din("norm_ffn", [2, 1024])
    norm_final = din("norm_final", [1, 1024])
    a_re = din("ssm_a_re", [64, 64])
    a_im = din("ssm_a_im", [64, 64])
    log_dt = din("ssm_log_dt", [64])
    b_re = din("ssm_b_re", [64, 64, 16])
    b_im = din("ssm_b_im", [64, 64, 16])
    c_re = din("ssm_c_re", [64, 16, 64])
    c_im = din("ssm_c_im", [64, 16, 64])
    ssm_d = din("ssm_d", [8, 128])
    w_glu = din("ssm_w_glu", [1024, 2048])
    conv_w_in = din("conv_w_in", [1024, 3072])
    conv_w = din("conv_w", [24, 128])
    conv_w_out = din("conv_w_out", [1024, 1024])
    mem_w_q = din("mem_w_q", [2, 1024, 1024])
    mem_w_k = din("mem_w_k", [2, 1024, 1024])
    mem_w_v = din("mem_w_v", [2, 1024, 1024])
    mem_w_o = din("mem_w_o", [2, 1024, 1024])
    peer_wq = din("peer_w_query", [2, 1024, 2048])
    peer_k1 = din("peer_key1", [2, 128, 128])
    peer_k2 = din("peer_key2", [2, 128, 128])
    peer_u = din("peer_u", [32768 if stop >= 3 else 128, 1024])
    peer_v = din("peer_v", [32768 if stop >= 3 else 128, 1024])

    o_yp = dout("o_yp", [2048, 1024])
    o_ys = dout("o_ys", [64, 1024])
    o_sre_p = dout("o_sre_p", [32, 128])
    o_sim_p = dout("o_sim_p", [32, 128])
    o_conv_p = dout("o_conv_p", [2, 1024])
    o_mk = dout("o_mk", [512, 1024])
    o_mv = dout("o_mv", [512, 1024])
    o_sre_s = dout("o_sre_s", [16, 4096])
    o_sim_s = dout("o_sim_s", [16, 4096])
    o_conv_s = dout("o_conv_s", [32, 1024])

    with ExitStack() as st:
        k = K(nc, st)

        uid = [0]

        def T(stack, name, shape, dtype):
            uid[0] += 1
            return stack.enter_context(nc.sbuf_tensor("%s_%d" % (name, uid[0]), shape, dtype))

        def PS(stack, name, shape, dtype=F32):
            uid[0] += 1
            return stack.enter_context(nc.psum_tensor("%s_%d" % (name, uid[0]), shape, dtype))

        def TT(e, out, in0, in1, op, r, w):
            return k.op(e, lambda E: E.tensor_tensor(out=out, in0=in0, in1=in1, op=op), r, w)

        def TS(e, out, in0, s1, op0, r, w, s2=None, op1=None):
            if op1 is None:
                return k.op(e, lambda E: E.tensor_scalar(out=out, in0=in0, scalar1=s1, scalar2=None, op0=op0), r, w)
            return k.op(e, lambda E: E.tensor_scalar(out=out, in0=in0, scalar1=s1, scalar2=s2, op0=op0, op1=op1), r, w)

        def STT(e, out, in0, scalar, in1, op0, op1, r, w):
            return k.op(e, lambda E: E.scalar_tensor_tensor(out=out, in0=in0, scalar=scalar, in1=in1, op0=op0, op1=op1), r, w)

        def ACT(out, in_, func, r, w, **kw):
            return k.op("act", lambda E: E.activation(out=out, in_=in_, func=func, **kw), r, w)

        def CP(e, out, in_, r, w):
            if e == "act":
                return k.op("act", lambda E: E.activation(out=out, in_=in_, func=AF.Copy), r, w)
            return k.op(e, lambda E: E.tensor_copy(out=out, in_=in_), r, w)

        def MM(out, lhsT, rhs, start, stop, r, w):
            return k.op("pe", lambda E: E.matmul(out, lhsT=lhsT, rhs=rhs, start=start, stop=stop), r, w)

        def TR(out, in_, ident, r, w):
            return k.op("pe", lambda E: E.transpose(out=out, in_=in_, identity=ident), r, w)

        def MS(e, ap, val, w):
            return k.op(e, lambda E: E.memset(ap, val), (), w)

        def LD(out, in_, w, r=(), q="sp"):
            return k.dma(q, lambda E: E.dma_start(out=out, in_=in_), r, w)

        ldc_n = [0]

        def LDC(out, in_, w, r=()):
            F = out.shape[-1]
            for c0 in range(0, F, 1024):
                c1 = min(F, c0 + 1024)
                i = ldc_n[0] % 2
                ldc_n[0] += 1
                LD(wstg[i][:, 0:c1 - c0], in_[:, c0:c1], ["wstg%d" % i])
                CP("pool" if i == 0 else "act", out[:, c0:c1], wstg[i][:, 0:c1 - c0], ["wstg%d" % i], w)

        def ckp(v):
            if stop <= v and not k.halt:
                k.barrier()
                k.halt = True

        X = T(st, "X", [128, NT, 1024], F32)
        identf = T(st, "identf", [128, 128], F32)
        identb = T(st, "identb", [128, 128], BF16)
        gbc = T(st, "gbc", [128, 1024], F32)
        junk = T(st, "junk", [128, 1024], BF16)
        htok = [T(st, "htok%d" % i, [128, 1024], BF16) for i in range(2)]
        nsm = T(st, "nsm", [128, 8], F32)
        epsT = T(st, "epsT", [128, 1], F32)
        memT = T(st, "memT", [128, 8, 256], BF16)
        wstg = [T(st, "wstg%d" % i, [128, 1024], F32) for i in range(2)]

        MS("pool", identf[:], 1.0, ["identf"])
        k.op("pool", lambda E: E.affine_select(out=identf[:], in_=identf[:], pattern=[[-1, 128]],
                                               compare_op=ALU.is_equal, fill=0.0, base=0,
                                               channel_multiplier=1), ["identf"], ["identf"])
        CP("dve", identb[:], identf[:], ["identf"], ["identb"])
        MS("dve", epsT[:], EPS, ["epsT"])
        MS("pool", X[:, 16, :], 0.0, ["X16"])
        xpv = xp.rearrange("(t p) d -> p t d", p=128)
        for g4 in range(4):
            LD(X[:, 4 * g4:4 * g4 + 4, :], xpv[:, 4 * g4:4 * g4 + 4, :], ["X%d" % t for t in range(4 * g4, 4 * g4 + 4)])
        LD(X[0:64, 16, :], xs, ["X16"])

        def load_gain(vec_ap):
            LD(gbc[:], vec_ap.to_broadcast([128, 1024]), ["gbc"])

        def norm_tile(t, psn, hT_out, hT_names, col0):
            i = t % 2
            ht = htok[i]
            ACT(junk[:], X[:, t, :], AF.Square, ["X%d" % t], ["junk", "nss%d" % i], accum_out=nsm[:, 2 * i:2 * i + 1])
            ACT(nsm[:, 2 * i + 1:2 * i + 2], nsm[:, 2 * i:2 * i + 1], AF.Sqrt, ["nss%d" % i], ["nrs%d" % i],
                scale=1.0 / 1024.0, bias=epsT[:, 0:1])
            k.op("dve", lambda E: E.reciprocal(out=nsm[:, 4 + i:5 + i], in_=nsm[:, 2 * i + 1:2 * i + 2]),
                 ["nrs%d" % i], ["nri%d" % i])
            STT("dve", ht[:], X[:, t, :], nsm[:, 4 + i:5 + i], gbc[:], ALU.mult, ALU.mult,
                ["X%d" % t, "nri%d" % i, "gbc"], ["htok%d" % i])
            if hT_out is None:
                return
            for dt in range(8):
                TR(psn[:, dt * 128:(dt + 1) * 128], ht[:, dt * 128:(dt + 1) * 128], identb[:],
                   ["htok%d" % i, "identb"], ["psn"])
            CP("act", hT_out[:, :, col0:col0 + 128], psn[:].rearrange("p (a b) -> p a b", a=8),
               ["psn"], hT_names)

        try:
          with ExitStack() as ph:
            hT = T(ph, "hT", [128, 8, TC], BF16)
            hTn = ["hT%d" % d for d in range(8)]
            load_gain(norm_mix[0:1, :])
            with ExitStack() as phn:
                psn = PS(phn, "psn", [128, 1024], BF16)
                for t in range(NT):
                    norm_tile(t, psn, hT, hTn, t * 128)
                k.barrier()
            ckp(0.1)

            with ExitStack() as ph2:
                are = T(ph2, "are", [32, 128], F32)
                aim = T(ph2, "aim", [32, 128], F32)
                ldt = T(ph2, "ldt", [32, 2], F32)
                dtb = T(ph2, "dtb", [32, 128], F32)
                qa = [T(ph2, "qa%d" % i, [32, 128], F32) for i in range(10)]
                sp6 = T(ph2, "sp6", [128, 6, 32], F32)
                dsk = T(ph2, "dsk", [128, 8], F32)
                dsk8 = T(ph2, "dsk8", [8, 128], F32)
                pA = PS(ph2, "pA", [128, 1024], F32)
                pB = PS(ph2, "pB", [128, 1024], F32)
                pY = PS(ph2, "pY", [128, 1024], F32)
                LD(are[:], a_re.rearrange("(q two) p -> q (two p)", two=2), ["are"])
                LD(aim[:], a_im.rearrange("(q two) p -> q (two p)", two=2), ["aim"])
                LD(ldt[:], log_dt.rearrange("(q two) -> q two", two=2), ["ldt"])
                LD(dsk8[:], ssm_d, ["dsk8"])
                ACT(ldt[:], ldt[:], AF.Exp, ["ldt"], ["ldt"])
                CP("dve", dtb[:].rearrange("q (two p) -> q two p", two=2),
                   ldt[:, :].unsqueeze(2).to_broadcast([32, 2, 64]), ["ldt"], ["dtb"])
                mag, ang, t1, t2, sina, cosa, lbre, lbim, fre, fim = [x[:] for x in qa]
                N = ["qa%d" % i for i in range(10)]
                TT("dve", t1, are[:], dtb[:], ALU.mult, ["are", "dtb"], [N[2]])
                ACT(mag, t1, AF.Exp, [N[2]], [N[0]])
                TT("dve", ang, aim[:], dtb[:], ALU.mult, ["aim", "dtb"], [N[1]])
                TS("dve", ang, ang, INV2PI, ALU.mult, [N[1]], [N[1]])
                TS("dve", t1, ang, MAGIC, ALU.add, [N[1]], [N[2]], s2=MAGIC, op1=ALU.subtract)
                TT("dve", t1, ang, t1, ALU.subtract, [N[1], N[2]], [N[2]])
                ACT(sina, t1, AF.Sin, [N[2]], [N[4]], scale=SC2PI)
                TS("dve", t2, ang, 0.25, ALU.add, [N[1]], [N[3]])
                TS("dve", t1, t2, MAGIC, ALU.add, [N[3]], [N[2]], s2=MAGIC, op1=ALU.subtract)
                TT("dve", t1, t2, t1, ALU.subtract, [N[3], N[2]], [N[2]])
                ACT(cosa, t1, AF.Sin, [N[2]], [N[5]], scale=SC2PI)
                TT("dve", lbre, mag, cosa, ALU.mult, [N[0], N[5]], [N[6]])
                TT("dve", lbim, mag, sina, ALU.mult, [N[0], N[4]], [N[7]])
                TT("dve", t1, are[:], are[:], ALU.mult, ["are"], [N[2]])
                TT("dve", t2, aim[:], aim[:], ALU.mult, ["aim"], [N[3]])
                TT("dve", t1, t1, t2, ALU.add, [N[2], N[3]], [N[2]])
                k.op("dve", lambda E: E.reciprocal(out=sina, in_=t1), [N[2]], [N[4]])
                TS("dve", cosa, lbre, -1.0, ALU.add, [N[6]], [N[5]])
                TT("dve", t1, cosa, are[:], ALU.mult, [N[5], "are"], [N[2]])
                TT("dve", t2, lbim, aim[:], ALU.mult, [N[7], "aim"], [N[3]])
                TT("dve", t1, t1, t2, ALU.add, [N[2], N[3]], [N[2]])
                TT("dve", fre, t1, sina, ALU.mult, [N[2], N[4]], [N[8]])
                TT("dve", t1, lbim, are[:], ALU.mult, [N[7], "are"], [N[2]])
                TT("dve", t2, cosa, aim[:], ALU.mult, [N[5], "aim"], [N[3]])
                TT("dve", t1, t1, t2, ALU.subtract, [N[2], N[3]], [N[2]])
                TT("dve", fim, t1, sina, ALU.mult, [N[2], N[4]], [N[9]])
                for j, (src, nm) in enumerate([(fre, N[8]), (fim, N[9]), (mag, N[0]), (ang, N[1]), (lbre, N[6]), (lbim, N[7])]):
                    TR(pA[:, j * 32:(j + 1) * 32], src, identf[0:32, 0:32], [nm, "identf"], ["pA"])
                CP("act", sp6[:].rearrange("p a b -> p (a b)"), pA[:, 0:192], ["pA"], ["sp6"])
                TR(pA[:, 256:264], dsk8[:], identf[0:8, 0:8], ["dsk8", "identf"], ["pA"])
                CP("act", dsk[:], pA[:, 256:264], ["pA"], ["dsk"])
                F_RE, F_IM, MAG, FQ, LBRE, LBIM = range(6)
                ckp(0.2)

                LT_re = T(ph2, "LT_re", [128, 32, 128], BF16)
                LT_im = T(ph2, "LT_im", [128, 32, 128], BF16)
                Cre = T(ph2, "Cre", [128, 32, 64], BF16)
                Cimn = T(ph2, "Cimn", [128, 32, 64], BF16)
                MS("pool", Cre[:], 0.0, ["Cre"])
                MS("pool", Cimn[:], 0.0, ["Cimn"])
                s0T_re = T(ph2, "s0T_re", [128, 32, 16], F32)
                s0T_im = T(ph2, "s0T_im", [128, 32, 16], F32)
                with ExitStack() as ph3:
                    bre = T(ph3, "bre", [128, 32, 16], F32)
                    bim = T(ph3, "bim", [128, 32, 16], F32)
                    bt1 = T(ph3, "bt1", [128, 32, 16], F32)
                    bt2 = T(ph3, "bt2", [128, 32, 16], F32)
                    bbr = T(ph3, "bbr", [128, 32, 16], F32)
                    bbi = T(ph3, "bbi", [128, 32, 16], F32)
                    Pre = T(ph3, "Pre", [128, 32, 128], BF16)
                    Pim = Pre
                    psb = PS(ph3, "psb", [128, 1024], BF16)
                    LD(bre[:], b_re.rearrange("(q two) p c -> (two p) q c", two=2), ["bre"])
                    LD(bim[:], b_im.rearrange("(q two) p c -> (two p) q c", two=2), ["bim"])
                    frb = sp6[:, F_RE, :].unsqueeze(2).to_broadcast([128, 32, 16])
                    fib = sp6[:, F_IM, :].unsqueeze(2).to_broadcast([128, 32, 16])
                    TT("dve", bt1[:], bre[:], frb, ALU.mult, ["bre", "sp6"], ["bt1"])
                    TT("dve", bt2[:], bim[:], fib, ALU.mult, ["bim", "sp6"], ["bt2"])
                    TT("dve", bbr[:], bt1[:], bt2[:], ALU.subtract, ["bt1", "bt2"], ["bbr"])
                    TT("dve", bt1[:], bre[:], fib, ALU.mult, ["bre", "sp6"], ["bt1"])
                    TT("dve", bt2[:], bim[:], frb, ALU.mult, ["bim", "sp6"], ["bt2"])
                    TT("dve", bbi[:], bt1[:], bt2[:], ALU.add, ["bt1", "bt2"], ["bbi"])
                    MS("pool", Pre[:], 0.0, ["Pre"])
                    for (Pd, bb, nmP, nmb, LT, nmL) in ((Pre, bbr, "Pre", "bbr", LT_re, "LT_re"), (Pre, bbi, "Pre", "bbi", LT_im, "LT_im")):
                        Pv = Pd[:].rearrange("p (qa qb) n -> p qa qb n", qb=4)
                        bv = bb[:].rearrange("p (qa qb) c -> p qa qb c", qb=4)
                        for qb in range(4):
                            CP("dve", Pv[0:64, :, qb, 32 * qb:32 * qb + 16], bv[0:64, :, qb, :], [nmb], [nmP])
                            CP("dve", Pv[64:128, :, qb, 32 * qb + 16:32 * qb + 32], bv[64:128, :, qb, :], [nmb], [nmP])
                        for b8 in range(4):
                            for j in range(8):
                                q = b8 * 8 + j
                                TR(psb[:, j * 128:(j + 1) * 128], Pd[:, q, :], identb[:], [nmP, "identb"], ["psb"])
                            CP("act", LT[:, b8 * 8:(b8 + 1) * 8, :], psb[:].rearrange("p (a b) -> p a b", a=8), ["psb"], [nmL])
                    k.barrier()
                ckp(0.3)
                with ExitStack() as ph3:
                    Sre = T(ph3, "Sre", [32, 32, 128], F32)
                    MS("pool", Sre[:], 0.0, ["Sre"])
                    for (csrc, Cd, nmC, sc) in ((c_re, Cre, "Cre", 1.0), (c_im, Cimn, "Cimn", -1.0)):
                        cvw = csrc.rearrange("(q two) c p -> two c q p", two=2)
                        LD(Sre[0:16, :, 0:64], cvw[0], ["Sre"])
                        LD(Sre[16:32, :, 64:128], cvw[1], ["Sre"])
                        for q in range(32):
                            TR(pA[:, q * 32:(q + 1) * 32], Sre[:, q, :], identf[0:32, 0:32], ["Sre", "identf"], ["pA"])
                        ACT(Cd[:, :, 32:64], pA[:].rearrange("p (a b) -> p a b", a=32), AF.Copy, ["pA"], [nmC], scale=sc)
                    k.barrier()
                ckp(0.35)
                with ExitStack() as ph3:
                    s0sb = T(ph3, "s0sb", [16, 1024], F32)
                    for (src, dst, nmd) in ((s0re, s0T_re, "s0T_re"), (s0im, s0T_im, "s0T_im")):
                        for b8 in range(4):
                            LD(s0sb[:], src[:, b8 * 1024:(b8 + 1) * 1024], ["s0sb"])
                            for j in range(8):
                                TR(pA[:, j * 16:(j + 1) * 16], s0sb[:, j * 128:(j + 1) * 128], identf[0:16, 0:16],
                                   ["s0sb", "identf"], ["pA"])
                            CP("act", dst[:, b8 * 8:(b8 + 1) * 8, :].rearrange("p a b -> p (a b)"), pA[:, 0:128], ["pA"], [nmd])
                    k.barrier()

                ckp(0.4)
                CH = 256
                NCH = 2048 // CH
                iot = T(ph2, "iot", [128, CH], F32)
                csb = [T(ph2, "cs%d" % i, [128, CH], F32) for i in range(2)]
                snb = [T(ph2, "sn%d" % i, [128, CH], F32) for i in range(2)]
                x2 = T(ph2, "x2", [128, CH], F32)
                bpr = [T(ph2, "bpr%d" % i, [128, CH], F32) for i in range(2)]
                bpi = [T(ph2, "bpi%d" % i, [128, CH], F32) for i in range(2)]
                vr = [T(ph2, "vr%d" % i, [128, CH], F32) for i in range(2)]
                vi = [T(ph2, "vi%d" % i, [128, CH], F32) for i in range(2)]
                zr = [T(ph2, "zr%d" % i, [128, CH], BF16) for i in range(2)]
                zi = [T(ph2, "zi%d" % i, [128, CH], BF16) for i in range(2)]
                u1 = T(ph2, "u1", [128, CH], F32)
                u2 = T(ph2, "u2", [128, CH], F32)
                d1 = T(ph2, "d1", [128, CH], F32)
                d2 = T(ph2, "d2", [128, CH], F32)
                ypre = T(ph2, "ypre", [128, CH], F32)
                tiny = T(ph2, "tiny", [128, 16], F32)
                cinit = T(ph2, "cinit", [128, 2, 2], F32)
                fin_re = T(ph2, "fin_re", [128, 32], F32)
                fin_im = T(ph2, "fin_im", [128, 32], F32)
                fins_re = T(ph2, "fins_re", [128, 32, 16], F32)
                fins_im = T(ph2, "fins_im", [128, 32, 16], F32)
                m0 = T(ph2, "m0", [128, 16, 4], F32)
                decs = T(ph2, "decs", [128, 64], F32)
                sm = [T(ph2, "sm%d" % i, [128, 64], F32) for i in range(6)]
                zsb = [T(ph2, "zsb%d" % i, [128, 64], BF16) for i in range(2)]
                lam = T(ph2, "lam", [128, 2, 16], F32)

                k.op("pool", lambda E: E.iota(iot[:], pattern=[[1, CH]], base=0, channel_multiplier=0,
                                              allow_small_or_imprecise_dtypes=True), (), ["iot"])
                MS("dve", m0[:], 1.0, ["m0"])
                MS("dve", m0[:, :, 0:1], 0.0, ["m0"])

                import os as _os
                QSTOP = int(_os.environ.get("QSTOP", "99"))
                for q in range(32):
                    if q == QSTOP:
                        ckp(0.65)
                    dti = q // 4
                    qq = q % 4
                    pi = q % 2
                    cs, sn = csb[pi], snb[pi]
                    csn, snn = "cs%d" % pi, "sn%d" % pi
                    fq = sp6[:, FQ, q:q + 1]
                    magq = sp6[:, MAG, q:q + 1]
                    TS("pool", sn[:], iot[:], fq, ALU.mult, ["iot", "sp6"], [snn])
                    TS("pool", x2[:], sn[:], MAGIC, ALU.add, [snn], ["x2"], s2=MAGIC, op1=ALU.subtract)
                    TT("pool", sn[:], sn[:], x2[:], ALU.subtract, [snn, "x2"], [snn])
                    ACT(sn[:], sn[:], AF.Sin, [snn], [snn], scale=SC2PI)
                    TS("pool", cs[:], iot[:], fq, ALU.mult, ["iot", "sp6"], [csn], s2=0.25, op1=ALU.add)
                    TS("pool", x2[:], cs[:], MAGIC, ALU.add, [csn], ["x2"], s2=MAGIC, op1=ALU.subtract)
                    TT("pool", cs[:], cs[:], x2[:], ALU.subtract, [csn, "x2"], [csn])
                    ACT(cs[:], cs[:], AF.Sin, [csn], [csn], scale=SC2PI)

                    ckp(0.5)
                    for c in range(NCH):
                        bi = c % 2
                        cols = slice(c * CH, (c + 1) * CH)
                        MM(pA[:, bi * 512:bi * 512 + CH], LT_re[:, q, :], hT[:, dti, cols], True, True,
                           ["LT_re", hTn[dti]], ["pA%d" % bi])
                        MM(pB[:, bi * 512:bi * 512 + CH], LT_im[:, q, :], hT[:, dti, cols], True, True,
                           ["LT_im", hTn[dti]], ["pB%d" % bi])
                        par = pA[:, bi * 512:bi * 512 + CH]
                        pai = pB[:, bi * 512:bi * 512 + CH]
                        TT("dve", d1[:], par, cs[:], ALU.mult, ["pA%d" % bi, csn], ["d1"])
                        TT("dve", d2[:], pai, sn[:], ALU.mult, ["pB%d" % bi, snn], ["d2"])
                        TT("dve", bpr[bi][:], d1[:], d2[:], ALU.add, ["d1", "d2"], ["bpr%d" % bi])
                        TT("dve", d1[:], pai, cs[:], ALU.mult, ["pB%d" % bi, csn], ["d1"])
                        TT("dve", d2[:], par, sn[:], ALU.mult, ["pA%d" % bi, snn], ["d2"])
                        TT("dve", bpi[bi][:], d1[:], d2[:], ALU.subtract, ["d1", "d2"], ["bpi%d" % bi])
                        magb = magq.to_broadcast([128, CH])
                        if c == 0:
                            ir, ii = 0.0, 0.0
                            rd = []
                        else:
                            ir, ii = cinit[:, 0, 0:1], cinit[:, 0, 1:2]
                            rd = ["cinit"]
                        k.op("dve", lambda E: E.tensor_tensor_scan(out=vr[bi][:], data0=magb, data1=bpr[bi][:], initial=ir,
                                                                   op0=ALU.mult, op1=ALU.add),
                             ["sp6", "bpr%d" % bi] + rd, ["vr%d" % bi])
                        k.op("dve", lambda E: E.tensor_tensor_scan(out=vi[bi][:], data0=magb, data1=bpi[bi][:], initial=ii,
                                                                   op0=ALU.mult, op1=ALU.add),
                             ["sp6", "bpi%d" % bi] + rd, ["vi%d" % bi])
                        TT("pool", u1[:], vr[bi][:], cs[:], ALU.mult, ["vr%d" % bi, csn], ["u1"])
                        TT("pool", u2[:], vi[bi][:], sn[:], ALU.mult, ["vi%d" % bi, snn], ["u2"])
                        TT("pool", zr[bi][:], u1[:], u2[:], ALU.subtract, ["u1", "u2"], ["zr%d" % bi])
                        TT("pool", u1[:], vr[bi][:], sn[:], ALU.mult, ["vr%d" % bi, snn], ["u1"])
                        TT("pool", u2[:], vi[bi][:], cs[:], ALU.mult, ["vi%d" % bi, csn], ["u2"])
                        TT("pool", zi[bi][:], u1[:], u2[:], ALU.add, ["u1", "u2"], ["zi%d" % bi])
                        L = slice(CH - 1, CH)
                        TS("dve", tiny[:, 0:1], vi[bi][:, L], sn[:, L], ALU.mult, ["vi%d" % bi, snn], ["tiny"])
                        STT("dve", tiny[:, 1:2], vr[bi][:, L], cs[:, L], tiny[:, 0:1], ALU.mult, ALU.subtract,
                            ["vr%d" % bi, csn, "tiny"], ["tiny"])
                        TS("dve", tiny[:, 2:3], vi[bi][:, L], cs[:, L], ALU.mult, ["vi%d" % bi, csn], ["tiny"])
                        STT("dve", tiny[:, 3:4], vr[bi][:, L], sn[:, L], tiny[:, 2:3], ALU.mult, ALU.add,
                            ["vr%d" % bi, snn, "tiny"], ["tiny"])
                        if c < NCH - 1:
                            TS("dve", tiny[:, 4:5], tiny[:, 3:4], sn[:, 1:2], ALU.mult, ["tiny", snn], ["tiny"])
                            STT("dve", cinit[:, 0, 0:1], tiny[:, 1:2], cs[:, 1:2], tiny[:, 4:5], ALU.mult, ALU.subtract,
                                ["tiny", csn], ["cinit"])
                            TS("dve", tiny[:, 5:6], tiny[:, 3:4], cs[:, 1:2], ALU.mult, ["tiny", csn], ["tiny"])
                            STT("dve", cinit[:, 0, 1:2], tiny[:, 1:2], sn[:, 1:2], tiny[:, 5:6], ALU.mult, ALU.add,
                                ["tiny", snn], ["cinit"])
                        else:
                            CP("dve", fin_re[:, q:q + 1], tiny[:, 1:2], ["tiny"], ["fin_re"])
                            CP("dve", fin_im[:, q:q + 1], tiny[:, 3:4], ["tiny"], ["fin_im"])
                        yo = pY[32 * qq:32 * qq + 32, bi * 512:bi * 512 + CH]
                        if qq % 2 == 0:
                            ymm, csl = yo, slice(32, 64)
                        else:
                            ymm, csl = pY[32 * (qq - 1):32 * (qq + 1), bi * 512:bi * 512 + CH], slice(0, 64)
                        MM(ymm, Cre[:, q, csl], zr[bi][:], True, False, ["Cre", "zr%d" % bi], ["pY%d" % bi])
                        MM(ymm, Cimn[:, q, csl], zi[bi][:], False, True, ["Cimn", "zi%d" % bi], ["pY%d" % bi])
                        rows = slice(32 * qq, 32 * qq + 32)
                        STT("dve", ypre[rows, :], hT[rows, dti, cols], dsk[rows, dti:dti + 1], yo, ALU.mult, ALU.add,
                            [hTn[dti], "dsk", "pY%d" % bi], ["ypre"])
                        ACT(hT[rows, dti, cols], ypre[rows, :], AF.Gelu, ["ypre"], [hTn[dti]])

                    ckp(0.55)
                    scol = slice(2048, 2112)
                    MM(pA[:, 0:64], LT_re[:, q, :], hT[:, dti, scol], True, True, ["LT_re", hTn[dti]], ["pA0"])
                    MM(pB[:, 0:64], LT_im[:, q, :], hT[:, dti, scol], True, True, ["LT_im", hTn[dti]], ["pB0"])
                    v3 = lambda ap: ap.rearrange("p (s t) -> p s t", t=4)
                    cs4 = cs[:, 0:4].unsqueeze(1).to_broadcast([128, 16, 4])
                    sn4 = sn[:, 0:4].unsqueeze(1).to_broadcast([128, 16, 4])
                    par = v3(pA[:, 0:64])
                    pai = v3(pB[:, 0:64])
                    s0, s1, s2_, s3, s4, s5 = [v3(x[:]) for x in sm]
                    TT("dve", s0, par, cs4, ALU.mult, ["pA0", csn], ["sm0"])
                    TT("dve", s1, pai, sn4, ALU.mult, ["pB0", snn], ["sm1"])
                    TT("dve", s2_, s0, s1, ALU.add, ["sm0", "sm1"], ["sm2"])
                    TT("dve", s0, pai, cs4, ALU.mult, ["pB0", csn], ["sm0"])
                    TT("dve", s1, par, sn4, ALU.mult, ["pA0", snn], ["sm1"])
                    TT("dve", s3, s0, s1, ALU.subtract, ["sm0", "sm1"], ["sm3"])
                    lbr = sp6[:, LBRE, q:q + 1]
                    lbi = sp6[:, LBIM, q:q + 1]
                    TS("dve", lam[:, 0, :], s0T_im[:, q, :], lbi, ALU.mult, ["s0T_im", "sp6"], ["lam"])
                    STT("dve", lam[:, 0, :], s0T_re[:, q, :], lbr, lam[:, 0, :], ALU.mult, ALU.subtract,
                        ["s0T_re", "sp6", "lam"], ["lam"])
                    TS("dve", lam[:, 1, :], s0T_re[:, q, :], lbi, ALU.mult, ["s0T_re", "sp6"], ["lam"])
                    STT("dve", lam[:, 1, :], s0T_im[:, q, :], lbr, lam[:, 1, :], ALU.mult, ALU.add,
                        ["s0T_im", "sp6", "lam"], ["lam"])
                    TT("dve", s2_[:, :, 0:1], s2_[:, :, 0:1], lam[:, 0, :].unsqueeze(2), ALU.add, ["sm2", "lam"], ["sm2"])
                    TT("dve", s3[:, :, 0:1], s3[:, :, 0:1], lam[:, 1, :].unsqueeze(2), ALU.add, ["sm3", "lam"], ["sm3"])
                    TS("dve", decs[:], m0[:].rearrange("p s t -> p (s t)"), magq, ALU.mult, ["m0", "sp6"], ["decs"])
                    k.op("dve", lambda E: E.tensor_tensor_scan(out=sm[4][:], data0=decs[:], data1=sm[2][:], initial=0.0,
                                                               op0=ALU.mult, op1=ALU.add), ["decs", "sm2"], ["sm4"])
                    k.op("dve", lambda E: E.tensor_tensor_scan(out=sm[5][:], data0=decs[:], data1=sm[3][:], initial=0.0,
                                                               op0=ALU.mult, op1=ALU.add), ["decs", "sm3"], ["sm5"])
                    TT("dve", s0, s4, cs4, ALU.mult, ["sm4", csn], ["sm0"])
                    TT("dve", s1, s5, sn4, ALU.mult, ["sm5", snn], ["sm1"])
                    TT("dve", s2_, s0, s1, ALU.subtract, ["sm0", "sm1"], ["sm2"])
                    TT("dve", s0, s4, sn4, ALU.mult, ["sm4", snn], ["sm0"])
                    TT("dve", s1, s5, cs4, ALU.mult, ["sm5", csn], ["sm1"])
                    TT("dve", s3, s0, s1, ALU.add, ["sm0", "sm1"], ["sm3"])
                    CP("dve", zsb[0][:], sm[2][:], ["sm2"], ["zsb0"])
                    CP("dve", zsb[1][:], sm[3][:], ["sm3"], ["zsb1"])
                    CP("dve", fins_re[:, q, :].unsqueeze(2), s2_[:, :, 3:4], ["sm2"], ["fins_re"])
                    CP("dve", fins_im[:, q, :].unsqueeze(2), s3[:, :, 3:4], ["sm3"], ["fins_im"])
                    yo = pY[32 * qq:32 * qq + 32, 0:64]
                    if qq % 2 == 0:
                        ymm, csl = yo, slice(32, 64)
                    else:
                        ymm, csl = pY[32 * (qq - 1):32 * (qq + 1), 0:64], slice(0, 64)
                    MM(ymm, Cre[:, q, csl], zsb[0][:], True, False, ["Cre", "zsb0"], ["pY0"])
                    MM(ymm, Cimn[:, q, csl], zsb[1][:], False, True, ["Cimn", "zsb1"], ["pY0"])
                    rows = slice(32 * qq, 32 * qq + 32)
                    STT("dve", ypre[rows, 0:64], hT[rows, dti, scol], dsk[rows, dti:dti + 1], yo, ALU.mult, ALU.add,
                        [hTn[dti], "dsk", "pY0"], ["ypre"])
                    ACT(hT[rows, dti, scol], ypre[rows, 0:64], AF.Gelu, ["ypre"], [hTn[dti]])

                    ckp(0.6)
                ckp(0.7)
                stg = T(ph2, "stg", [32, 128], F32)
                stg2 = T(ph2, "stg2", [16, 1024], F32)
                for (fin, dst, nm) in ((fin_re, o_sre_p, "fin_re"), (fin_im, o_sim_p, "fin_im")):
                    TR(pA[0:32, 0:128], fin[:], identf[:], [nm, "identf"], ["pA0"])
                    CP("act", stg[:], pA[0:32, 0:128], ["pA0"], ["stg"])
                    LD(dst, stg[:], [], ["stg"])
                for (fin, dst, nm) in ((fins_re, o_sre_s, "fins_re"), (fins_im, o_sim_s, "fins_im")):
                    for b8 in range(4):
                        for j in range(8):
                            q = b8 * 8 + j
                            TR(pB[0:16, j * 128:(j + 1) * 128], fin[:, q, :], identf[:], [nm, "identf"], ["pB0"])
                        CP("act", stg2[:], pB[0:16, :], ["pB0"], ["stg2"])
                        LD(dst[:, b8 * 1024:(b8 + 1) * 1024], stg2[:], [], ["stg2"])
                k.barrier()

            ckp(0.8)
            with ExitStack() as ph2:
                wglu = T(ph2, "wglu", [128, 8, 2048], BF16)
                sig = T(ph2, "sig", [128, 1024], F32)
                gtmp = T(ph2, "gtmp", [128, 1024], F32)
                pg = [PS(ph2, "pg%d" % i, [128, 2048], F32) for i in range(2)]
                wv_ = w_glu.rearrange("(dt p) f -> p dt f", p=128)
                for dt in range(8):
                    LDC(wglu[:, dt, :], wv_[:, dt, :], ["wglu"])
                for t in range(NT):
                    pi = t % 2
                    for fc in range(4):
                        for dt in range(8):
                            MM(pg[pi][:, fc * 512:(fc + 1) * 512], hT[:, dt, t * 128:(t + 1) * 128],
                               wglu[:, dt, fc * 512:(fc + 1) * 512], dt == 0, dt == 7,
                               [hTn[dt], "wglu"], ["pg%d" % pi])
                    ACT(sig[:], pg[pi][:, 1024:2048], AF.Sigmoid, ["pg%d" % pi], ["sig"])
                    TT("dve", gtmp[:], pg[pi][:, 0:1024], sig[:], ALU.mult, ["pg%d" % pi, "sig"], ["gtmp"])
                    TT("dve", X[:, t, :], X[:, t, :], gtmp[:], ALU.add, ["X%d" % t, "gtmp"], ["X%d" % t])
                k.barrier()

        except _Stop:
            pass
        k.halt = False

        def attention(l):
            with ExitStack() as ph:
                psn = PS(ph, "psn", [128, 1024], BF16)
                pA = PS(ph, "pA", [128, 1024], F32)
                pB = PS(ph, "pB", [128, 2048], F32)
                psx = PS(ph, "psx", [128, 1024], BF16)
                KT = T(ph, "KT", [128, 8, 256], BF16)
                Vb = T(ph, "Vb", [128, 2, 1024], BF16)
                load_gain(norm_mem[l:l + 1, :])
                with ExitStack() as ph2:
                    wk = T(ph2, "wk", [128, 8, 1024], BF16)
                    wv = T(ph2, "wv", [128, 8, 1024], BF16)
                    kst = [T(ph2, "kst%d" % i, [128, 1024], F32) for i in range(2)]
                    wkv = mem_w_k[l].rearrange("(dt p) f -> p dt f", p=128)
                    wvv = mem_w_v[l].rearrange("(dt p) f -> p dt f", p=128)
                    for dt in range(8):
                        LDC(wk[:, dt, :], wkv[:, dt, :], ["wk"])
                        LDC(wv[:, dt, :], wvv[:, dt, :], ["wv"])
                    if l == 0:
                        memb = T(ph2, "memb", [128, 2, 1024], BF16)
                        for mt in range(2):
                            LDC(memb[:, mt, :], memp[mt * 128:(mt + 1) * 128, :], ["memb"])
                        for mt in range(2):
                            for dt in range(8):
                                TR(psx[:, dt * 128:(dt + 1) * 128], memb[:, mt, dt * 128:(dt + 1) * 128], identb[:],
                                   ["memb", "identb"], ["psx"])
                            CP("act", memT[:, :, mt * 128:(mt + 1) * 128], psx[:].rearrange("p (a b) -> p a b", a=8),
                               ["psx"], ["memT"])
                    ckp(1.25)
                    n = 0
                    for (w_, wn, dst, isv) in ((wk, "wk", o_mk, False), (wv, "wv", o_mv, True)):
                        if isv and _os2.environ.get("NOV"):
                            continue
                        for mt in range(2):
                            ks = kst[n % 2]
                            ksn = "kst%d" % (n % 2)
                            n += 1
                            for fc in range(2):
                                for dt in range(8):
                                    MM(pA[:, fc * 512:(fc + 1) * 512], memT[:, dt, mt * 128:(mt + 1) * 128],
                                       w_[:, dt, fc * 512:(fc + 1) * 512], dt == 0, dt == 7, ["memT", wn], ["pA"])
                            CP("act", ks[:], pA[:], ["pA"], [ksn])
                            if isv:
                                CP("dve", Vb[:, mt, :], ks[:], [ksn], ["Vb"])
                            if NOOUT is None:
                                LD(dst[l * 256 + mt * 128:l * 256 + (mt + 1) * 128, :], ks[:], [], [ksn])
                    ckp(1.3)
                    for ft in range(8):
                        for dt in range(8):
                            MM(pB[:, (ft % 4) * 512:(ft % 4) * 512 + 256], wk[:, dt, ft * 128:(ft + 1) * 128], memT[:, dt, :],
                               dt == 0, dt == 7, ["wk", "memT"], ["pB"])
                        CP("act", KT[:, ft, :], pB[:, (ft % 4) * 512:(ft % 4) * 512 + 256], ["pB"], ["KT"])
                    k.barrier()
                ckp(1.4)

                with ExitStack() as ph2:
                    wq = T(ph2, "wq", [128, 8, 1024], BF16)
                    wo = T(ph2, "wo", [128, 8, 1024], BF16)
                    hTg = T(ph2, "hTg", [128, 8, 512], BF16)
                    qT = T(ph2, "qT", [128, 8, 512], BF16)
                    Pf = T(ph2, "Pf", [128, 4, 256], F32)
                    Pn = T(ph2, "Pn", [128, 4, 256], BF16)
                    PT = T(ph2, "PT", [128, 8, 128], BF16)
                    oT = T(ph2, "oT", [128, 8, 128], BF16)
                    stat = T(ph2, "stat", [128, 16], F32)
                    Kc = T(ph2, "Kc", [128, 2, 1024], BF16)
                    Vc = [T(ph2, "Vc%d" % i, [128, 2, 1024], BF16) for i in range(2)]
                    KTc = [T(ph2, "KTc%d" % i, [128, 8, 256], BF16) for i in range(2)]
                    qTm = [T(ph2, "qTm%d" % i, [128, 8, 128], BF16) for i in range(2)]
                    wqv = mem_w_q[l].rearrange("(dt p) f -> p dt f", p=128)
                    wov = mem_w_o[l].rearrange("(dt p) f -> p dt f", p=128)
                    for dt in range(8):
                        LDC(wq[:, dt, :], wqv[:, dt, :], ["wq"])
                        LDC(wo[:, dt, :], wov[:, dt, :], ["wo"])

                    def softmax_pv_o(t, nheadbanks):
                        hs = 512 if nheadbanks == 4 else 256
                        sview = pB[:, 0:4 * hs].rearrange("p (h m) -> p h m", h=4)[:, :, 0:256]
                        k.op("dve", lambda E: E.tensor_reduce(out=stat[:, 0:4], in_=sview, axis=AX.X, op=ALU.max),
                             ["pB"], ["stat"])
                        TS("dve", stat[:, 4:8], stat[:, 0:4], -1.0, ALU.mult, ["stat"], ["stat"])
                        for h in range(4):
                            ACT(Pf[:, h, :], pB[:, h * hs:h * hs + 256], AF.Exp, ["pB", "stat"], ["Pf", "stat"],
                                bias=stat[:, 4 + h:5 + h], scale=1.0, accum_out=stat[:, 8 + h:9 + h])
                        k.op("dve", lambda E: E.reciprocal(out=stat[:, 12:16], in_=stat[:, 8:12]), ["stat"], ["stat"])
                        TT("dve", Pn[:], Pf[:], stat[:, 12:16].unsqueeze(2).to_broadcast([128, 4, 256]), ALU.mult,
                           ["Pf", "stat"], ["Pn"])
                        for h in range(4):
                            for mt in range(2):
                                TR(psx[:, (h * 2 + mt) * 128:(h * 2 + mt + 1) * 128], Pn[:, h, mt * 128:(mt + 1) * 128],
                                   identb[:], ["Pn", "identb"], ["psx"])
                        CP("act", PT[:], psx[:].rearrange("p (a b) -> p a b", a=8), ["psx"], ["PT"])

                    def oproj(t):
                        CP("act", oT[:], pA[:].rearrange("p (a b) -> p a b", a=8), ["pA"], ["oT"])
                        for fc in range(2):
                            for ft in range(8):
                                MM(pA[:, fc * 512:(fc + 1) * 512], oT[:, ft, :], wo[:, ft, fc * 512:(fc + 1) * 512],
                                   ft == 0, ft == 7, ["oT", "wo"], ["pA"])
                        TT("dve", X[:, t, :], X[:, t, :], pA[:], ALU.add, ["X%d" % t, "pA"], ["X%d" % t])

                    for g in range(4):
                        for j in range(4):
                            norm_tile(4 * g + j, psn, hTg, ["hTg"], j * 128)
                        for ft in range(8):
                            bo = 1024 + (ft % 2) * 512
                            for dt in range(8):
                                MM(pB[:, bo:bo + 512], wq[:, dt, ft * 128:(ft + 1) * 128], hTg[:, dt, :], dt == 0, dt == 7,
                                   ["wq", "hTg"], ["pBq%d" % (ft % 2)])
                            ACT(qT[:, ft, :], pB[:, bo:bo + 512], AF.Copy, ["pBq%d" % (ft % 2)], ["qT"], scale=1.0 / 16.0)
                        for j in range(4):
                            t = 4 * g + j
                            tc_ = slice(j * 128, (j + 1) * 128)
                            for h in range(4):
                                for jj in range(2):
                                    MM(pB[:, h * 256:(h + 1) * 256], qT[:, 2 * h + jj, tc_], KT[:, 2 * h + jj, :],
                                       jj == 0, jj == 1, ["qT", "KT"], ["pB"])
                            if j == 0 and g == 0:
                                ckp(1.45)
                            softmax_pv_o(t, 2)
                            if j == 0 and g == 0:
                                ckp(1.5)
                            for ft in range(8):
                                h = ft // 2
                                for mt in range(2):
                                    MM(pA[:, ft * 128:(ft + 1) * 128], Vb[:, mt, ft * 128:(ft + 1) * 128], PT[:, h * 2 + mt, :],
                                       mt == 0, mt == 1, ["Vb", "PT"], ["pA"])
                            oproj(t)
                            if j == 0 and g == 0:
                                ckp(1.6)
                    ckp(1.7)
                    t = 16
                    norm_tile(t, psn, hTg, ["hTg"], 0)
                    for ft in range(8):
                        for dt in range(8):
                            MM(pA[:, ft * 128:(ft + 1) * 128], wq[:, dt, ft * 128:(ft + 1) * 128], hTg[:, dt, 0:128],
                               dt == 0, dt == 7, ["wq", "hTg"], ["pA"])
                    ACT(qT[:, :, 0:128], pA[:].rearrange("p (a b) -> p a b", a=8), AF.Copy, ["pA"], ["qT"], scale=1.0 / 16.0)
                    for i in range(16):
                        pi = i % 2
                        for mt in range(2):
                            LDC(Kc[:, mt, :], ck[l, i, mt * 128:(mt + 1) * 128, :], ["Kc"])
                        for mt in range(2):
                            for ft in range(8):
                                TR(psx[:, ft * 128:(ft + 1) * 128], Kc[:, mt, ft * 128:(ft + 1) * 128], identb[:],
                                   ["Kc", "identb"], ["psx"])
                            CP("act", KTc[pi][:, :, mt * 128:(mt + 1) * 128], psx[:].rearrange("p (a b) -> p a b", a=8),
                               ["psx"], ["KTc%d" % pi])
                        MS("pool", qTm[pi][:], 0.0, ["qTm%d" % pi])
                        CP("pool", qTm[pi][:, :, 4 * i:4 * i + 4], qT[:, :, 4 * i:4 * i + 4], ["qT"], ["qTm%d" % pi])
                        for h in range(4):
                            for jj in range(2):
                                MM(pB[:, h * 512:h * 512 + 256], qTm[pi][:, 2 * h + jj, :], KTc[pi][:, 2 * h + jj, :],
                                   (i == 0 and jj == 0), (i == 15 and jj == 1), ["qTm%d" % pi, "KTc%d" % pi], ["pB"])
                    ckp(1.8)
                    softmax_pv_o(t, 4)
                    MS("dve", oT[:], 0.0, ["oT"])
                    for i in range(16):
                        pi = i % 2
                        for mt in range(2):
                            LDC(Vc[pi][:, mt, :], cv[l, i, mt * 128:(mt + 1) * 128, :], ["Vc%d" % pi])
                        for ft in range(8):
                            h = ft // 2
                            for mt in range(2):
                                MM(pA[:, ft * 128 + 4 * i:ft * 128 + 4 * i + 4], Vc[pi][:, mt, ft * 128:(ft + 1) * 128],
                                   PT[:, h * 2 + mt, 4 * i:4 * i + 4], mt == 0, mt == 1, ["Vc%d" % pi, "PT"], ["pA"])
                    ckp(1.9)
                    pAv = pA[:].rearrange("p (a b) -> p a b", a=8)
                    CP("act", oT[:, :, 0:64], pAv[:, :, 0:64], ["pA"], ["oT"])
                    for fc in range(2):
                        for ft in range(8):
                            MM(pA[:, fc * 512:(fc + 1) * 512], oT[:, ft, :], wo[:, ft, fc * 512:(fc + 1) * 512],
                               ft == 0, ft == 7, ["oT", "wo"], ["pA"])
                    TT("dve", X[0:64, t, :], X[0:64, t, :], pA[0:64, :], ALU.add, ["X16", "pA"], ["X16"])
                    k.barrier()

        def peer(l):
            NU = 4
            NV = 4
            with ExitStack() as ph:
                psn = PS(ph, "psn", [128, 1024], BF16)
                pQ = PS(ph, "pQ", [128, 2048], F32)
                pO = PS(ph, "pO", [128, 1024], F32)
                wqp = T(ph, "wqp", [128, 8, 2048], BF16)
                keyT = T(ph, "keyT", [128, 2, 128], BF16)
                kld = T(ph, "kld", [128, 128], F32)
                hTt = T(ph, "hTt", [128, 8, 128], BF16)
                qpT = T(ph, "qpT", [128, 16, 128], BF16)
                V12 = T(ph, "V12", [128, 16, 16], F32)
                I12u = T(ph, "I12u", [128, 16, 16], U32)
                I12f = T(ph, "I12f", [128, 16, 16], F32)
                work = T(ph, "work", [128, 128], F32)
                cand = T(ph, "cand", [128, 8, 256], F32)
                work2 = T(ph, "work2", [128, 256], F32)
                scv = T(ph, "scv", [128, 8, 16], F32)
                ciu = T(ph, "ciu", [128, 8, 16], U32)
                cff = T(ph, "cff", [128, 128], F32)
                caf = T(ph, "caf", [128, 128], F32)
                cbf = T(ph, "cbf", [128, 128], F32)
                io16 = T(ph, "io16", [128, 16], F32)
                eq = cand[:].rearrange("p h (k a) -> p (h k) a", a=16)
                isel = T(ph, "isel", [128, 2, 128], F32)
                ef = T(ph, "ef", [128, 128], F32)
                eu = T(ph, "eu", [128, 128], U32)
                gsm = T(ph, "gsm", [128, 8, 16], F32)
                gst = T(ph, "gst", [128, 16], F32)
                actv = T(ph, "actv", [128, 128], F32)
                wgt = T(ph, "wgt", [128, 128], F32)
                Ub = [T(ph, "Ub%d" % i, [128, 1024], F32) for i in range(NU)]
                Vg = [T(ph, "Vg%d" % i, [128, 1024], F32) for i in range(NV)]
                Vh = [T(ph, "Vh%d" % i, [128, 1024], BF16) for i in range(2)]
                dg = [T(ph, "dg%d" % i, [128, 128], BF16) for i in range(4)]
                load_gain(norm_ffn[l:l + 1, :])
                wv_ = peer_wq[l].rearrange("(dt p) f -> p dt f", p=128)
                for dt in range(8):
                    LDC(wqp[:, dt, :], wv_[:, dt, :], ["wqp"])
                for j, ksrc in enumerate((peer_k1, peer_k2)):
                    LD(kld[:], ksrc[l], ["kld"])
                    TR(pO[:, 0:128], kld[:], identf[:], ["kld", "identf"], ["pO"])
                    CP("act", keyT[:, j, :], pO[:, 0:128], ["pO"], ["keyT"])
                k.op("pool", lambda E: E.iota(io16[:], pattern=[[1, 16]], base=0, channel_multiplier=0,
                                              allow_small_or_imprecise_dtypes=True), (), ["io16"])
                for t in range(NT):
                    norm_tile(t, psn, hTt, ["hTt"], 0)
                    ht = htok[t % 2]
                    htn = "htok%d" % (t % 2)
                    for ft in range(16):
                        for dt in range(8):
                            MM(pQ[:, ft * 128:(ft + 1) * 128], wqp[:, dt, ft * 128:(ft + 1) * 128], hTt[:, dt, :],
                               dt == 0, dt == 7, ["wqp", "hTt"], ["pQ"])
                    CP("act", qpT[:], pQ[:].rearrange("p (a b) -> p a b", a=16), ["pQ"], ["qpT"])
                    for blk in range(16):
                        MM(pQ[:, blk * 128:(blk + 1) * 128], qpT[:, blk, :], keyT[:, blk % 2, :], True, True,
                           ["qpT", "keyT"], ["pQ"])
                    for blk in range(16):
                        sb = pQ[:, blk * 128:(blk + 1) * 128]
                        k.op("dve", lambda E: E.max(out=V12[:, blk, 0:8], in_=sb), ["pQ"], ["V12"])
                        k.op("dve", lambda E: E.max_index(out=I12u[:, blk, 0:8], in_max=V12[:, blk, 0:8], in_values=sb),
                             ["pQ", "V12"], ["I12u"])
                        k.op("dve", lambda E: E.match_replace(out=work[:], in_to_replace=V12[:, blk, 0:8], in_values=sb,
                                                              imm_value=-1e30), ["pQ", "V12"], ["work"])
                        k.op("dve", lambda E: E.max(out=V12[:, blk, 8:16], in_=work[:]), ["work"], ["V12"])
                        k.op("dve", lambda E: E.max_index(out=I12u[:, blk, 8:16], in_max=V12[:, blk, 8:16], in_values=work[:]),
                             ["work", "V12"], ["I12u"])
                    CP("dve", I12f[:], I12u[:], ["I12u"], ["I12f"])
                    V4 = V12[:].rearrange("p (h two) a -> p h two a", two=2)
                    I4 = I12f[:].rearrange("p (h two) a -> p h two a", two=2)
                    TT("dve", cand[:].rearrange("p h (a b) -> p h a b", a=16),
                       V4[:, :, 0, :].unsqueeze(3).to_broadcast([128, 8, 16, 16]),
                       V4[:, :, 1, :].unsqueeze(2).to_broadcast([128, 8, 16, 16]), ALU.add, ["V12"], ["cand"])
                    for h in range(8):
                        k.op("dve", lambda E: E.max(out=scv[:, h, 0:8], in_=cand[:, h, :]), ["cand"], ["scv"])
                        k.op("dve", lambda E: E.max_index(out=ciu[:, h, 0:8], in_max=scv[:, h, 0:8], in_values=cand[:, h, :]),
                             ["cand", "scv"], ["ciu"])
                        k.op("dve", lambda E: E.match_replace(out=work2[:], in_to_replace=scv[:, h, 0:8], in_values=cand[:, h, :],
                                                              imm_value=-1e30), ["cand", "scv"], ["work2"])
                        k.op("dve", lambda E: E.max(out=scv[:, h, 8:16], in_=work2[:]), ["work2"], ["scv"])
                        k.op("dve", lambda E: E.max_index(out=ciu[:, h, 8:16], in_max=scv[:, h, 8:16], in_values=work2[:]),
                             ["work2", "scv"], ["ciu"])
                    civ = ciu[:].rearrange("p h k -> p (h k)")
                    CP("dve", cff[:], civ, ["ciu"], ["cff"])
                    TS("dve", caf[:], cff[:], 1.0 / 16.0, ALU.mult, ["cff"], ["caf"], s2=-0.46875, op1=ALU.add)
                    TS("dve", caf[:], caf[:], MAGIC, ALU.add, ["caf"], ["caf"], s2=MAGIC, op1=ALU.subtract)
                    STT("dve", cbf[:], caf[:], -16.0, cff[:], ALU.mult, ALU.add, ["caf", "cff"], ["cbf"])
                    io_b = io16[:, :].unsqueeze(1).to_broadcast([128, 128, 16])
                    for half, cf, cfn in ((0, caf, "caf"), (1, cbf, "cbf")):
                        TT("dve", eq, io_b, cf[:, :].unsqueeze(2).to_broadcast([128, 128, 16]), ALU.is_equal,
                           ["io16", cfn], ["cand"])
                        eq4 = cand[:].rearrange("p h (k a) -> p h k a", a=16)
                        TT("dve", eq4, eq4, I4[:, :, half, :].unsqueeze(2).to_broadcast([128, 8, 16, 16]), ALU.mult,
                           ["cand", "I12f"], ["cand"])
                        k.op("dve", lambda E: E.tensor_reduce(out=isel[:, half, :], in_=eq, axis=AX.X, op=ALU.add),
                             ["cand"], ["isel"])
                    STT("dve", ef[:], isel[:, 0, :], 128.0, isel[:, 1, :], ALU.mult, ALU.add, ["isel"], ["ef"])
                    TS("dve", ef[:], ef[:], float(16384 * l), ALU.add, ["ef"], ["ef"])
                    CP("dve", eu[:], ef[:], ["ef"], ["eu"])
                    TT("dve", gsm[:], scv[:], scv[:, :, 0:1].to_broadcast([128, 8, 16]), ALU.subtract, ["scv"], ["gsm"])
                    ACT(gsm[:], gsm[:], AF.Exp, ["gsm"], ["gsm"])
                    k.op("dve", lambda E: E.tensor_reduce(out=gst[:, 0:8], in_=gsm[:], axis=AX.X, op=ALU.add), ["gsm"], ["gst"])
                    k.op("dve", lambda E: E.reciprocal(out=gst[:, 8:16], in_=gst[:, 0:8]), ["gst"], ["gst"])
                    TT("dve", gsm[:], gsm[:], gst[:, 8:16].unsqueeze(2).to_broadcast([128, 8, 16]), ALU.mult,
                       ["gsm", "gst"], ["gsm"])
                    for s in range(128):
                        ub = Ub[s % NU]
                        ubn = "Ub%d" % (s % NU)
                        k.dma("pool", lambda E: E.indirect_dma_start(
                            out=ub[:], out_offset=None, in_=peer_u,
                            in_offset=bass.IndirectOffsetOnAxis(ap=eu[:, s:s + 1], axis=0)), ["eu"], [ubn])
                        k.op("dve", lambda E: E.scalar_tensor_tensor(
                            out=junk[:], in0=ub[:], scalar=1.0, in1=ht[:], op0=ALU.mult, op1=ALU.mult,
                            accum_out=actv[:, s:s + 1]), [ubn, htn], ["junk", "actv"])
                    ACT(actv[:], actv[:], AF.Gelu, ["actv"], ["actv"])
                    TT("dve", wgt[:], actv[:], gsm[:].rearrange("p h k -> p (h k)"), ALU.mult, ["actv", "gsm"], ["wgt"])
                    for s in range(128):
                        vg = Vg[s % NV]
                        vgn = "Vg%d" % (s % NV)
                        dgi = dg[s % 4]
                        dgn = "dg%d" % (s % 4)
                        k.dma("pool", lambda E: E.indirect_dma_start(
                            out=vg[:], out_offset=None, in_=peer_v,
                            in_offset=bass.IndirectOffsetOnAxis(ap=eu[:, s:s + 1], axis=0)), ["eu"], [vgn])
                        TS("dve", dgi[:], identf[:], wgt[:, s:s + 1], ALU.mult, ["identf", "wgt"], [dgn])
                        vh = Vh[s % 2]
                        vhn = "Vh%d" % (s % 2)
                        CP("act", vh[:], vg[:], [vgn], [vhn])
                        for fc in range(2):
                            MM(pO[:, fc * 512:(fc + 1) * 512], dgi[:], vh[:, fc * 512:(fc + 1) * 512], s == 0, s == 127,
                               [dgn, vhn], ["pO"])
                    TT("dve", X[:, t, :], X[:, t, :], pO[:], ALU.add, ["X%d" % t, "pO"], ["X%d" % t])
                k.barrier()

        def conv_mixer():
            GW = 256
            with ExitStack() as ph:
                psn = PS(ph, "psn", [128, 1024], BF16)
                pC = [PS(ph, "pC%d" % i, [128, 512], F32) for i in range(3)]
                pX = PS(ph, "pX", [128, 1024], F32)
                win = T(ph, "win", [128, 8, 3072], BF16)
                wout = T(ph, "wout", [128, 8, 1024], BF16)
                cw24 = T(ph, "cw24", [24, 128], F32)
                cw = T(ph, "cw", [128, 24], F32)
                hTg = T(ph, "hTg", [128, 8, GW], BF16)
                uT = T(ph, "uT", [128, 8, GW], BF16)
                vp = [T(ph, "vp%d" % i, [128, GW + 2], F32) for i in range(2)]
                hvs = T(ph, "hvs", [128, GW], F32)
                acc = T(ph, "acc", [128, GW], F32)
                cbuf = T(ph, "cbuf", [128, 8, 2], F32)
                scb = T(ph, "scb", [128, 8, 32], F32)
                cso = T(ph, "cso", [128, 8, 32], F32)
                vps = T(ph, "vps", [128, 16, 6], F32)
                accs = T(ph, "accs", [128, 16, 4], F32)
                ostg = T(ph, "ostg", [32, 1024], F32)
                load_gain(norm_mix[1:2, :])
                wiv = conv_w_in.rearrange("(dt p) f -> p dt f", p=128)
                wov = conv_w_out.rearrange("(dt p) f -> p dt f", p=128)
                for dt in range(8):
                    LDC(win[:, dt, :], wiv[:, dt, :], ["win"])
                    LDC(wout[:, dt, :], wov[:, dt, :], ["wout"])
                LD(cw24[:], conv_w, ["cw24"])
                TR(pX[:, 0:24], cw24[:], identf[0:24, 0:24], ["cw24", "identf"], ["pX"])
                CP("act", cw[:], pX[:, 0:24], ["pX"], ["cw"])
                LD(ostg[:], sconv, ["ostg"])
                for ft in range(8):
                    TR(pX[:, 32 * ft:32 * ft + 32], ostg[:, ft * 128:(ft + 1) * 128], identf[0:32, 0:32],
                       ["ostg", "identf"], ["pX"])
                CP("act", scb[:].rearrange("p a b -> p (a b)"), pX[:, 0:256], ["pX"], ["scb"])
                MS("dve", cbuf[:], 0.0, ["cbuf"])

                def wout_tile(t, c0):
                    for fc in range(2):
                        for ft in range(8):
                            MM(pX[:, fc * 512:(fc + 1) * 512], uT[:, ft, c0:c0 + 128], wout[:, ft, fc * 512:(fc + 1) * 512],
                               ft == 0, ft == 7, ["uT", "wout"], ["pX"])
                    TT("dve", X[:, t, :], X[:, t, :], pX[:], ALU.add, ["X%d" % t, "pX"], ["X%d" % t])

                NG = 2048 // GW
                TPG = GW // 128
                for g in range(NG + 1):
                    sample = (g == NG)
                    ncol = 128 if sample else GW
                    if sample:
                        norm_tile(16, psn, hTg, ["hTg"], 0)
                    else:
                        for j in range(TPG):
                            norm_tile(TPG * g + j, psn, hTg, ["hTg"], j * 128)
                    for ft in range(8):
                        for j3, off in enumerate((0, 1024, 2048)):
                            for dt in range(8):
                                MM(pC[j3][:, 0:ncol], win[:, dt, off + ft * 128:off + (ft + 1) * 128], hTg[:, dt, 0:ncol],
                                   dt == 0, dt == 7, ["win", "hTg"], ["pC%d" % j3])
                        w0 = cw[:, 0 * 8 + ft:0 * 8 + ft + 1]
                        w1 = cw[:, 1 * 8 + ft:1 * 8 + ft + 1]
                        w2 = cw[:, 2 * 8 + ft:2 * 8 + ft + 1]
                        if not sample:
                            vpp = vp[ft % 2]
                            vpn = "vp%d" % (ft % 2)
                            CP("act", hvs[:], pC[2][:, 0:GW], ["pC2"], ["hvs"])
                            CP("dve", vpp[:, 0:2], cbuf[:, ft, :], ["cbuf"], [vpn])
                            TT("dve", vpp[:, 2:GW + 2], pC[1][:, 0:GW], hvs[:], ALU.mult, ["pC1", "hvs"], [vpn])
                            CP("dve", cbuf[:, ft, :], vpp[:, GW:GW + 2], [vpn], ["cbuf"])
                            TS("dve", acc[:], vpp[:, 0:GW], w0, ALU.mult, [vpn, "cw"], ["acc"])
                            STT("dve", acc[:], vpp[:, 1:GW + 1], w1, acc[:], ALU.mult, ALU.add, [vpn, "cw", "acc"], ["acc"])
                            STT("dve", acc[:], vpp[:, 2:GW + 2], w2, acc[:], ALU.mult, ALU.add, [vpn, "cw", "acc"], ["acc"])
                            TT("dve", uT[:, ft, :], pC[0][:, 0:GW], acc[:], ALU.mult, ["pC0", "acc"], ["uT"])
                        else:
                            CP("act", hvs[:, 0:64], pC[2][:, 0:64], ["pC2"], ["hvs"])
                            CP("dve", vps[:, :, 0:2], scb[:, ft, :].rearrange("p (s r) -> p s r", r=2), ["scb"], ["vps"])
                            TT("dve", vps[:, :, 2:6], pC[1][:, 0:64].rearrange("p (s t) -> p s t", t=4),
                               hvs[:, 0:64].rearrange("p (s t) -> p s t", t=4), ALU.mult, ["pC1", "hvs"], ["vps"])
                            CP("dve", cso[:, ft, :].rearrange("p (s r) -> p s r", r=2), vps[:, :, 4:6], ["vps"], ["cso"])
                            TS("dve", accs[:], vps[:, :, 0:4], w0, ALU.mult, ["vps", "cw"], ["accs"])
                            STT("dve", accs[:], vps[:, :, 1:5], w1, accs[:], ALU.mult, ALU.add, ["vps", "cw", "accs"], ["accs"])
                            STT("dve", accs[:], vps[:, :, 2:6], w2, accs[:], ALU.mult, ALU.add, ["vps", "cw", "accs"], ["accs"])
                            MS("dve", uT[:, ft, 0:128], 0.0, ["uT"])
                            TT("dve", uT[:, ft, 0:64].rearrange("p (s t) -> p s t", t=4),
                               pC[0][:, 0:64].rearrange("p (s t) -> p s t", t=4), accs[:], ALU.mult, ["pC0", "accs"], ["uT"])
                    if not sample:
                        for j in range(TPG):
                            wout_tile(TPG * g + j, j * 128)
                    else:
                        wout_tile(16, 0)
                    if g == NG - 1:
                        for ft in range(8):
                            TR(pX[0:2, ft * 128:(ft + 1) * 128], cbuf[:, ft, :], identf[:], ["cbuf", "identf"], ["pX"])
                        CP("act", ostg[0:2, :], pX[0:2, :], ["pX"], ["ostg"])
                        LD(o_conv_p, ostg[0:2, :], [], ["ostg"])
                for ft in range(8):
                    TR(pX[0:32, ft * 128:(ft + 1) * 128], cso[:, ft, :], identf[:], ["cso", "identf"], ["pX"])
                CP("act", ostg[:], pX[0:32, :], ["pX"], ["ostg"])
                LD(o_conv_s, ostg[:], [], ["ostg"])
                k.barrier()

        if stop >= 1.2:
            attention(0)
            k.halt = False
        if stop >= 3:
            peer(0)
        if stop >= 4:
            conv_mixer()
        if stop >= 5:
            attention(1)
        if stop >= 6:
            peer(1)

        with ExitStack() as ph:
            yo = [T(ph, "yo%d" % i, [128, 1024], F32) for i in range(2)]
            load_gain(norm_final[0:1, :])
            for t in range(NT):
                i = t % 2
                ACT(junk[:], X[:, t, :], AF.Square, ["X%d" % t], ["junk", "nss%d" % i], accum_out=nsm[:, 2 * i:2 * i + 1])
                ACT(nsm[:, 2 * i + 1:2 * i + 2], nsm[:, 2 * i:2 * i + 1], AF.Sqrt, ["nss%d" % i], ["nrs%d" % i],
                    scale=1.0 / 1024.0, bias=epsT[:, 0:1])
                k.op("dve", lambda E: E.reciprocal(out=nsm[:, 4 + i:5 + i], in_=nsm[:, 2 * i + 1:2 * i + 2]),
                     ["nrs%d" % i], ["nri%d" % i])
                STT("dve", yo[i][:], X[:, t, :], nsm[:, 4 + i:5 + i], gbc[:], ALU.mult, ALU.mult,
                    ["X%d" % t, "nri%d" % i, "gbc"], ["yo%d" % i])
                if t < 16:
                    LD(o_yp[t * 128:(t + 1) * 128, :], yo[i][:], [], ["yo%d" % i])
                else:
                    LD(o_ys, yo[i][0:64, :], [], ["yo%d" % i])
            k.barrier()
        print("ninstr", k.ninstr)
    return nc


_NC = None
_STOP = 99


def kernel(**inp):
    global _NC
    if _NC is None:
        _NC = build(_STOP)
    nc = _NC
    f = lambda a: np.ascontiguousarray(np.asarray(a, dtype=np.float32))
    common = {
        "norm_mix": f(inp["norm_mix"]), "norm_mem": f(inp["norm_mem"]), "norm_ffn": f(inp["norm_ffn"]),
        "norm_final": f(inp["norm_final"]).reshape(1, 1024),
        "ssm_a_re": f(inp["ssm_a_re"])[0], "ssm_a_im": f(inp["ssm_a_im"])[0], "ssm_log_dt": f(inp["ssm_log_dt"])[0],
        "ssm_b_re": f(inp["ssm_b_re"])[0], "ssm_b_im": f(inp["ssm_b_im"])[0],
        "ssm_c_re": f(inp["ssm_c_re"])[0], "ssm_c_im": f(inp["ssm_c_im"])[0],
        "ssm_d": f(inp["ssm_d"]).reshape(8, 128), "ssm_w_glu": f(inp["ssm_w_glu"])[0],
        "conv_w_in": f(inp["conv_w_in"])[0], "conv_w": f(inp["conv_w"]).reshape(24, 128),
        "conv_w_out": f(inp["conv_w_out"])[0],
        "mem_w_q": f(inp["mem_w_q"]), "mem_w_k": f(inp["mem_w_k"]), "mem_w_v": f(inp["mem_w_v"]),
        "mem_w_o": f(inp["mem_w_o"]), "peer_w_query": f(inp["peer_w_query"]),
        "peer_key1": f(inp["peer_key1"]), "peer_key2": f(inp["peer_key2"]),
        "peer_u": f(inp["peer_u"]).reshape(32768, 1024)[:(32768 if _STOP >= 3 else 128)],
        "peer_v": f(inp["peer_v"]).reshape(32768, 1024)[:(32768 if _STOP >= 3 else 128)],
    }
    x_prompt = f(inp["x_prompt"]); x_sample = f(inp["x_sample"]); mem_prompt = f(inp["mem_prompt"])
    sre = f(inp["state_ssm_re"]); sim = f(inp["state_ssm_im"]); sconv = f(inp["state_conv"])
    ckk = f(inp["cache_mem_k"]); cvv = f(inp["cache_mem_v"])
    in_maps = []
    for c in range(NCORES):
        s = slice(16 * c, 16 * c + 16)
        m = dict(common)
        m["xp"] = x_prompt[c]
        m["xs"] = np.ascontiguousarray(x_sample[s].reshape(64, 1024))
        m["memp"] = mem_prompt[c]
        m["s0re"] = np.ascontiguousarray(sre[0, s].reshape(16, 4096))
        m["s0im"] = np.ascontiguousarray(sim[0, s].reshape(16, 4096))
        m["sconv"] = np.ascontiguousarray(sconv[0, s].reshape(32, 1024))
        m["ck"] = np.ascontiguousarray(ckk[:, s].reshape(2, 16, 256, 1024))
        m["cv"] = np.ascontiguousarray(cvv[:, s].reshape(2, 16, 256, 1024))
        in_maps.append(m)
    res = run_bass_kernel_spmd(nc, in_maps, core_ids=list(range(NCORES)))
    R = res.results
    y_prompt = np.stack([R[c]["o_yp"] for c in range(NCORES)]).reshape(8, 2048, 1024)
    y_sample = np.concatenate([R[c]["o_ys"].reshape(16, 4, 1024) for c in range(NCORES)], axis=0)
    ssm_re_p = np.stack([R[c]["o_sre_p"].reshape(64, 64) for c in range(NCORES)])[None]
    ssm_im_p = np.stack([R[c]["o_sim_p"].reshape(64, 64) for c in range(NCORES)])[None]
    conv_p = np.stack([R[c]["o_conv_p"] for c in range(NCORES)])[None]
    mk = np.stack([R[c]["o_mk"].reshape(2, 256, 1024) for c in range(NCORES)], axis=1).reshape(2, 8, 256, 4, 256)
    mv = np.stack([R[c]["o_mv"].reshape(2, 256, 1024) for c in range(NCORES)], axis=1).reshape(2, 8, 256, 4, 256)
    ssm_re_s = np.concatenate([R[c]["o_sre_s"].reshape(16, 64, 64) for c in range(NCORES)], axis=0)[None]
    ssm_im_s = np.concatenate([R[c]["o_sim_s"].reshape(16, 64, 64) for c in range(NCORES)], axis=0)[None]
    conv_s = np.concatenate([R[c]["o_conv_s"].reshape(16, 2, 1024) for c in range(NCORES)], axis=0)[None]
    out = (y_prompt, y_sample, ssm_re_p, ssm_im_p, conv_p, mk, mv, ssm_re_s, ssm_im_s, conv_s)
    return tuple(np.ascontiguousarray(o, dtype=np.float32) for o in out)
```

```python
from contextlib import ExitStack
import numpy as np
import concourse.bass as bass
import concourse.mybir as mybir
from concourse.bass_utils import run_bass_kernel_spmd

F32 = mybir.dt.float32
BF16 = mybir.dt.bfloat16
U32 = mybir.dt.uint32
ALU = mybir.AluOpType
AF = mybir.ActivationFunctionType
AX = mybir.AxisListType

NCORES = 8
import os as _os2
NOOUT = _os2.environ.get("NOOUT")
NT = 17
TC = NT * 128
MAGIC = 12582912.0
SC2PI = 6.283185
INV2PI = 0.15915494309189535
EPS = 1e-6


class Buf:
    __slots__ = ("w", "r")

    def __init__(self):
        self.w = None
        self.r = []


class K:
    def __init__(self, nc, stack, n_dma_sems=32):
        self.nc = nc
        self.eng = {"pe": nc.tensor, "dve": nc.vector, "act": nc.scalar,
                    "pool": nc.gpsimd, "sp": nc.sync}
        self.sem = {}
        self.cnt = {}
        for e in self.eng:
            self.sem[e] = stack.enter_context(nc.semaphore("s_" + e))
            self.cnt[e] = 0
        self.seen = {e: {} for e in self.eng}
        self.dsem = []
        for i in range(n_dma_sems):
            key = "d%d" % i
            self.sem[key] = stack.enter_context(nc.semaphore("s_" + key))
            self.cnt[key] = 0
            self.dsem.append(key)
        self.dnext = 0
        self.bufs = {}
        self.ninstr = 0
        self.halt = False

    def buf(self, name):
        b = self.bufs.get(name)
        if b is None:
            b = Buf()
            self.bufs[name] = b
        return b

    def _deps(self, reads, writes):
        deps = {}

        def add(ev):
            if ev is not None and deps.get(ev[0], 0) < ev[1]:
                deps[ev[0]] = ev[1]
        for b in reads:
            add(self.buf(b).w)
        for b in writes:
            bb = self.buf(b)
            add(bb.w)
            for ev in bb.r:
                add(ev)
        return deps

    def _wait(self, e, deps, skip_self=False):
        eng = self.eng[e]
        for kk, v in deps.items():
            if kk == e and skip_self:
                continue
            if self.seen[e].get(kk, 0) >= v:
                continue
            eng.wait_ge(self.sem[kk], v)
            self.seen[e][kk] = v

    def _mark(self, ev, reads, writes):
        for b in reads:
            self.buf(b).r.append(ev)
        for b in writes:
            bb = self.buf(b)
            bb.w = ev
            bb.r = []

    def op(self, e, fn, reads=(), writes=()):
        if self.halt:
            return None
        deps = self._deps(reads, writes)
        self._wait(e, deps, skip_self=(e == "pe"))
        ins = fn(self.eng[e])
        self.cnt[e] += 1
        ins.then_inc(self.sem[e], 1)
        ev = (e, self.cnt[e])
        self._mark(ev, reads, writes)
        self.ninstr += 1
        return ev

    def dma(self, q, fn, reads=(), writes=()):
        if self.halt:
            return None
        deps = self._deps(reads, writes)
        self._wait(q, deps)
        dkey = self.dsem[self.dnext % len(self.dsem)]
        self.dnext += 1
        if self.cnt[dkey] > 0:
            self._wait(q, {dkey: self.cnt[dkey]})
        ins = fn(self.eng[q])
        self.cnt[dkey] += 16
        ins.then_inc(self.sem[dkey], 16)
        ev = (dkey, self.cnt[dkey])
        self._mark(ev, reads, writes)
        self.ninstr += 1
        return ev

    def barrier(self):
        if self.halt:
            return
        deps = {kk: v for kk, v in self.cnt.items() if v > 0}
        for e in self.eng:
            self._wait(e, deps)
        self.bufs = {}


class _Stop(Exception):
    pass


def build(stop=99):
    nc = bass.Bass("TRN2", target_bir_lowering=False)

    def din(name, shape, dtype=F32):
        return nc.dram_tensor(name, shape, dtype, kind="ExternalInput").ap()

    def dout(name, shape):
        return nc.dram_tensor(name, shape, F32, kind="ExternalOutput").ap()

    xp = din("xp", [2048, 1024])
    xs = din("xs", [64, 1024])
    memp = din("memp", [256, 1024])
    s0re = din("s0re", [16, 4096])
    s0im = din("s0im", [16, 4096])
    sconv = din("sconv", [32, 1024])
    ck = din("ck", [2, 16, 256, 1024])
    cv = din("cv", [2, 16, 256, 1024])
    norm_mix = din("norm_mix", [2, 1024])
    norm_mem = din("norm_mem", [2, 1024])
    norm_ffn = din("norm_ffn", [2, 1024])
    norm_final = din("norm_final", [1, 1024])
    a_re = din("ssm_a_re", [64, 64])
    a_im = din("ssm_a_im", [64, 64])
    log_dt = din("ssm_log_dt", [64])
    b_re = din("ssm_b_re", [64, 64, 16])
    b_im = din("ssm_b_im", [64, 64, 16])
    c_re = din("ssm_c_re", [64, 16, 64])
    c_im = din("ssm_c_im", [64, 16, 64])
    ssm_d = din("ssm_d", [8, 128])
    w_glu = din("ssm_w_glu", [1024, 2048])
    conv_w_in = din("conv_w_in", [1024, 3072])
    conv_w = din("conv_w", [24, 128])
    conv_w_out = din("conv_w_out", [1024, 1024])
    mem_w_q = din("mem_w_q", [2, 1024, 1024])
    mem_w_k = din("mem_w_k", [2, 1024, 1024])
    mem_w_v = din("mem_w_v", [2, 1024, 1024])
    mem_w_o = din("mem_w_o", [2, 1024, 1024])
    peer_wq = din("peer_w_query", [2, 1024, 2048])
    peer_k1 = din("peer_key1", [2, 128, 128])
    peer_k2 = din("peer_key2", [2, 128, 128])
    peer_u = din("peer_u", [32768 if stop >= 3 else 128, 1024])
    peer_v = din("peer_v", [32768 if stop >= 3 else 128, 1024])

    o_yp = dout("o_yp", [2048, 1024])
    o_ys = dout("o_ys", [64, 1024])
    o_sre_p = dout("o_sre_p", [32, 128])
    o_sim_p = dout("o_sim_p", [32, 128])
    o_conv_p = dout("o_conv_p", [2, 1024])
    o_mk = dout("o_mk", [512, 1024])
    o_mv = dout("o_mv", [512, 1024])
    o_sre_s = dout("o_sre_s", [16, 4096])
    o_sim_s = dout("o_sim_s", [16, 4096])
    o_conv_s = dout("o_conv_s", [32, 1024])

    with ExitStack() as st:
        k = K(nc, st)

        uid = [0]

        def T(stack, name, shape, dtype):
            uid[0] += 1
            return stack.enter_context(nc.sbuf_tensor("%s_%d" % (name, uid[0]), shape, dtype))

        def PS(stack, name, shape, dtype=F32):
            uid[0] += 1
            return stack.enter_context(nc.psum_tensor("%s_%d" % (name, uid[0]), shape, dtype))

        def TT(e, out, in0, in1, op, r, w):
            return k.op(e, lambda E: E.tensor_tensor(out=out, in0=in0, in1=in1, op=op), r, w)

        def TS(e, out, in0, s1, op0, r, w, s2=None, op1=None):
            if op1 is None:
                return k.op(e, lambda E: E.tensor_scalar(out=out, in0=in0, scalar1=s1, scalar2=None, op0=op0), r, w)
            return k.op(e, lambda E: E.tensor_scalar(out=out, in0=in0, scalar1=s1, scalar2=s2, op0=op0, op1=op1), r, w)

        def STT(e, out, in0, scalar, in1, op0, op1, r, w):
            return k.op(e, lambda E: E.scalar_tensor_tensor(out=out, in0=in0, scalar=scalar, in1=in1, op0=op0, op1=op1), r, w)

        def ACT(out, in_, func, r, w, **kw):
            return k.op("act", lambda E: E.activation(out=out, in_=in_, func=func, **kw), r, w)

        def CP(e, out, in_, r, w):
            if e == "act":
                return k.op("act", lambda E: E.activation(out=out, in_=in_, func=AF.Copy), r, w)
            return k.op(e, lambda E: E.tensor_copy(out=out, in_=in_), r, w)

        def MM(out, lhsT, rhs, start, stop, r, w):
            return k.op("pe", lambda E: E.matmul(out, lhsT=lhsT, rhs=rhs, start=start, stop=stop), r, w)

        def TR(out, in_, ident, r, w):
            return k.op("pe", lambda E: E.transpose(out=out, in_=in_, identity=ident), r, w)

        def MS(e, ap, val, w):
            return k.op(e, lambda E: E.memset(ap, val), (), w)

        def LD(out, in_, w, r=(), q="sp"):
            return k.dma(q, lambda E: E.dma_start(out=out, in_=in_), r, w)

        ldc_n = [0]

        def LDC(out, in_, w, r=()):
            F = out.shape[-1]
            for c0 in range(0, F, 1024):
                c1 = min(F, c0 + 1024)
                i = ldc_n[0] % 2
                ldc_n[0] += 1
                LD(wstg[i][:, 0:c1 - c0], in_[:, c0:c1], ["wstg%d" % i])
                CP("pool" if i == 0 else "act", out[:, c0:c1], wstg[i][:, 0:c1 - c0], ["wstg%d" % i], w)

        def ckp(v):
            if stop <= v and not k.halt:
                k.barrier()
                k.halt = True

        X = T(st, "X", [128, NT, 1024], F32)
        identf = T(st, "identf", [128, 128], F32)
        identb = T(st, "identb", [128, 128], BF16)
        gbc = T(st, "gbc", [128, 1024], F32)
        junk = T(st, "junk", [128, 1024], BF16)
        htok = [T(st, "htok%d" % i, [128, 1024], BF16) for i in range(2)]
        nsm = T(st, "nsm", [128, 8], F32)
        epsT = T(st, "epsT", [128, 1], F32)
        memT = T(st, "memT", [128, 8, 256], BF16)
        wstg = [T(st, "wstg%d" % i, [128, 1024], F32) for i in range(2)]

        MS("pool", identf[:], 1.0, ["identf"])
        k.op("pool", lambda E: E.affine_select(out=identf[:], in_=identf[:], pattern=[[-1, 128]],
                                               compare_op=ALU.is_equal, fill=0.0, base=0,
                                               channel_multiplier=1), ["identf"], ["identf"])
        CP("dve", identb[:], identf[:], ["identf"], ["identb"])
        MS("dve", epsT[:], EPS, ["epsT"])
        MS("pool", X[:, 16, :], 0.0, ["X16"])
        xpv = xp.rearrange("(t p) d -> p t d", p=128)
        for g4 in range(4):
            LD(X[:, 4 * g4:4 * g4 + 4, :], xpv[:, 4 * g4:4 * g4 + 4, :], ["X%d" % t for t in range(4 * g4, 4 * g4 + 4)])
        LD(X[0:64, 16, :], xs, ["X16"])

        def load_gain(vec_ap):
            LD(gbc[:], vec_ap.to_broadcast([128, 1024]), ["gbc"])

        def norm_tile(t, psn, hT_out, hT_names, col0):
            i = t % 2
            ht = htok[i]
            ACT(junk[:], X[:, t, :], AF.Square, ["X%d" % t], ["junk", "nss%d" % i], accum_out=nsm[:, 2 * i:2 * i + 1])
            ACT(nsm[:, 2 * i + 1:2 * i + 2], nsm[:, 2 * i:2 * i + 1], AF.Sqrt, ["nss%d" % i], ["nrs%d" % i],
                scale=1.0 / 1024.0, bias=epsT[:, 0:1])
            k.op("dve", lambda E: E.reciprocal(out=nsm[:, 4 + i:5 + i], in_=nsm[:, 2 * i + 1:2 * i + 2]),
                 ["nrs%d" % i], ["nri%d" % i])
            STT("dve", ht[:], X[:, t, :], nsm[:, 4 + i:5 + i], gbc[:], ALU.mult, ALU.mult,
                ["X%d" % t, "nri%d" % i, "gbc"], ["htok%d" % i])
            if hT_out is None:
                return
            for dt in range(8):
                TR(psn[:, dt * 128:(dt + 1) * 128], ht[:, dt * 128:(dt + 1) * 128], identb[:],
                   ["htok%d" % i, "identb"], ["psn"])
            CP("act", hT_out[:, :, col0:col0 + 128], psn[:].rearrange("p (a b) -> p a b", a=8),
               ["psn"], hT_names)

        try:
          with ExitStack() as ph:
            hT = T(ph, "hT", [128, 8, TC], BF16)
            hTn = ["hT%d" % d for d in range(8)]
            load_gain(norm_mix[0:1, :])
            with ExitStack() as phn:
                psn = PS(phn, "psn", [128, 1024], BF16)
                for t in range(NT):
                    norm_tile(t, psn, hT, hTn, t * 128)
                k.barrier()
            ckp(0.1)

            with ExitStack() as ph2:
                are = T(ph2, "are", [32, 128], F32)
                aim = T(ph2, "aim", [32, 128], F32)
                ldt = T(ph2, "ldt", [32, 2], F32)
                dtb = T(ph2, "dtb", [32, 128], F32)
                qa = [T(ph2, "qa%d" % i, [32, 128], F32) for i in range(10)]
                sp6 = T(ph2, "sp6", [128, 6, 32], F32)
                dsk = T(ph2, "dsk", [128, 8], F32)
                dsk8 = T(ph2, "dsk8", [8, 128], F32)
                pA = PS(ph2, "pA", [128, 1024], F32)
                pB = PS(ph2, "pB", [128, 1024], F32)
                pY = PS(ph2, "pY", [128, 1024], F32)
                LD(are[:], a_re.rearrange("(q two) p -> q (two p)", two=2), ["are"])
                LD(aim[:], a_im.rearrange("(q two) p -> q (two p)", two=2), ["aim"])
                LD(ldt[:], log_dt.rearrange("(q two) -> q two", two=2), ["ldt"])
                LD(dsk8[:], ssm_d, ["dsk8"])
                ACT(ldt[:], ldt[:], AF.Exp, ["ldt"], ["ldt"])
                CP("dve", dtb[:].rearrange("q (two p) -> q two p", two=2),
                   ldt[:, :].unsqueeze(2).to_broadcast([32, 2, 64]), ["ldt"], ["dtb"])
                mag, ang, t1, t2, sina, cosa, lbre, lbim, fre, fim = [x[:] for x in qa]
                N = ["qa%d" % i for i in range(10)]
                TT("dve", t1, are[:], dtb[:], ALU.mult, ["are", "dtb"], [N[2]])
                ACT(mag, t1, AF.Exp, [N[2]], [N[0]])
                TT("dve", ang, aim[:], dtb[:], ALU.mult, ["aim", "dtb"], [N[1]])
                TS("dve", ang, ang, INV2PI, ALU.mult, [N[1]], [N[1]])
                TS("dve", t1, ang, MAGIC, ALU.add, [N[1]], [N[2]], s2=MAGIC, op1=ALU.subtract)
                TT("dve", t1, ang, t1, ALU.subtract, [N[1], N[2]], [N[2]])
                ACT(sina, t1, AF.Sin, [N[2]], [N[4]], scale=SC2PI)
                TS("dve", t2, ang, 0.25, ALU.add, [N[1]], [N[3]])
                TS("dve", t1, t2, MAGIC, ALU.add, [N[3]], [N[2]], s2=MAGIC, op1=ALU.subtract)
                TT("dve", t1, t2, t1, ALU.subtract, [N[3], N[2]], [N[2]])
                ACT(cosa, t1, AF.Sin, [N[2]], [N[5]], scale=SC2PI)
                TT("dve", lbre, mag, cosa, ALU.mult, [N[0], N[5]], [N[6]])
                TT("dve", lbim, mag, sina, ALU.mult, [N[0], N[4]], [N[7]])
                TT("dve", t1, are[:], are[:], ALU.mult, ["are"], [N[2]])
                TT("dve", t2, aim[:], aim[:], ALU.mult, ["aim"], [N[3]])
                TT("dve", t1, t1, t2, ALU.add, [N[2], N[3]], [N[2]])
                k.op("dve", lambda E: E.reciprocal(out=sina, in_=t1), [N[2]], [N[4]])
                TS("dve", cosa, lbre, -1.0, ALU.add, [N[6]], [N[5]])
                TT("dve", t1, cosa, are[:], ALU.mult, [N[5], "are"], [N[2]])
                TT("dve", t2, lbim, aim[:], ALU.mult, [N[7], "aim"], [N[3]])
                TT("dve", t1, t1, t2, ALU.add, [N[2], N[3]], [N[2]])
                TT("dve", fre, t1, sina, ALU.mult, [N[2], N[4]], [N[8]])
                TT("dve", t1, lbim, are[:], ALU.mult, [N[7], "are"], [N[2]])
                TT("dve", t2, cosa, aim[:], ALU.mult, [N[5], "aim"], [N[3]])
                TT("dve", t1, t1, t2, ALU.subtract, [N[2], N[3]], [N[2]])
                TT("dve", fim, t1, sina, ALU.mult, [N[2], N[4]], [N[9]])
                for j, (src, nm) in enumerate([(fre, N[8]), (fim, N[9]), (mag, N[0]), (ang, N[1]), (lbre, N[6]), (lbim, N[7])]):
                    TR(pA[:, j * 32:(j + 1) * 32], src, identf[0:32, 0:32], [nm, "identf"], ["pA"])
                CP("act", sp6[:].rearrange("p a b -> p (a b)"), pA[:, 0:192], ["pA"], ["sp6"])
                TR(pA[:, 256:264], dsk8[:], identf[0:8, 0:8], ["dsk8", "identf"], ["pA"])
                CP("act", dsk[:], pA[:, 256:264], ["pA"], ["dsk"])
                F_RE, F_IM, MAG, FQ, LBRE, LBIM = range(6)
                ckp(0.2)

                LT_re = T(ph2, "LT_re", [128, 32, 128], BF16)
                LT_im = T(ph2, "LT_im", [128, 32, 128], BF16)
                Cre = T(ph2, "Cre", [128, 32, 64], BF16)
                Cimn = T(ph2, "Cimn", [128, 32, 64], BF16)
                MS("pool", Cre[:], 0.0, ["Cre"])
                MS("pool", Cimn[:], 0.0, ["Cimn"])
                s0T_re = T(ph2, "s0T_re", [128, 32, 16], F32)
                s0T_im = T(ph2, "s0T_im", [128, 32, 16], F32)
                with ExitStack() as ph3:
                    bre = T(ph3, "bre", [128, 32, 16], F32)
                    bim = T(ph3, "bim", [128, 32, 16], F32)
                    bt1 = T(ph3, "bt1", [128, 32, 16], F32)
                    bt2 = T(ph3, "bt2", [128, 32, 16], F32)
                    bbr = T(ph3, "bbr", [128, 32, 16], F32)
                    bbi = T(ph3, "bbi", [128, 32, 16], F32)
                    Pre = T(ph3, "Pre", [128, 32, 128], BF16)
                    Pim = Pre
                    psb = PS(ph3, "psb", [128, 1024], BF16)
                    LD(bre[:], b_re.rearrange("(q two) p c -> (two p) q c", two=2), ["bre"])
                    LD(bim[:], b_im.rearrange("(q two) p c -> (two p) q c", two=2), ["bim"])
                    frb = sp6[:, F_RE, :].unsqueeze(2).to_broadcast([128, 32, 16])
                    fib = sp6[:, F_IM, :].unsqueeze(2).to_broadcast([128, 32, 16])
                    TT("dve", bt1[:], bre[:], frb, ALU.mult, ["bre", "sp6"], ["bt1"])
                    TT("dve", bt2[:], bim[:], fib, ALU.mult, ["bim", "sp6"], ["bt2"])
                    TT("dve", bbr[:], bt1[:], bt2[:], ALU.subtract, ["bt1", "bt2"], ["bbr"])
                    TT("dve", bt1[:], bre[:], fib, ALU.mult, ["bre", "sp6"], ["bt1"])
                    TT("dve", bt2[:], bim[:], frb, ALU.mult, ["bim", "sp6"], ["bt2"])
                    TT("dve", bbi[:], bt1[:], bt2[:], ALU.add, ["bt1", "bt2"], ["bbi"])
                    MS("pool", Pre[:], 0.0, ["Pre"])
                    for (Pd, bb, nmP, nmb, LT, nmL) in ((Pre, bbr, "Pre", "bbr", LT_re, "LT_re"), (Pre, bbi, "Pre", "bbi", LT_im, "LT_im")):
                        Pv = Pd[:].rearrange("p (qa qb) n -> p qa qb n", qb=4)
                        bv = bb[:].rearrange("p (qa qb) c -> p qa qb c", qb=4)
                        for qb in range(4):
                            CP("dve", Pv[0:64, :, qb, 32 * qb:32 * qb + 16], bv[0:64, :, qb, :], [nmb], [nmP])
                            CP("dve", Pv[64:128, :, qb, 32 * qb + 16:32 * qb + 32], bv[64:128, :, qb, :], [nmb], [nmP])
                        for b8 in range(4):
                            for j in range(8):
                                q = b8 * 8 + j
                                TR(psb[:, j * 128:(j + 1) * 128], Pd[:, q, :], identb[:], [nmP, "identb"], ["psb"])
                            CP("act", LT[:, b8 * 8:(b8 + 1) * 8, :], psb[:].rearrange("p (a b) -> p a b", a=8), ["psb"], [nmL])
                    k.barrier()
                ckp(0.3)
                with ExitStack() as ph3:
                    Sre = T(ph3, "Sre", [32, 32, 128], F32)
                    MS("pool", Sre[:], 0.0, ["Sre"])
                    for (csrc, Cd, nmC, sc) in ((c_re, Cre, "Cre", 1.0), (c_im, Cimn, "Cimn", -1.0)):
                        cvw = csrc.rearrange("(q two) c p -> two c q p", two=2)
                        LD(Sre[0:16, :, 0:64], cvw[0], ["Sre"])
                        LD(Sre[16:32, :, 64:128], cvw[1], ["Sre"])
                        for q in range(32):
                            TR(pA[:, q * 32:(q + 1) * 32], Sre[:, q, :], identf[0:32, 0:32], ["Sre", "identf"], ["pA"])
                        ACT(Cd[:, :, 32:64], pA[:].rearrange("p (a b) -> p a b", a=32), AF.Copy, ["pA"], [nmC], scale=sc)
                    k.barrier()
                ckp(0.35)
                with ExitStack() as ph3:
                    s0sb = T(ph3, "s0sb", [16, 1024], F32)
                    for (src, dst, nmd) in ((s0re, s0T_re, "s0T_re"), (s0im, s0T_im, "s0T_im")):
                        for b8 in range(4):
                            LD(s0sb[:], src[:, b8 * 1024:(b8 + 1) * 1024], ["s0sb"])
                            for j in range(8):
                                TR(pA[:, j * 16:(j + 1) * 16], s0sb[:, j * 128:(j + 1) * 128], identf[0:16, 0:16],
                                   ["s0sb", "identf"], ["pA"])
                            CP("act", dst[:, b8 * 8:(b8 + 1) * 8, :].rearrange("p a b -> p (a b)"), pA[:, 0:128], ["pA"], [nmd])
                    k.barrier()

                ckp(0.4)
                CH = 256
                NCH = 2048 // CH
                iot = T(ph2, "iot", [128, CH], F32)
                csb = [T(ph2, "cs%d" % i, [128, CH], F32) for i in range(2)]
                snb = [T(ph2, "sn%d" % i, [128, CH], F32) for i in range(2)]
                x2 = T(ph2, "x2", [128, CH], F32)
                bpr = [T(ph2, "bpr%d" % i, [128, CH], F32) for i in range(2)]
                bpi = [T(ph2, "bpi%d" % i, [128, CH], F32) for i in range(2)]
                vr = [T(ph2, "vr%d" % i, [128, CH], F32) for i in range(2)]
                vi = [T(ph2, "vi%d" % i, [128, CH], F32) for i in range(2)]
                zr = [T(ph2, "zr%d" % i, [128, CH], BF16) for i in range(2)]
                zi = [T(ph2, "zi%d" % i, [128, CH], BF16) for i in range(2)]
                u1 = T(ph2, "u1", [128, CH], F32)
                u2 = T(ph2, "u2", [128, CH], F32)
                d1 = T(ph2, "d1", [128, CH], F32)
                d2 = T(ph2, "d2", [128, CH], F32)
                ypre = T(ph2, "ypre", [128, CH], F32)
                tiny = T(ph2, "tiny", [128, 16], F32)
                cinit = T(ph2, "cinit", [128, 2, 2], F32)
                fin_re = T(ph2, "fin_re", [128, 32], F32)
                fin_im = T(ph2, "fin_im", [128, 32], F32)
                fins_re = T(ph2, "fins_re", [128, 32, 16], F32)
                fins_im = T(ph2, "fins_im", [128, 32, 16], F32)
                m0 = T(ph2, "m0", [128, 16, 4], F32)
                decs = T(ph2, "decs", [128, 64], F32)
                sm = [T(ph2, "sm%d" % i, [128, 64], F32) for i in range(6)]
                zsb = [T(ph2, "zsb%d" % i, [128, 64], BF16) for i in range(2)]
                lam = T(ph2, "lam", [128, 2, 16], F32)

                k.op("pool", lambda E: E.iota(iot[:], pattern=[[1, CH]], base=0, channel_multiplier=0,
                                              allow_small_or_imprecise_dtypes=True), (), ["iot"])
                MS("dve", m0[:], 1.0, ["m0"])
                MS("dve", m0[:, :, 0:1], 0.0, ["m0"])

                import os as _os
                QSTOP = int(_os.environ.get("QSTOP", "99"))
                for q in range(32):
                    if q == QSTOP:
                        ckp(0.65)
                    dti = q // 4
                    qq = q % 4
                    pi = q % 2
                    cs, sn = csb[pi], snb[pi]
                    csn, snn = "cs%d" % pi, "sn%d" % pi
                    fq = sp6[:, FQ, q:q + 1]
                    magq = sp6[:, MAG, q:q + 1]
                    TS("pool", sn[:], iot[:], fq, ALU.mult, ["iot", "sp6"], [snn])
                    TS("pool", x2[:], sn[:], MAGIC, ALU.add, [snn], ["x2"], s2=MAGIC, op1=ALU.subtract)
                    TT("pool", sn[:], sn[:], x2[:], ALU.subtract, [snn, "x2"], [snn])
                    ACT(sn[:], sn[:], AF.Sin, [snn], [snn], scale=SC2PI)
                    TS("pool", cs[:], iot[:], fq, ALU.mult, ["iot", "sp6"], [csn], s2=0.25, op1=ALU.add)
                    TS("pool", x2[:], cs[:], MAGIC, ALU.add, [csn], ["x2"], s2=MAGIC, op1=ALU.subtract)
                    TT("pool", cs[:], cs[:], x2[:], ALU.subtract, [csn, "x2"], [csn])
                    ACT(cs[:], cs[:], AF.Sin, [csn], [csn], scale=SC2PI)

                    ckp(0.5)
                    for c in range(NCH):
                        bi = c % 2
                        cols = slice(c * CH, (c + 1) * CH)
                        MM(pA[:, bi * 512:bi * 512 + CH], LT_re[:, q, :], hT[:, dti, cols], True, True,
                           ["LT_re", hTn[dti]], ["pA%d" % bi])
                        MM(pB[:, bi * 512:bi * 512 + CH], LT_im[:, q, :], hT[:, dti, cols], True, True,
                           ["LT_im", hTn[dti]], ["pB%d" % bi])
                        par = pA[:, bi * 512:bi * 512 + CH]
                        pai = pB[:, bi * 512:bi * 512 + CH]
                        TT("dve", d1[:], par, cs[:], ALU.mult, ["pA%d" % bi, csn], ["d1"])
                        TT("dve", d2[:], pai, sn[:], ALU.mult, ["pB%d" % bi, snn], ["d2"])
                        TT("dve", bpr[bi][:], d1[:], d2[:], ALU.add, ["d1", "d2"], ["bpr%d" % bi])
                        TT("dve", d1[:], pai, cs[:], ALU.mult, ["pB%d" % bi, csn], ["d1"])
                        TT("dve", d2[:], par, sn[:], ALU.mult, ["pA%d" % bi, snn], ["d2"])
                        TT("dve", bpi[bi][:], d1[:], d2[:], ALU.subtract, ["d1", "d2"], ["bpi%d" % bi])
                        magb = magq.to_broadcast([128, CH])
                        if c == 0:
                            ir, ii = 0.0, 0.0
                            rd = []
                        else:
                            ir, ii = cinit[:, 0, 0:1], cinit[:, 0, 1:2]
                            rd = ["cinit"]
                        k.op("dve", lambda E: E.tensor_tensor_scan(out=vr[bi][:], data0=magb, data1=bpr[bi][:], initial=ir,
                                                                   op0=ALU.mult, op1=ALU.add),
                             ["sp6", "bpr%d" % bi] + rd, ["vr%d" % bi])
                        k.op("dve", lambda E: E.tensor_tensor_scan(out=vi[bi][:], data0=magb, data1=bpi[bi][:], initial=ii,
                                                                   op0=ALU.mult, op1=ALU.add),
                             ["sp6", "bpi%d" % bi] + rd, ["vi%d" % bi])
                        TT("pool", u1[:], vr[bi][:], cs[:], ALU.mult, ["vr%d" % bi, csn], ["u1"])
                        TT("pool", u2[:], vi[bi][:], sn[:], ALU.mult, ["vi%d" % bi, snn], ["u2"])
                        TT("pool", zr[bi][:], u1[:], u2[:], ALU.subtract, ["u1", "u2"], ["zr%d" % bi])
                        TT("pool", u1[:], vr[bi][:], sn[:], ALU.mult, ["vr%d" % bi, snn], ["u1"])
                        TT("pool", u2[:], vi[bi][:], cs[:], ALU.mult, ["vi%d" % bi, csn], ["u2"])
                        TT("pool", zi[bi][:], u1[:], u2[:], ALU.add, ["u1", "u2"], ["zi%d" % bi])
                        L = slice(CH - 1, CH)
                        TS("dve", tiny[:, 0:1], vi[bi][:, L], sn[:, L], ALU.mult, ["vi%d" % bi, snn], ["tiny"])
                        STT("dve", tiny[:, 1:2], vr[bi][:, L], cs[:, L], tiny[:, 0:1], ALU.mult, ALU.subtract,
                            ["vr%d" % bi, csn, "tiny"], ["tiny"])
                        TS("dve", tiny[:, 2:3], vi[bi][:, L], cs[:, L], ALU.mult, ["vi%d" % bi, csn], ["tiny"])
                        STT("dve", tiny[:, 3:4], vr[bi][:, L], sn[:, L], tiny[:, 2:3], ALU.mult, ALU.add,
                            ["vr%d" % bi, snn, "tiny"], ["tiny"])
                        if c < NCH - 1:
                            TS("dve", tiny[:, 4:5], tiny[:, 3:4], sn[:, 1:2], ALU.mult, ["tiny", snn], ["tiny"])
                            STT("dve", cinit[:, 0, 0:1], tiny[:, 1:2], cs[:, 1:2], tiny[:, 4:5], ALU.mult, ALU.subtract,
                                ["tiny", csn], ["cinit"])
                            TS("dve", tiny[:, 5:6], tiny[:, 3:4], cs[:, 1:2], ALU.mult, ["tiny", csn], ["tiny"])
                            STT("dve", cinit[:, 0, 1:2], tiny[:, 1:2], sn[:, 1:2], tiny[:, 5:6], ALU.mult, ALU.add,
                                ["tiny", snn], ["cinit"])
                        else:
                            CP("dve", fin_re[:, q:q + 1], tiny[:, 1:2], ["tiny"], ["fin_re"])
                            CP("dve", fin_im[:, q:q + 1], tiny[:, 3:4], ["tiny"], ["fin_im"])
                        yo = pY[32 * qq:32 * qq + 32, bi * 512:bi * 512 + CH]
                        if qq % 2 == 0:
                            ymm, csl = yo, slice(32, 64)
                        else:
                            ymm, csl = pY[32 * (qq - 1):32 * (qq + 1), bi * 512:bi * 512 + CH], slice(0, 64)
                        MM(ymm, Cre[:, q, csl], zr[bi][:], True, False, ["Cre", "zr%d" % bi], ["pY%d" % bi])
                        MM(ymm, Cimn[:, q, csl], zi[bi][:], False, True, ["Cimn", "zi%d" % bi], ["pY%d" % bi])
                        rows = slice(32 * qq, 32 * qq + 32)
                        STT("dve", ypre[rows, :], hT[rows, dti, cols], dsk[rows, dti:dti + 1], yo, ALU.mult, ALU.add,
                            [hTn[dti], "dsk", "pY%d" % bi], ["ypre"])
                        ACT(hT[rows, dti, cols], ypre[rows, :], AF.Gelu, ["ypre"], [hTn[dti]])

                    ckp(0.55)
                    scol = slice(2048, 2112)
                    MM(pA[:, 0:64], LT_re[:, q, :], hT[:, dti, scol], True, True, ["LT_re", hTn[dti]], ["pA0"])
                    MM(pB[:, 0:64], LT_im[:, q, :], hT[:, dti, scol], True, True, ["LT_im", hTn[dti]], ["pB0"])
                    v3 = lambda ap: ap.rearrange("p (s t) -> p s t", t=4)
                    cs4 = cs[:, 0:4].unsqueeze(1).to_broadcast([128, 16, 4])
                    sn4 = sn[:, 0:4].unsqueeze(1).to_broadcast([128, 16, 4])
                    par = v3(pA[:, 0:64])
                    pai = v3(pB[:, 0:64])
                    s0, s1, s2_, s3, s4, s5 = [v3(x[:]) for x in sm]
                    TT("dve", s0, par, cs4, ALU.mult, ["pA0", csn], ["sm0"])
                    TT("dve", s1, pai, sn4, ALU.mult, ["pB0", snn], ["sm1"])
                    TT("dve", s2_, s0, s1, ALU.add, ["sm0", "sm1"], ["sm2"])
                    TT("dve", s0, pai, cs4, ALU.mult, ["pB0", csn], ["sm0"])
                    TT("dve", s1, par, sn4, ALU.mult, ["pA0", snn], ["sm1"])
                    TT("dve", s3, s0, s1, ALU.subtract, ["sm0", "sm1"], ["sm3"])
                    lbr = sp6[:, LBRE, q:q + 1]
                    lbi = sp6[:, LBIM, q:q + 1]
                    TS("dve", lam[:, 0, :], s0T_im[:, q, :], lbi, ALU.mult, ["s0T_im", "sp6"], ["lam"])
                    STT("dve", lam[:, 0, :], s0T_re[:, q, :], lbr, lam[:, 0, :], ALU.mult, ALU.subtract,
                        ["s0T_re", "sp6", "lam"], ["lam"])
                    TS("dve", lam[:, 1, :], s0T_re[:, q, :], lbi, ALU.mult, ["s0T_re", "sp6"], ["lam"])
                    STT("dve", lam[:, 1, :], s0T_im[:, q, :], lbr, lam[:, 1, :], ALU.mult, ALU.add,
                        ["s0T_im", "sp6", "lam"], ["lam"])
                    TT("dve", s2_[:, :, 0:1], s2_[:, :, 0:1], lam[:, 0, :].unsqueeze(2), ALU.add, ["sm2", "lam"], ["sm2"])
                    TT("dve", s3[:, :, 0:1], s3[:, :, 0:1], lam[:, 1, :].unsqueeze(2), ALU.add, ["sm3", "lam"], ["sm3"])
                    TS("dve", decs[:], m0[:].rearrange("p s t -> p (s t)"), magq, ALU.mult, ["m0", "sp6"], ["decs"])
                    k.op("dve", lambda E: E.tensor_tensor_scan(out=sm[4][:], data0=decs[:], data1=sm[2][:], initial=0.0,
                                                               op0=ALU.mult, op1=ALU.add), ["decs", "sm2"], ["sm4"])
                    k.op("dve", lambda E: E.tensor_tensor_scan(out=sm[5][:], data0=decs[:], data1=sm[3][:], initial=0.0,
                                                               op0=ALU.mult, op1=ALU.add), ["decs", "sm3"], ["sm5"])
                    TT("dve", s0, s4, cs4, ALU.mult, ["sm4", csn], ["sm0"])
                    TT("dve", s1, s5, sn4, ALU.mult, ["sm5", snn], ["sm1"])
                    TT("dve", s2_, s0, s1, ALU.subtract, ["sm0", "sm1"], ["sm2"])
                    TT("dve", s0, s4, sn4, ALU.mult, ["sm4", snn], ["sm0"])
                    TT("dve", s1, s5, cs4, ALU.mult, ["sm5", csn], ["sm1"])
                    TT("dve", s3, s0, s1, ALU.add, ["sm0", "sm1"], ["sm3"])
                    CP("dve", zsb[0][:], sm[2][:], ["sm2"], ["zsb0"])
                    CP("dve", zsb[1][:], sm[3][:], ["sm3"], ["zsb1"])
                    CP("dve", fins_re[:, q, :].unsqueeze(2), s2_[:, :, 3:4], ["sm2"], ["fins_re"])
                    CP("dve", fins_im[:, q, :].unsqueeze(2), s3[:, :, 3:4], ["sm3"], ["fins_im"])
                    yo = pY[32 * qq:32 * qq + 32, 0:64]
                    if qq % 2 == 0:
                        ymm, csl = yo, slice(32, 64)
                    else:
                        ymm, csl = pY[32 * (qq - 1):32 * (qq + 1), 0:64], slice(0, 64)
                    MM(ymm, Cre[:, q, csl], zsb[0][:], True, False, ["Cre", "zsb0"], ["pY0"])
                    MM(ymm, Cimn[:, q, csl], zsb[1][:], False, True, ["Cimn", "zsb1"], ["pY0"])
                    rows = slice(32 * qq, 32 * qq + 32)
                    STT("dve", ypre[rows, 0:64], hT[rows, dti, scol], dsk[rows, dti:dti + 1], yo, ALU.mult, ALU.add,
                        [hTn[dti], "dsk", "pY0"], ["ypre"])
                    ACT(hT[rows, dti, scol], ypre[rows, 0:64], AF.Gelu, ["ypre"], [hTn[dti]])

                    ckp(0.6)
                ckp(0.7)
                stg = T(ph2, "stg", [32, 128], F32)
                stg2 = T(ph2, "stg2", [16, 1024], F32)
                for (fin, dst, nm) in ((fin_re, o_sre_p, "fin_re"), (fin_im, o_sim_p, "fin_im")):
                    TR(pA[0:32, 0:128], fin[:], identf[:], [nm, "identf"], ["pA0"])
                    CP("act", stg[:], pA[0:32, 0:128], ["pA0"], ["stg"])
                    LD(dst, stg[:], [], ["stg"])
                for (fin, dst, nm) in ((fins_re, o_sre_s, "fins_re"), (fins_im, o_sim_s, "fins_im")):
                    for b8 in range(4):
                        for j in range(8):
                            q = b8 * 8 + j
                            TR(pB[0:16, j * 128:(j + 1) * 128], fin[:, q, :], identf[:], [nm, "identf"], ["pB0"])
                        CP("act", stg2[:], pB[0:16, :], ["pB0"], ["stg2"])
                        LD(dst[:, b8 * 1024:(b8 + 1) * 1024], stg2[:], [], ["stg2"])
                k.barrier()

            ckp(0.8)
            with ExitStack() as ph2:
                wglu = T(ph2, "wglu", [128, 8, 2048], BF16)
                sig = T(ph2, "sig", [128, 1024], F32)
                gtmp = T(ph2, "gtmp", [128, 1024], F32)
                pg = [PS(ph2, "pg%d" % i, [128, 2048], F32) for i in range(2)]
                wv_ = w_glu.rearrange("(dt p) f -> p dt f", p=128)
                for dt in range(8):
                    LDC(wglu[:, dt, :], wv_[:, dt, :], ["wglu"])
                for t in range(NT):
                    pi = t % 2
                    for fc in range(4):
                        for dt in range(8):
                            MM(pg[pi][:, fc * 512:(fc + 1) * 512], hT[:, dt, t * 128:(t + 1) * 128],
                               wglu[:, dt, fc * 512:(fc + 1) * 512], dt == 0, dt == 7,
                               [hTn[dt], "wglu"], ["pg%d" % pi])
                    ACT(sig[:], pg[pi][:, 1024:2048], AF.Sigmoid, ["pg%d" % pi], ["sig"])
                    TT("dve", gtmp[:], pg[pi][:, 0:1024], sig[:], ALU.mult, ["pg%d" % pi, "sig"], ["gtmp"])
                    TT("dve", X[:, t, :], X[:, t, :], gtmp[:], ALU.add, ["X%d" % t, "gtmp"], ["X%d" % t])
                k.barrier()

        except _Stop:
            pass
        k.halt = False

        def attention(l):
            with ExitStack() as ph:
                psn = PS(ph, "psn", [128, 1024], BF16)
                pA = PS(ph, "pA", [128, 1024], F32)
                pB = PS(ph, "pB", [128, 2048], F32)
                psx = PS(ph, "psx", [128, 1024], BF16)
                KT = T(ph, "KT", [128, 8, 256], BF16)
                Vb = T(ph, "Vb", [128, 2, 1024], BF16)
                load_gain(norm_mem[l:l + 1, :])
                with ExitStack() as ph2:
                    wk = T(ph2, "wk", [128, 8, 1024], BF16)
                    wv = T(ph2, "wv", [128, 8, 1024], BF16)
                    kst = [T(ph2, "kst%d" % i, [128, 1024], F32) for i in range(2)]
                    wkv = mem_w_k[l].rearrange("(dt p) f -> p dt f", p=128)
                    wvv = mem_w_v[l].rearrange("(dt p) f -> p dt f", p=128)
                    for dt in range(8):
                        LDC(wk[:, dt, :], wkv[:, dt, :], ["wk"])
                        LDC(wv[:, dt, :], wvv[:, dt, :], ["wv"])
                    if l == 0:
                        memb = T(ph2, "memb", [128, 2, 1024], BF16)
                        for mt in range(2):
                            LDC(memb[:, mt, :], memp[mt * 128:(mt + 1) * 128, :], ["memb"])
                        for mt in range(2):
                            for dt in range(8):
                                TR(psx[:, dt * 128:(dt + 1) * 128], memb[:, mt, dt * 128:(dt + 1) * 128], identb[:],
                                   ["memb", "identb"], ["psx"])
                            CP("act", memT[:, :, mt * 128:(mt + 1) * 128], psx[:].rearrange("p (a b) -> p a b", a=8),
                               ["psx"], ["memT"])
                    ckp(1.25)
                    n = 0
                    for (w_, wn, dst, isv) in ((wk, "wk", o_mk, False), (wv, "wv", o_mv, True)):
                        if isv and _os2.environ.get("NOV"):
                            continue
                        for mt in range(2):
                            ks = kst[n % 2]
                            ksn = "kst%d" % (n % 2)
                            n += 1
                            for fc in range(2):
                                for dt in range(8):
                                    MM(pA[:, fc * 512:(fc + 1) * 512], memT[:, dt, mt * 128:(mt + 1) * 128],
                                       w_[:, dt, fc * 512:(fc + 1) * 512], dt == 0, dt == 7, ["memT", wn], ["pA"])
                            CP("act", ks[:], pA[:], ["pA"], [ksn])
                            if isv:
                                CP("dve", Vb[:, mt, :], ks[:], [ksn], ["Vb"])
                            if NOOUT is None:
                                LD(dst[l * 256 + mt * 128:l * 256 + (mt + 1) * 128, :], ks[:], [], [ksn])
                    ckp(1.3)
                    for ft in range(8):
                        for dt in range(8):
                            MM(pB[:, (ft % 4) * 512:(ft % 4) * 512 + 256], wk[:, dt, ft * 128:(ft + 1) * 128], memT[:, dt, :],
                               dt == 0, dt == 7, ["wk", "memT"], ["pB"])
                        CP("act", KT[:, ft, :], pB[:, (ft % 4) * 512:(ft % 4) * 512 + 256], ["pB"], ["KT"])
                    k.barrier()
                ckp(1.4)

                with ExitStack() as ph2:
                    wq = T(ph2, "wq", [128, 8, 1024], BF16)
                    wo = T(ph2, "wo", [128, 8, 1024], BF16)
                    hTg = T(ph2, "hTg", [128, 8, 512], BF16)
                    qT = T(ph2, "qT", [128, 8, 512], BF16)
                    Pf = T(ph2, "Pf", [128, 4, 256], F32)
                    Pn = T(ph2, "Pn", [128, 4, 256], BF16)
                    PT = T(ph2, "PT", [128, 8, 128], BF16)
                    oT = T(ph2, "oT", [128, 8, 128], BF16)
                    stat = T(ph2, "stat", [128, 16], F32)
                    Kc = T(ph2, "Kc", [128, 2, 1024], BF16)
                    Vc = [T(ph2, "Vc%d" % i, [128, 2, 1024], BF16) for i in range(2)]
                    KTc = [T(ph2, "KTc%d" % i, [128, 8, 256], BF16) for i in range(2)]
                    qTm = [T(ph2, "qTm%d" % i, [128, 8, 128], BF16) for i in range(2)]
                    wqv = mem_w_q[l].rearrange("(dt p) f -> p dt f", p=128)
                    wov = mem_w_o[l].rearrange("(dt p) f -> p dt f", p=128)
                    for dt in range(8):
                        LDC(wq[:, dt, :], wqv[:, dt, :], ["wq"])
                        LDC(wo[:, dt, :], wov[:, dt, :], ["wo"])

                    def softmax_pv_o(t, nheadbanks):
                        hs = 512 if nheadbanks == 4 else 256
                        sview = pB[:, 0:4 * hs].rearrange("p (h m) -> p h m", h=4)[:, :, 0:256]
                        k.op("dve", lambda E: E.tensor_reduce(out=stat[:, 0:4], in_=sview, axis=AX.X, op=ALU.max),
                             ["pB"], ["stat"])
                        TS("dve", stat[:, 4:8], stat[:, 0:4], -1.0, ALU.mult, ["stat"], ["stat"])
                        for h in range(4):
                            ACT(Pf[:, h, :], pB[:, h * hs:h * hs + 256], AF.Exp, ["pB", "stat"], ["Pf", "stat"],
                                bias=stat[:, 4 + h:5 + h], scale=1.0, accum_out=stat[:, 8 + h:9 + h])
                        k.op("dve", lambda E: E.reciprocal(out=stat[:, 12:16], in_=stat[:, 8:12]), ["stat"], ["stat"])
                        TT("dve", Pn[:], Pf[:], stat[:, 12:16].unsqueeze(2).to_broadcast([128, 4, 256]), ALU.mult,
                           ["Pf", "stat"], ["Pn"])
                        for h in range(4):
                            for mt in range(2):
                                TR(psx[:, (h * 2 + mt) * 128:(h * 2 + mt + 1) * 128], Pn[:, h, mt * 128:(mt + 1) * 128],
                                   identb[:], ["Pn", "identb"], ["psx"])
                        CP("act", PT[:], psx[:].rearrange("p (a b) -> p a b", a=8), ["psx"], ["PT"])

                    def oproj(t):
                        CP("act", oT[:], pA[:].rearrange("p (a b) -> p a b", a=8), ["pA"], ["oT"])
                        for fc in range(2):
                            for ft in range(8):
                                MM(pA[:, fc * 512:(fc + 1) * 512], oT[:, ft, :], wo[:, ft, fc * 512:(fc + 1) * 512],
                                   ft == 0, ft == 7, ["oT", "wo"], ["pA"])
                        TT("dve", X[:, t, :], X[:, t, :], pA[:], ALU.add, ["X%d" % t, "pA"], ["X%d" % t])

                    for g in range(4):
                        for j in range(4):
                            norm_tile(4 * g + j, psn, hTg, ["hTg"], j * 128)
                        for ft in range(8):
                            bo = 1024 + (ft % 2) * 512
                            for dt in range(8):
                                MM(pB[:, bo:bo + 512], wq[:, dt, ft * 128:(ft + 1) * 128], hTg[:, dt, :], dt == 0, dt == 7,
                                   ["wq", "hTg"], ["pBq%d" % (ft % 2)])
                            ACT(qT[:, ft, :], pB[:, bo:bo + 512], AF.Copy, ["pBq%d" % (ft % 2)], ["qT"], scale=1.0 / 16.0)
                        for j in range(4):
                            t = 4 * g + j
                            tc_ = slice(j * 128, (j + 1) * 128)
                            for h in range(4):
                                for jj in range(2):
                                    MM(pB[:, h * 256:(h + 1) * 256], qT[:, 2 * h + jj, tc_], KT[:, 2 * h + jj, :],
                                       jj == 0, jj == 1, ["qT", "KT"], ["pB"])
                            if j == 0 and g == 0:
                                ckp(1.45)
                            softmax_pv_o(t, 2)
                            if j == 0 and g == 0:
                                ckp(1.5)
                            for ft in range(8):
                                h = ft // 2
                                for mt in range(2):
                                    MM(pA[:, ft * 128:(ft + 1) * 128], Vb[:, mt, ft * 128:(ft + 1) * 128], PT[:, h * 2 + mt, :],
                                       mt == 0, mt == 1, ["Vb", "PT"], ["pA"])
                            oproj(t)
                            if j == 0 and g == 0:
                                ckp(1.6)
                    ckp(1.7)
                    t = 16
                    norm_tile(t, psn, hTg, ["hTg"], 0)
                    for ft in range(8):
                        for dt in range(8):
                            MM(pA[:, ft * 128:(ft + 1) * 128], wq[:, dt, ft * 128:(ft + 1) * 128], hTg[:, dt, 0:128],
                               dt == 0, dt == 7, ["wq", "hTg"], ["pA"])
                    ACT(qT[:, :, 0:128], pA[:].rearrange("p (a b) -> p a b", a=8), AF.Copy, ["pA"], ["qT"], scale=1.0 / 16.0)
                    for i in range(16):
                        pi = i % 2
                        for mt in range(2):
                            LDC(Kc[:, mt, :], ck[l, i, mt * 128:(mt + 1) * 128, :], ["Kc"])
                        for mt in range(2):
                            for ft in range(8):
                                TR(psx[:, ft * 128:(ft + 1) * 128], Kc[:, mt, ft * 128:(ft + 1) * 128], identb[:],
                                   ["Kc", "identb"], ["psx"])
                            CP("act", KTc[pi][:, :, mt * 128:(mt + 1) * 128], psx[:].rearrange("p (a b) -> p a b", a=8),
                               ["psx"], ["KTc%d" % pi])
                        MS("pool", qTm[pi][:], 0.0, ["qTm%d" % pi])
                        CP("pool", qTm[pi][:, :, 4 * i:4 * i + 4], qT[:, :, 4 * i:4 * i + 4], ["qT"], ["qTm%d" % pi])
                        for h in range(4):
                            for jj in range(2):
                                MM(pB[:, h * 512:h * 512 + 256], qTm[pi][:, 2 * h + jj, :], KTc[pi][:, 2 * h + jj, :],
                                   (i == 0 and jj == 0), (i == 15 and jj == 1), ["qTm%d" % pi, "KTc%d" % pi], ["pB"])
                    ckp(1.8)
                    softmax_pv_o(t, 4)
                    MS("dve", oT[:], 0.0, ["oT"])
                    for i in range(16):
                        pi = i % 2
                        for mt in range(2):
                            LDC(Vc[pi][:, mt, :], cv[l, i, mt * 128:(mt + 1) * 128, :], ["Vc%d" % pi])
                        for ft in range(8):
                            h = ft // 2
                            for mt in range(2):
                                MM(pA[:, ft * 128 + 4 * i:ft * 128 + 4 * i + 4], Vc[pi][:, mt, ft * 128:(ft + 1) * 128],
                                   PT[:, h * 2 + mt, 4 * i:4 * i + 4], mt == 0, mt == 1, ["Vc%d" % pi, "PT"], ["pA"])
                    ckp(1.9)
                    pAv = pA[:].rearrange("p (a b) -> p a b", a=8)
                    CP("act", oT[:, :, 0:64], pAv[:, :, 0:64], ["pA"], ["oT"])
                    for fc in range(2):
                        for ft in range(8):
                            MM(pA[:, fc * 512:(fc + 1) * 512], oT[:, ft, :], wo[:, ft, fc * 512:(fc + 1) * 512],
                               ft == 0, ft == 7, ["oT", "wo"], ["pA"])
                    TT("dve", X[0:64, t, :], X[0:64, t, :], pA[0:64, :], ALU.add, ["X16", "pA"], ["X16"])
                    k.barrier()

        def peer(l):
            NU = 5
            NV = 5
            with ExitStack() as ph:
                psn = PS(ph, "psn", [128, 1024], BF16)
                pQ = PS(ph, "pQ", [128, 2048], F32)
                pO = PS(ph, "pO", [128, 1024], F32)
                wqp = T(ph, "wqp", [128, 8, 2048], BF16)
                keyT = T(ph, "keyT", [128, 2, 128], BF16)
                kld = T(ph, "kld", [128, 128], F32)
                hTt = T(ph, "hTt", [128, 8, 128], BF16)
                qpT = T(ph, "qpT", [128, 16, 128], BF16)
                V12 = T(ph, "V12", [128, 16, 16], F32)
                I12u = T(ph, "I12u", [128, 16, 16], U32)
                I12f = T(ph, "I12f", [128, 16, 16], F32)
                work = T(ph, "work", [128, 128], F32)
                cand = T(ph, "cand", [128, 8, 256], F32)
                work2 = T(ph, "work2", [128, 256], F32)
                scv = T(ph, "scv", [128, 8, 16], F32)
                ciu = T(ph, "ciu", [128, 8, 16], U32)
                cff = T(ph, "cff", [128, 128], F32)
                caf = T(ph, "caf", [128, 128], F32)
                cbf = T(ph, "cbf", [128, 128], F32)
                io16 = T(ph, "io16", [128, 16], F32)
                eq = cand[:].rearrange("p h (k a) -> p (h k) a", a=16)
                isel = T(ph, "isel", [128, 2, 128], F32)
                ef = T(ph, "ef", [128, 128], F32)
                eu = T(ph, "eu", [128, 128], U32)
                gsm = T(ph, "gsm", [128, 8, 16], F32)
                gst = T(ph, "gst", [128, 16], F32)
                actv = T(ph, "actv", [128, 128], F32)
                wgt = T(ph, "wgt", [128, 128], F32)
                Ub = [T(ph, "Ub%d" % i, [128, 1024], F32) for i in range(NU)]
                Vg = [T(ph, "Vg%d" % i, [128, 1024], F32) for i in range(NV)]
                Vh = [T(ph, "Vh%d" % i, [128, 1024], BF16) for i in range(2)]
                dg = [T(ph, "dg%d" % i, [128, 128], BF16) for i in range(4)]
                load_gain(norm_ffn[l:l + 1, :])
                wv_ = peer_wq[l].rearrange("(dt p) f -> p dt f", p=128)
                for dt in range(8):
                    LDC(wqp[:, dt, :], wv_[:, dt, :], ["wqp"])
                for j, ksrc in enumerate((peer_k1, peer_k2)):
                    LD(kld[:], ksrc[l], ["kld"])
                    TR(pO[:, 0:128], kld[:], identf[:], ["kld", "identf"], ["pO"])
                    CP("act", keyT[:, j, :], pO[:, 0:128], ["pO"], ["keyT"])
                k.op("pool", lambda E: E.iota(io16[:], pattern=[[1, 16]], base=0, channel_multiplier=0,
                                              allow_small_or_imprecise_dtypes=True), (), ["io16"])
                for t in range(NT):
                    norm_tile(t, psn, hTt, ["hTt"], 0)
                    ht = htok[t % 2]
                    htn = "htok%d" % (t % 2)
                    for ft in range(16):
                        for dt in range(8):
                            MM(pQ[:, ft * 128:(ft + 1) * 128], wqp[:, dt, ft * 128:(ft + 1) * 128], hTt[:, dt, :],
                               dt == 0, dt == 7, ["wqp", "hTt"], ["pQ"])
                    CP("act", qpT[:], pQ[:].rearrange("p (a b) -> p a b", a=16), ["pQ"], ["qpT"])
                    for blk in range(16):
                        MM(pQ[:, blk * 128:(blk + 1) * 128], qpT[:, blk, :], keyT[:, blk % 2, :], True, True,
                           ["qpT", "keyT"], ["pQ"])
                    for blk in range(16):
                        sb = pQ[:, blk * 128:(blk + 1) * 128]
                        k.op("dve", lambda E: E.max(out=V12[:, blk, 0:8], in_=sb), ["pQ"], ["V12"])
                        k.op("dve", lambda E: E.max_index(out=I12u[:, blk, 0:8], in_max=V12[:, blk, 0:8], in_values=sb),
                             ["pQ", "V12"], ["I12u"])
                        k.op("dve", lambda E: E.match_replace(out=work[:], in_to_replace=V12[:, blk, 0:8], in_values=sb,
                                                              imm_value=-1e30), ["pQ", "V12"], ["work"])
                        k.op("dve", lambda E: E.max(out=V12[:, blk, 8:16], in_=work[:]), ["work"], ["V12"])
                        k.op("dve", lambda E: E.max_index(out=I12u[:, blk, 8:16], in_max=V12[:, blk, 8:16], in_values=work[:]),
                             ["work", "V12"], ["I12u"])
                    CP("dve", I12f[:], I12u[:], ["I12u"], ["I12f"])
                    V4 = V12[:].rearrange("p (h two) a -> p h two a", two=2)
                    I4 = I12f[:].rearrange("p (h two) a -> p h two a", two=2)
                    TT("dve", cand[:].rearrange("p h (a b) -> p h a b", a=16),
                       V4[:, :, 0, :].unsqueeze(3).to_broadcast([128, 8, 16, 16]),
                       V4[:, :, 1, :].unsqueeze(2).to_broadcast([128, 8, 16, 16]), ALU.add, ["V12"], ["cand"])
                    for h in range(8):
                        k.op("dve", lambda E: E.max(out=scv[:, h, 0:8], in_=cand[:, h, :]), ["cand"], ["scv"])
                        k.op("dve", lambda E: E.max_index(out=ciu[:, h, 0:8], in_max=scv[:, h, 0:8], in_values=cand[:, h, :]),
                             ["cand", "scv"], ["ciu"])
                        k.op("dve", lambda E: E.match_replace(out=work2[:], in_to_replace=scv[:, h, 0:8], in_values=cand[:, h, :],
                                                              imm_value=-1e30), ["cand", "scv"], ["work2"])
                        k.op("dve", lambda E: E.max(out=scv[:, h, 8:16], in_=work2[:]), ["work2"], ["scv"])
                        k.op("dve", lambda E: E.max_index(out=ciu[:, h, 8:16], in_max=scv[:, h, 8:16], in_values=work2[:]),
                             ["work2", "scv"], ["ciu"])
                    civ = ciu[:].rearrange("p h k -> p (h k)")
                    CP("dve", cff[:], civ, ["ciu"], ["cff"])
                    TS("dve", caf[:], cff[:], 1.0 / 16.0, ALU.mult, ["cff"], ["caf"], s2=-0.46875, op1=ALU.add)
                    TS("dve", caf[:], caf[:], MAGIC, ALU.add, ["caf"], ["caf"], s2=MAGIC, op1=ALU.subtract)
                    STT("dve", cbf[:], caf[:], -16.0, cff[:], ALU.mult, ALU.add, ["caf", "cff"], ["cbf"])
                    io_b = io16[:, :].unsqueeze(1).to_broadcast([128, 128, 16])
                    for half, cf, cfn in ((0, caf, "caf"), (1, cbf, "cbf")):
                        TT("dve", eq, io_b, cf[:, :].unsqueeze(2).to_broadcast([128, 128, 16]), ALU.is_equal,
                           ["io16", cfn], ["cand"])
                        eq4 = cand[:].rearrange("p h (k a) -> p h k a", a=16)
                        TT("dve", eq4, eq4, I4[:, :, half, :].unsqueeze(2).to_broadcast([128, 8, 16, 16]), ALU.mult,
                           ["cand", "I12f"], ["cand"])
                        k.op("dve", lambda E: E.tensor_reduce(out=isel[:, half, :], in_=eq, axis=AX.X, op=ALU.add),
                             ["cand"], ["isel"])
                    STT("dve", ef[:], isel[:, 0, :], 128.0, isel[:, 1, :], ALU.mult, ALU.add, ["isel"], ["ef"])
                    TS("dve", ef[:], ef[:], float(16384 * l), ALU.add, ["ef"], ["ef"])
                    CP("dve", eu[:], ef[:], ["ef"], ["eu"])
                    TT("dve", gsm[:], scv[:], scv[:, :, 0:1].to_broadcast([128, 8, 16]), ALU.subtract, ["scv"], ["gsm"])
                    ACT(gsm[:], gsm[:], AF.Exp, ["gsm"], ["gsm"])
                    k.op("dve", lambda E: E.tensor_reduce(out=gst[:, 0:8], in_=gsm[:], axis=AX.X, op=ALU.add), ["gsm"], ["gst"])
                    k.op("dve", lambda E: E.reciprocal(out=gst[:, 8:16], in_=gst[:, 0:8]), ["gst"], ["gst"])
                    TT("dve", gsm[:], gsm[:], gst[:, 8:16].unsqueeze(2).to_broadcast([128, 8, 16]), ALU.mult,
                       ["gsm", "gst"], ["gsm"])
                    for s in range(128):
                        ub = Ub[s % NU]
                        ubn = "Ub%d" % (s % NU)
                        k.dma("pool", lambda E: E.indirect_dma_start(
                            out=ub[:], out_offset=None, in_=peer_u,
                            in_offset=bass.IndirectOffsetOnAxis(ap=eu[:, s:s + 1], axis=0)), ["eu"], [ubn])
                        k.op("dve", lambda E: E.scalar_tensor_tensor(
                            out=junk[:], in0=ub[:], scalar=1.0, in1=ht[:], op0=ALU.mult, op1=ALU.mult,
                            accum_out=actv[:, s:s + 1]), [ubn, htn], ["junk", "actv"])
                    ACT(actv[:], actv[:], AF.Gelu, ["actv"], ["actv"])
                    TT("dve", wgt[:], actv[:], gsm[:].rearrange("p h k -> p (h k)"), ALU.mult, ["actv", "gsm"], ["wgt"])
                    for s in range(128):
                        vg = Vg[s % NV]
                        vgn = "Vg%d" % (s % NV)
                        dgi = dg[s % 4]
                        dgn = "dg%d" % (s % 4)
                        k.dma("pool", lambda E: E.indirect_dma_start(
                            out=vg[:], out_offset=None, in_=peer_v,
                            in_offset=bass.IndirectOffsetOnAxis(ap=eu[:, s:s + 1], axis=0)), ["eu"], [vgn])
                        TS("dve", dgi[:], identf[:], wgt[:, s:s + 1], ALU.mult, ["identf", "wgt"], [dgn])
                        vh = Vh[s % 2]
                        vhn = "Vh%d" % (s % 2)
                        CP("act", vh[:], vg[:], [vgn], [vhn])
                        for fc in range(2):
                            MM(pO[:, fc * 512:(fc + 1) * 512], dgi[:], vh[:, fc * 512:(fc + 1) * 512], s == 0, s == 127,
                               [dgn, vhn], ["pO"])
                    TT("dve", X[:, t, :], X[:, t, :], pO[:], ALU.add, ["X%d" % t, "pO"], ["X%d" % t])
                k.barrier()

        def conv_mixer():
            GW = 256
            with ExitStack() as ph:
                psn = PS(ph, "psn", [128, 1024], BF16)
                pC = [PS(ph, "pC%d" % i, [128, 512], F32) for i in range(3)]
                pX = PS(ph, "pX", [128, 1024], F32)
                win = T(ph, "win", [128, 8, 3072], BF16)
                wout = T(ph, "wout", [128, 8, 1024], BF16)
                cw24 = T(ph, "cw24", [24, 128], F32)
                cw = T(ph, "cw", [128, 24], F32)
                hTg = T(ph, "hTg", [128, 8, GW], BF16)
                uT = T(ph, "uT", [128, 8, GW], BF16)
                vp = [T(ph, "vp%d" % i, [128, GW + 2], F32) for i in range(2)]
                hvs = T(ph, "hvs", [128, GW], F32)
                acc = T(ph, "acc", [128, GW], F32)
                cbuf = T(ph, "cbuf", [128, 8, 2], F32)
                scb = T(ph, "scb", [128, 8, 32], F32)
                cso = T(ph, "cso", [128, 8, 32], F32)
                vps = T(ph, "vps", [128, 16, 6], F32)
                accs = T(ph, "accs", [128, 16, 4], F32)
                ostg = T(ph, "ostg", [32, 1024], F32)
                load_gain(norm_mix[1:2, :])
                wiv = conv_w_in.rearrange("(dt p) f -> p dt f", p=128)
                wov = conv_w_out.rearrange("(dt p) f -> p dt f", p=128)
                for dt in range(8):
                    LDC(win[:, dt, :], wiv[:, dt, :], ["win"])
                    LDC(wout[:, dt, :], wov[:, dt, :], ["wout"])
                LD(cw24[:], conv_w, ["cw24"])
                TR(pX[:, 0:24], cw24[:], identf[0:24, 0:24], ["cw24", "identf"], ["pX"])
                CP("act", cw[:], pX[:, 0:24], ["pX"], ["cw"])
                LD(ostg[:], sconv, ["ostg"])
                for ft in range(8):
                    TR(pX[:, 32 * ft:32 * ft + 32], ostg[:, ft * 128:(ft + 1) * 128], identf[0:32, 0:32],
                       ["ostg", "identf"], ["pX"])
                CP("act", scb[:].rearrange("p a b -> p (a b)"), pX[:, 0:256], ["pX"], ["scb"])
                MS("dve", cbuf[:], 0.0, ["cbuf"])

                def wout_tile(t, c0):
                    for fc in range(2):
                        for ft in range(8):
                            MM(pX[:, fc * 512:(fc + 1) * 512], uT[:, ft, c0:c0 + 128], wout[:, ft, fc * 512:(fc + 1) * 512],
                               ft == 0, ft == 7, ["uT", "wout"], ["pX"])
                    TT("dve", X[:, t, :], X[:, t, :], pX[:], ALU.add, ["X%d" % t, "pX"], ["X%d" % t])

                NG = 2048 // GW
                TPG = GW // 128
                for g in range(NG + 1):
                    sample = (g == NG)
                    ncol = 128 if sample else GW
                    if sample:
                        norm_tile(16, psn, hTg, ["hTg"], 0)
                    else:
                        for j in range(TPG):
                            norm_tile(TPG * g + j, psn, hTg, ["hTg"], j * 128)
                    for ft in range(8):
                        for j3, off in enumerate((0, 1024, 2048)):
                            for dt in range(8):
                                MM(pC[j3][:, 0:ncol], win[:, dt, off + ft * 128:off + (ft + 1) * 128], hTg[:, dt, 0:ncol],
                                   dt == 0, dt == 7, ["win", "hTg"], ["pC%d" % j3])
                        w0 = cw[:, 0 * 8 + ft:0 * 8 + ft + 1]
                        w1 = cw[:, 1 * 8 + ft:1 * 8 + ft + 1]
                        w2 = cw[:, 2 * 8 + ft:2 * 8 + ft + 1]
                        if not sample:
                            vpp = vp[ft % 2]
                            vpn = "vp%d" % (ft % 2)
                            CP("act", hvs[:], pC[2][:, 0:GW], ["pC2"], ["hvs"])
                            CP("dve", vpp[:, 0:2], cbuf[:, ft, :], ["cbuf"], [vpn])
                            TT("dve", vpp[:, 2:GW + 2], pC[1][:, 0:GW], hvs[:], ALU.mult, ["pC1", "hvs"], [vpn])
                            CP("dve", cbuf[:, ft, :], vpp[:, GW:GW + 2], [vpn], ["cbuf"])
                            TS("dve", acc[:], vpp[:, 0:GW], w0, ALU.mult, [vpn, "cw"], ["acc"])
                            STT("dve", acc[:], vpp[:, 1:GW + 1], w1, acc[:], ALU.mult, ALU.add, [vpn, "cw", "acc"], ["acc"])
                            STT("dve", acc[:], vpp[:, 2:GW + 2], w2, acc[:], ALU.mult, ALU.add, [vpn, "cw", "acc"], ["acc"])
                            TT("dve", uT[:, ft, :], pC[0][:, 0:GW], acc[:], ALU.mult, ["pC0", "acc"], ["uT"])
                        else:
                            CP("act", hvs[:, 0:64], pC[2][:, 0:64], ["pC2"], ["hvs"])
                            CP("dve", vps[:, :, 0:2], scb[:, ft, :].rearrange("p (s r) -> p s r", r=2), ["scb"], ["vps"])
                            TT("dve", vps[:, :, 2:6], pC[1][:, 0:64].rearrange("p (s t) -> p s t", t=4),
                               hvs[:, 0:64].rearrange("p (s t) -> p s t", t=4), ALU.mult, ["pC1", "hvs"], ["vps"])
                            CP("dve", cso[:, ft, :].rearrange("p (s r) -> p s r", r=2), vps[:, :, 4:6], ["vps"], ["cso"])
                            TS("dve", accs[:], vps[:, :, 0:4], w0, ALU.mult, ["vps", "cw"], ["accs"])
                            STT("dve", accs[:], vps[:, :, 1:5], w1, accs[:], ALU.mult, ALU.add, ["vps", "cw", "accs"], ["accs"])
                            STT("dve", accs[:], vps[:, :, 2:6], w2, accs[:], ALU.mult, ALU.add, ["vps", "cw", "accs"], ["accs"])
                            MS("dve", uT[:, ft, 0:128], 0.0, ["uT"])
                            TT("dve", uT[:, ft, 0:64].rearrange("p (s t) -> p s t", t=4),
                               pC[0][:, 0:64].rearrange("p (s t) -> p s t", t=4), accs[:], ALU.mult, ["pC0", "accs"], ["uT"])
                    if not sample:
                        for j in range(TPG):
                            wout_tile(TPG * g + j, j * 128)
                    else:
                        wout_tile(16, 0)
                    if g == NG - 1:
                        for ft in range(8):
                            TR(pX[0:2, ft * 128:(ft + 1) * 128], cbuf[:, ft, :], identf[:], ["cbuf", "identf"], ["pX"])
                        CP("act", ostg[0:2, :], pX[0:2, :], ["pX"], ["ostg"])
                        LD(o_conv_p, ostg[0:2, :], [], ["ostg"])
                for ft in range(8):
                    TR(pX[0:32, ft * 128:(ft + 1) * 128], cso[:, ft, :], identf[:], ["cso", "identf"], ["pX"])
                CP("act", ostg[:], pX[0:32, :], ["pX"], ["ostg"])
                LD(o_conv_s, ostg[:], [], ["ostg"])
                k.barrier()

        if stop >= 1.2:
            attention(0)
            k.halt = False
        if stop >= 3:
            peer(0)
        if stop >= 4:
            conv_mixer()
        if stop >= 5:
            attention(1)
        if stop >= 6:
            peer(1)

        with ExitStack() as ph:
            yo = [T(ph, "yo%d" % i, [128, 1024], F32) for i in range(2)]
            load_gain(norm_final[0:1, :])
            for t in range(NT):
                i = t % 2
                ACT(junk[:], X[:, t, :], AF.Square, ["X%d" % t], ["junk", "nss%d" % i], accum_out=nsm[:, 2 * i:2 * i + 1])
                ACT(nsm[:, 2 * i + 1:2 * i + 2], nsm[:, 2 * i:2 * i + 1], AF.Sqrt, ["nss%d" % i], ["nrs%d" % i],
                    scale=1.0 / 1024.0, bias=epsT[:, 0:1])
                k.op("dve", lambda E: E.reciprocal(out=nsm[:, 4 + i:5 + i], in_=nsm[:, 2 * i + 1:2 * i + 2]),
                     ["nrs%d" % i], ["nri%d" % i])
                STT("dve", yo[i][:], X[:, t, :], nsm[:, 4 + i:5 + i], gbc[:], ALU.mult, ALU.mult,
                    ["X%d" % t, "nri%d" % i, "gbc"], ["yo%d" % i])
                if t < 16:
                    LD(o_yp[t * 128:(t + 1) * 128, :], yo[i][:], [], ["yo%d" % i])
                else:
                    LD(o_ys, yo[i][0:64, :], [], ["yo%d" % i])
            k.barrier()
        print("ninstr", k.ninstr)
    return nc


_NC = None
_STOP = 99


def kernel(**inp):
    global _NC
    if _NC is None:
        _NC = build(_STOP)
    nc = _NC
    f = lambda a: np.ascontiguousarray(np.asarray(a, dtype=np.float32))
    common = {
        "norm_mix": f(inp["norm_mix"]), "norm_mem": f(inp["norm_mem"]), "norm_ffn": f(inp["norm_ffn"]),
        "norm_final": f(inp["norm_final"]).reshape(1, 1024),
        "ssm_a_re": f(inp["ssm_a_re"])[0], "ssm_a_im": f(inp["ssm_a_im"])[0], "ssm_log_dt": f(inp["ssm_log_dt"])[0],
        "ssm_b_re": f(inp["ssm_b_re"])[0], "ssm_b_im": f(inp["ssm_b_im"])[0],
        "ssm_c_re": f(inp["ssm_c_re"])[0], "ssm_c_im": f(inp["ssm_c_im"])[0],
        "ssm_d": f(inp["ssm_d"]).reshape(8, 128), "ssm_w_glu": f(inp["ssm_w_glu"])[0],
        "conv_w_in": f(inp["conv_w_in"])[0], "conv_w": f(inp["conv_w"]).reshape(24, 128),
        "conv_w_out": f(inp["conv_w_out"])[0],
        "mem_w_q": f(inp["mem_w_q"]), "mem_w_k": f(inp["mem_w_k"]), "mem_w_v": f(inp["mem_w_v"]),
        "mem_w_o": f(inp["mem_w_o"]), "peer_w_query": f(inp["peer_w_query"]),
        "peer_key1": f(inp["peer_key1"]), "peer_key2": f(inp["peer_key2"]),
        "peer_u": f(inp["peer_u"]).reshape(32768, 1024)[:(32768 if _STOP >= 3 else 128)],
        "peer_v": f(inp["peer_v"]).reshape(32768, 1024)[:(32768 if _STOP >= 3 else 128)],
    }
    x_prompt = f(inp["x_prompt"]); x_sample = f(inp["x_sample"]); mem_prompt = f(inp["mem_prompt"])
    sre = f(inp["state_ssm_re"]); sim = f(inp["state_ssm_im"]); sconv = f(inp["state_conv"])
    ckk = f(inp["cache_mem_k"]); cvv = f(inp["cache_mem_v"])
    in_maps = []
    for c in range(NCORES):
        s = slice(16 * c, 16 * c + 16)
        m = dict(common)
        m["xp"] = x_prompt[c]
        m["xs"] = np.ascontiguousarray(x_sample[s].reshape(64, 1024))
        m["memp"] = mem_prompt[c]
        m["s0re"] = np.ascontiguousarray(sre[0, s].reshape(16, 4096))
        m["s0im"] = np.ascontiguousarray(sim[0, s].reshape(16, 4096))
        m["sconv"] = np.ascontiguousarray(sconv[0, s].reshape(32, 1024))
        m["ck"] = np.ascontiguousarray(ckk[:, s].reshape(2, 16, 256, 1024))
        m["cv"] = np.ascontiguousarray(cvv[:, s].reshape(2, 16, 256, 1024))
        in_maps.append(m)
    res = run_bass_kernel_spmd(nc, in_maps, core_ids=list(range(NCORES)))
    R = res.results
    y_prompt = np.stack([R[c]["o_yp"] for c in range(NCORES)]).reshape(8, 2048, 1024)
    y_sample = np.concatenate([R[c]["o_ys"].reshape(16, 4, 1024) for c in range(NCORES)], axis=0)
    ssm_re_p = np.stack([R[c]["o_sre_p"].reshape(64, 64) for c in range(NCORES)])[None]
    ssm_im_p = np.stack([R[c]["o_sim_p"].reshape(64, 64) for c in range(NCORES)])[None]
    conv_p = np.stack([R[c]["o_conv_p"] for c in range(NCORES)])[None]
    mk = np.stack([R[c]["o_mk"].reshape(2, 256, 1024) for c in range(NCORES)], axis=1).reshape(2, 8, 256, 4, 256)
    mv = np.stack([R[c]["o_mv"].reshape(2, 256, 1024) for c in range(NCORES)], axis=1).reshape(2, 8, 256, 4, 256)
    ssm_re_s = np.concatenate([R[c]["o_sre_s"].reshape(16, 64, 64) for c in range(NCORES)], axis=0)[None]
    ssm_im_s = np.concatenate([R[c]["o_sim_s"].reshape(16, 64, 64) for c in range(NCORES)], axis=0)[None]
    conv_s = np.concatenate([R[c]["o_conv_s"].reshape(16, 2, 1024) for c in range(NCORES)], axis=0)[None]
    out = (y_prompt, y_sample, ssm_re_p, ssm_im_p, conv_p, mk, mv, ssm_re_s, ssm_im_s, conv_s)
    return tuple(np.ascontiguousarray(o, dtype=np.float32) for o in out)
```

```python
from contextlib import ExitStack
import numpy as np
import concourse.bass as bass
import concourse.mybir as mybir
from concourse.bass_utils import run_bass_kernel_spmd

F32 = mybir.dt.float32
BF16 = mybir.dt.bfloat16
U32 = mybir.dt.uint32
ALU = mybir.AluOpType
AF = mybir.ActivationFunctionType
AX = mybir.AxisListType

NCORES = 8
import os as _os2
NOOUT = _os2.environ.get("NOOUT")
NT = 17
TC = NT * 128
MAGIC = 12582912.0
SC2PI = 6.283185
INV2PI = 0.15915494309189535
EPS = 1e-6


class Buf:
    __slots__ = ("w", "r")

    def __init__(self):
        self.w = None
        self.r = []


class K:
    def __init__(self, nc, stack, n_dma_sems=32):
        self.nc = nc
        self.eng = {"pe": nc.tensor, "dve": nc.vector, "act": nc.scalar,
                    "pool": nc.gpsimd, "sp": nc.sync}
        self.sem = {}
        self.cnt = {}
        for e in self.eng:
            self.sem[e] = stack.enter_context(nc.semaphore("s_" + e))
            self.cnt[e] = 0
        self.seen = {e: {} for e in self.eng}
        self.dsem = []
        for i in range(n_dma_sems):
            key = "d%d" % i
            self.sem[key] = stack.enter_context(nc.semaphore("s_" + key))
            self.cnt[key] = 0
            self.dsem.append(key)
        self.dnext = 0
        self.bufs = {}
        self.ninstr = 0
        self.halt = False

    def buf(self, name):
        b = self.bufs.get(name)
        if b is None:
            b = Buf()
            self.bufs[name] = b
        return b

    def _deps(self, reads, writes):
        deps = {}

        def add(ev):
            if ev is not None and deps.get(ev[0], 0) < ev[1]:
                deps[ev[0]] = ev[1]
        for b in reads:
            add(self.buf(b).w)
        for b in writes:
            bb = self.buf(b)
            add(bb.w)
            for ev in bb.r:
                add(ev)
        return deps

    def _wait(self, e, deps, skip_self=False):
        eng = self.eng[e]
        for kk, v in deps.items():
            if kk == e and skip_self:
                continue
            if self.seen[e].get(kk, 0) >= v:
                continue
            eng.wait_ge(self.sem[kk], v)
            self.seen[e][kk] = v

    def _mark(self, ev, reads, writes):
        for b in reads:
            self.buf(b).r.append(ev)
        for b in writes:
            bb = self.buf(b)
            bb.w = ev
            bb.r = []

    def op(self, e, fn, reads=(), writes=()):
        if self.halt:
            return None
        deps = self._deps(reads, writes)
        self._wait(e, deps, skip_self=(e == "pe"))
        ins = fn(self.eng[e])
        self.cnt[e] += 1
        ins.then_inc(self.sem[e], 1)
        ev = (e, self.cnt[e])
        self._mark(ev, reads, writes)
        self.ninstr += 1
        return ev

    def dma(self, q, fn, reads=(), writes=()):
        if self.halt:
            return None
        deps = self._deps(reads, writes)
        self._wait(q, deps)
        dkey = self.dsem[self.dnext % len(self.dsem)]
        self.dnext += 1
        if self.cnt[dkey] > 0:
            self._wait(q, {dkey: self.cnt[dkey]})
        ins = fn(self.eng[q])
        self.cnt[dkey] += 16
        ins.then_inc(self.sem[dkey], 16)
        ev = (dkey, self.cnt[dkey])
        self._mark(ev, reads, writes)
        self.ninstr += 1
        return ev

    def barrier(self):
        if self.halt:
            return
        deps = {kk: v for kk, v in self.cnt.items() if v > 0}
        for e in self.eng:
            self._wait(e, deps)
        self.bufs = {}


class _Stop(Exception):
    pass


def build(stop=99):
    nc = bass.Bass("TRN2", target_bir_lowering=False)

    def din(name, shape, dtype=F32):
        return nc.dram_tensor(name, shape, dtype, kind="ExternalInput").ap()

    def dout(name, shape):
        return nc.dram_tensor(name, shape, F32, kind="ExternalOutput").ap()

    xp = din("xp", [2048, 1024])
    xs = din("xs", [64, 1024])
    memp = din("memp", [256, 1024])
    s0re = din("s0re", [16, 4096])
    s0im = din("s0im", [16, 4096])
    sconv = din("sconv", [32, 1024])
    ck = din("ck", [2, 16, 256, 1024])
    cv = din("cv", [2, 16, 256, 1024])
    norm_mix = din("norm_mix", [2, 1024])
    norm_mem = din("norm_mem", [2, 1024])
    norm_ffn = din("norm_ffn", [2, 1024])
    norm_final = din("norm_final", [1, 1024])
    a_re = din("ssm_a_re", [64, 64])
    a_im = din("ssm_a_im", [64, 64])
    log_dt = din("ssm_log_dt", [64])
    b_re = din("ssm_b_re", [64, 64, 16])
    b_im = din("ssm_b_im", [64, 64, 16])
    c_re = din("ssm_c_re", [64, 16, 64])
    c_im = din("ssm_c_im", [64, 16, 64])
    ssm_d = din("ssm_d", [8, 128])
    w_glu = din("ssm_w_glu", [1024, 2048])
    conv_w_in = din("conv_w_in", [1024, 3072])
    conv_w = din("conv_w", [24, 128])
    conv_w_out = din("conv_w_out", [1024, 1024])
    mem_w_q = din("mem_w_q", [2, 1024, 1024])
    mem_w_k = din("mem_w_k", [2, 1024, 1024])
    mem_w_v = din("mem_w_v", [2, 1024, 1024])
    mem_w_o = din("mem_w_o", [2, 1024, 1024])
    peer_wq = din("peer_w_query", [2, 1024, 2048])
    peer_k1 = din("peer_key1", [2, 128, 128])
    peer_k2 = din("peer_key2", [2, 128, 128])
    peer_u = din("peer_u", [32768 if stop >= 3 else 128, 1024])
    peer_v = din("peer_v", [32768 if stop >= 3 else 128, 1024])

    o_yp = dout("o_yp", [2048, 1024])
    o_ys = dout("o_ys", [64, 1024])
    o_sre_p = dout("o_sre_p", [32, 128])
    o_sim_p = dout("o_sim_p", [32, 128])
    o_conv_p = dout("o_conv_p", [2, 1024])
    o_mk = dout("o_mk", [512, 1024])
    o_mv = dout("o_mv", [512, 1024])
    o_sre_s = dout("o_sre_s", [16, 4096])
    o_sim_s = dout("o_sim_s", [16, 4096])
    o_conv_s = dout("o_conv_s", [32, 1024])

    with ExitStack() as st:
        k = K(nc, st)

        uid = [0]

        def T(stack, name, shape, dtype):
            uid[0] += 1
            return stack.enter_context(nc.sbuf_tensor("%s_%d" % (name, uid[0]), shape, dtype))

        def PS(stack, name, shape, dtype=F32):
            uid[0] += 1
            return stack.enter_context(nc.psum_tensor("%s_%d" % (name, uid[0]), shape, dtype))

        def TT(e, out, in0, in1, op, r, w):
            return k.op(e, lambda E: E.tensor_tensor(out=out, in0=in0, in1=in1, op=op), r, w)

        def TS(e, out, in0, s1, op0, r, w, s2=None, op1=None):
            if op1 is None:
                return k.op(e, lambda E: E.tensor_scalar(out=out, in0=in0, scalar1=s1, scalar2=None, op0=op0), r, w)
            return k.op(e, lambda E: E.tensor_scalar(out=out, in0=in0, scalar1=s1, scalar2=s2, op0=op0, op1=op1), r, w)

        def STT(e, out, in0, scalar, in1, op0, op1, r, w):
            return k.op(e, lambda E: E.scalar_tensor_tensor(out=out, in0=in0, scalar=scalar, in1=in1, op0=op0, op1=op1), r, w)

        def ACT(out, in_, func, r, w, **kw):
            return k.op("act", lambda E: E.activation(out=out, in_=in_, func=func, **kw), r, w)

        def CP(e, out, in_, r, w):
            if e == "act":
                return k.op("act", lambda E: E.activation(out=out, in_=in_, func=AF.Copy), r, w)
            return k.op(e, lambda E: E.tensor_copy(out=out, in_=in_), r, w)

        def MM(out, lhsT, rhs, start, stop, r, w):
            return k.op("pe", lambda E: E.matmul(out, lhsT=lhsT, rhs=rhs, start=start, stop=stop), r, w)

        def TR(out, in_, ident, r, w):
            return k.op("pe", lambda E: E.transpose(out=out, in_=in_, identity=ident), r, w)

        def MS(e, ap, val, w):
            return k.op(e, lambda E: E.memset(ap, val), (), w)

        def LD(out, in_, w, r=(), q="sp"):
            return k.dma(q, lambda E: E.dma_start(out=out, in_=in_), r, w)

        ldc_n = [0]

        def LDC(out, in_, w, r=()):
            F = out.shape[-1]
            for c0 in range(0, F, 512):
                c1 = min(F, c0 + 512)
                i = ldc_n[0] % 2
                ldc_n[0] += 1
                LD(wstg[i][:, 0:c1 - c0], in_[:, c0:c1], ["wstg%d" % i])
                CP("pool" if i == 0 else "act", out[:, c0:c1], wstg[i][:, 0:c1 - c0], ["wstg%d" % i], w)

        def ckp(v):
            if stop <= v and not k.halt:
                k.barrier()
                k.halt = True

        X = T(st, "X", [128, NT, 1024], F32)
        identf = T(st, "identf", [128, 128], F32)
        identb = T(st, "identb", [128, 128], BF16)
        gbc = T(st, "gbc", [128, 1024], F32)
        junk = T(st, "junk", [128, 1024], BF16)
        htok = [T(st, "htok%d" % i, [128, 1024], BF16) for i in range(2)]
        nsm = T(st, "nsm", [128, 8], F32)
        epsT = T(st, "epsT", [128, 1], F32)
        memT = T(st, "memT", [128, 8, 256], BF16)
        wstg = [T(st, "wstg%d" % i, [128, 512], F32) for i in range(2)]

        MS("pool", identf[:], 1.0, ["identf"])
        k.op("pool", lambda E: E.affine_select(out=identf[:], in_=identf[:], pattern=[[-1, 128]],
                                               compare_op=ALU.is_equal, fill=0.0, base=0,
                                               channel_multiplier=1), ["identf"], ["identf"])
        CP("dve", identb[:], identf[:], ["identf"], ["identb"])
        MS("dve", epsT[:], EPS, ["epsT"])
        MS("pool", X[:, 16, :], 0.0, ["X16"])
        xpv = xp.rearrange("(t p) d -> p t d", p=128)
        for g4 in range(4):
            LD(X[:, 4 * g4:4 * g4 + 4, :], xpv[:, 4 * g4:4 * g4 + 4, :], ["X%d" % t for t in range(4 * g4, 4 * g4 + 4)])
        LD(X[0:64, 16, :], xs, ["X16"])

        def load_gain(vec_ap):
            LD(gbc[:], vec_ap.to_broadcast([128, 1024]), ["gbc"])

        def norm_tile(t, psn, hT_out, hT_names, col0):
            i = t % 2
            ht = htok[i]
            ACT(junk[:], X[:, t, :], AF.Square, ["X%d" % t], ["junk", "nss%d" % i], accum_out=nsm[:, 2 * i:2 * i + 1])
            ACT(nsm[:, 2 * i + 1:2 * i + 2], nsm[:, 2 * i:2 * i + 1], AF.Sqrt, ["nss%d" % i], ["nrs%d" % i],
                scale=1.0 / 1024.0, bias=epsT[:, 0:1])
            k.op("dve", lambda E: E.reciprocal(out=nsm[:, 4 + i:5 + i], in_=nsm[:, 2 * i + 1:2 * i + 2]),
                 ["nrs%d" % i], ["nri%d" % i])
            STT("dve", ht[:], X[:, t, :], nsm[:, 4 + i:5 + i], gbc[:], ALU.mult, ALU.mult,
                ["X%d" % t, "nri%d" % i, "gbc"], ["htok%d" % i])
            if hT_out is None:
                return
            for dt in range(8):
                TR(psn[:, dt * 128:(dt + 1) * 128], ht[:, dt * 128:(dt + 1) * 128], identb[:],
                   ["htok%d" % i, "identb"], ["psn"])
            CP("act", hT_out[:, :, col0:col0 + 128], psn[:].rearrange("p (a b) -> p a b", a=8),
               ["psn"], hT_names)

        try:
          with ExitStack() as ph:
            hT = T(ph, "hT", [128, 8, TC], BF16)
            hTn = ["hT%d" % d for d in range(8)]
            load_gain(norm_mix[0:1, :])
            with ExitStack() as phn:
                psn = PS(phn, "psn", [128, 1024], BF16)
                for t in range(NT):
                    norm_tile(t, psn, hT, hTn, t * 128)
                k.barrier()
            ckp(0.1)

            with ExitStack() as ph2:
                are = T(ph2, "are", [32, 128], F32)
                aim = T(ph2, "aim", [32, 128], F32)
                ldt = T(ph2, "ldt", [32, 2], F32)
                dtb = T(ph2, "dtb", [32, 128], F32)
                qa = [T(ph2, "qa%d" % i, [32, 128], F32) for i in range(10)]
                sp6 = T(ph2, "sp6", [128, 6, 32], F32)
                dsk = T(ph2, "dsk", [128, 8], F32)
                dsk8 = T(ph2, "dsk8", [8, 128], F32)
                pA = PS(ph2, "pA", [128, 1024], F32)
                pB = PS(ph2, "pB", [128, 1024], F32)
                pY = PS(ph2, "pY", [128, 1024], F32)
                LD(are[:], a_re.rearrange("(q two) p -> q (two p)", two=2), ["are"])
                LD(aim[:], a_im.rearrange("(q two) p -> q (two p)", two=2), ["aim"])
                LD(ldt[:], log_dt.rearrange("(q two) -> q two", two=2), ["ldt"])
                LD(dsk8[:], ssm_d, ["dsk8"])
                ACT(ldt[:], ldt[:], AF.Exp, ["ldt"], ["ldt"])
                CP("dve", dtb[:].rearrange("q (two p) -> q two p", two=2),
                   ldt[:, :].unsqueeze(2).to_broadcast([32, 2, 64]), ["ldt"], ["dtb"])
                mag, ang, t1, t2, sina, cosa, lbre, lbim, fre, fim = [x[:] for x in qa]
                N = ["qa%d" % i for i in range(10)]
                TT("dve", t1, are[:], dtb[:], ALU.mult, ["are", "dtb"], [N[2]])
                ACT(mag, t1, AF.Exp, [N[2]], [N[0]])
                TT("dve", ang, aim[:], dtb[:], ALU.mult, ["aim", "dtb"], [N[1]])
                TS("dve", ang, ang, INV2PI, ALU.mult, [N[1]], [N[1]])
                TS("dve", t1, ang, MAGIC, ALU.add, [N[1]], [N[2]], s2=MAGIC, op1=ALU.subtract)
                TT("dve", t1, ang, t1, ALU.subtract, [N[1], N[2]], [N[2]])
                ACT(sina, t1, AF.Sin, [N[2]], [N[4]], scale=SC2PI)
                TS("dve", t2, ang, 0.25, ALU.add, [N[1]], [N[3]])
                TS("dve", t1, t2, MAGIC, ALU.add, [N[3]], [N[2]], s2=MAGIC, op1=ALU.subtract)
                TT("dve", t1, t2, t1, ALU.subtract, [N[3], N[2]], [N[2]])
                ACT(cosa, t1, AF.Sin, [N[2]], [N[5]], scale=SC2PI)
                TT("dve", lbre, mag, cosa, ALU.mult, [N[0], N[5]], [N[6]])
                TT("dve", lbim, mag, sina, ALU.mult, [N[0], N[4]], [N[7]])
                TT("dve", t1, are[:], are[:], ALU.mult, ["are"], [N[2]])
                TT("dve", t2, aim[:], aim[:], ALU.mult, ["aim"], [N[3]])
                TT("dve", t1, t1, t2, ALU.add, [N[2], N[3]], [N[2]])
                k.op("dve", lambda E: E.reciprocal(out=sina, in_=t1), [N[2]], [N[4]])
                TS("dve", cosa, lbre, -1.0, ALU.add, [N[6]], [N[5]])
                TT("dve", t1, cosa, are[:], ALU.mult, [N[5], "are"], [N[2]])
                TT("dve", t2, lbim, aim[:], ALU.mult, [N[7], "aim"], [N[3]])
                TT("dve", t1, t1, t2, ALU.add, [N[2], N[3]], [N[2]])
                TT("dve", fre, t1, sina, ALU.mult, [N[2], N[4]], [N[8]])
                TT("dve", t1, lbim, are[:], ALU.mult, [N[7], "are"], [N[2]])
                TT("dve", t2, cosa, aim[:], ALU.mult, [N[5], "aim"], [N[3]])
                TT("dve", t1, t1, t2, ALU.subtract, [N[2], N[3]], [N[2]])
                TT("dve", fim, t1, sina, ALU.mult, [N[2], N[4]], [N[9]])
                for j, (src, nm) in enumerate([(fre, N[8]), (fim, N[9]), (mag, N[0]), (ang, N[1]), (lbre, N[6]), (lbim, N[7])]):
                    TR(pA[:, j * 32:(j + 1) * 32], src, identf[0:32, 0:32], [nm, "identf"], ["pA"])
                CP("act", sp6[:].rearrange("p a b -> p (a b)"), pA[:, 0:192], ["pA"], ["sp6"])
                TR(pA[:, 256:264], dsk8[:], identf[0:8, 0:8], ["dsk8", "identf"], ["pA"])
                CP("act", dsk[:], pA[:, 256:264], ["pA"], ["dsk"])
                F_RE, F_IM, MAG, FQ, LBRE, LBIM = range(6)
                ckp(0.2)

                LT_re = T(ph2, "LT_re", [128, 32, 128], BF16)
                LT_im = T(ph2, "LT_im", [128, 32, 128], BF16)
                Cre = T(ph2, "Cre", [128, 32, 64], BF16)
                Cimn = T(ph2, "Cimn", [128, 32, 64], BF16)
                MS("pool", Cre[:], 0.0, ["Cre"])
                MS("pool", Cimn[:], 0.0, ["Cimn"])
                s0T_re = T(ph2, "s0T_re", [128, 32, 16], F32)
                s0T_im = T(ph2, "s0T_im", [128, 32, 16], F32)
                with ExitStack() as ph3:
                    bre = T(ph3, "bre", [128, 32, 16], F32)
                    bim = T(ph3, "bim", [128, 32, 16], F32)
                    bt1 = T(ph3, "bt1", [128, 32, 16], F32)
                    bt2 = T(ph3, "bt2", [128, 32, 16], F32)
                    bbr = T(ph3, "bbr", [128, 32, 16], F32)
                    bbi = T(ph3, "bbi", [128, 32, 16], F32)
                    Pre = T(ph3, "Pre", [128, 32, 128], BF16)
                    Pim = Pre
                    psb = PS(ph3, "psb", [128, 1024], BF16)
                    LD(bre[:], b_re.rearrange("(q two) p c -> (two p) q c", two=2), ["bre"])
                    LD(bim[:], b_im.rearrange("(q two) p c -> (two p) q c", two=2), ["bim"])
                    frb = sp6[:, F_RE, :].unsqueeze(2).to_broadcast([128, 32, 16])
                    fib = sp6[:, F_IM, :].unsqueeze(2).to_broadcast([128, 32, 16])
                    TT("dve", bt1[:], bre[:], frb, ALU.mult, ["bre", "sp6"], ["bt1"])
                    TT("dve", bt2[:], bim[:], fib, ALU.mult, ["bim", "sp6"], ["bt2"])
                    TT("dve", bbr[:], bt1[:], bt2[:], ALU.subtract, ["bt1", "bt2"], ["bbr"])
                    TT("dve", bt1[:], bre[:], fib, ALU.mult, ["bre", "sp6"], ["bt1"])
                    TT("dve", bt2[:], bim[:], frb, ALU.mult, ["bim", "sp6"], ["bt2"])
                    TT("dve", bbi[:], bt1[:], bt2[:], ALU.add, ["bt1", "bt2"], ["bbi"])
                    MS("pool", Pre[:], 0.0, ["Pre"])
                    for (Pd, bb, nmP, nmb, LT, nmL) in ((Pre, bbr, "Pre", "bbr", LT_re, "LT_re"), (Pre, bbi, "Pre", "bbi", LT_im, "LT_im")):
                        Pv = Pd[:].rearrange("p (qa qb) n -> p qa qb n", qb=4)
                        bv = bb[:].rearrange("p (qa qb) c -> p qa qb c", qb=4)
                        for qb in range(4):
                            CP("dve", Pv[0:64, :, qb, 32 * qb:32 * qb + 16], bv[0:64, :, qb, :], [nmb], [nmP])
                            CP("dve", Pv[64:128, :, qb, 32 * qb + 16:32 * qb + 32], bv[64:128, :, qb, :], [nmb], [nmP])
                        for b8 in range(4):
                            for j in range(8):
                                q = b8 * 8 + j
                                TR(psb[:, j * 128:(j + 1) * 128], Pd[:, q, :], identb[:], [nmP, "identb"], ["psb"])
                            CP("act", LT[:, b8 * 8:(b8 + 1) * 8, :], psb[:].rearrange("p (a b) -> p a b", a=8), ["psb"], [nmL])
                    k.barrier()
                ckp(0.3)
                with ExitStack() as ph3:
                    Sre = T(ph3, "Sre", [32, 32, 128], F32)
                    MS("pool", Sre[:], 0.0, ["Sre"])
                    for (csrc, Cd, nmC, sc) in ((c_re, Cre, "Cre", 1.0), (c_im, Cimn, "Cimn", -1.0)):
                        cvw = csrc.rearrange("(q two) c p -> two c q p", two=2)
                        LD(Sre[0:16, :, 0:64], cvw[0], ["Sre"])
                        LD(Sre[16:32, :, 64:128], cvw[1], ["Sre"])
                        for q in range(32):
                            TR(pA[:, q * 32:(q + 1) * 32], Sre[:, q, :], identf[0:32, 0:32], ["Sre", "identf"], ["pA"])
                        ACT(Cd[:, :, 32:64], pA[:].rearrange("p (a b) -> p a b", a=32), AF.Copy, ["pA"], [nmC], scale=sc)
                    k.barrier()
                ckp(0.35)
                with ExitStack() as ph3:
                    s0sb = T(ph3, "s0sb", [16, 1024], F32)
                    for (src, dst, nmd) in ((s0re, s0T_re, "s0T_re"), (s0im, s0T_im, "s0T_im")):
                        for b8 in range(4):
                            LD(s0sb[:], src[:, b8 * 1024:(b8 + 1) * 1024], ["s0sb"])
                            for j in range(8):
                                TR(pA[:, j * 16:(j + 1) * 16], s0sb[:, j * 128:(j + 1) * 128], identf[0:16, 0:16],
                                   ["s0sb", "identf"], ["pA"])
                            CP("act", dst[:, b8 * 8:(b8 + 1) * 8, :].rearrange("p a b -> p (a b)"), pA[:, 0:128], ["pA"], [nmd])
                    k.barrier()

                ckp(0.4)
                CH = 256
                NCH = 2048 // CH
                iot = T(ph2, "iot", [128, CH], F32)
                csb = [T(ph2, "cs%d" % i, [128, CH], F32) for i in range(2)]
                snb = [T(ph2, "sn%d" % i, [128, CH], F32) for i in range(2)]
                x2 = T(ph2, "x2", [128, CH], F32)
                bpr = [T(ph2, "bpr%d" % i, [128, CH], F32) for i in range(2)]
                bpi = [T(ph2, "bpi%d" % i, [128, CH], F32) for i in range(2)]
                vr = [T(ph2, "vr%d" % i, [128, CH], F32) for i in range(2)]
                vi = [T(ph2, "vi%d" % i, [128, CH], F32) for i in range(2)]
                zr = [T(ph2, "zr%d" % i, [128, CH], BF16) for i in range(2)]
                zi = [T(ph2, "zi%d" % i, [128, CH], BF16) for i in range(2)]
                u1 = T(ph2, "u1", [128, CH], F32)
                u2 = T(ph2, "u2", [128, CH], F32)
                d1 = T(ph2, "d1", [128, CH], F32)
                d2 = T(ph2, "d2", [128, CH], F32)
                ypre = T(ph2, "ypre", [128, CH], F32)
                tiny = T(ph2, "tiny", [128, 16], F32)
                cinit = T(ph2, "cinit", [128, 2, 2], F32)
                fin_re = T(ph2, "fin_re", [128, 32], F32)
                fin_im = T(ph2, "fin_im", [128, 32], F32)
                fins_re = T(ph2, "fins_re", [128, 32, 16], F32)
                fins_im = T(ph2, "fins_im", [128, 32, 16], F32)
                m0 = T(ph2, "m0", [128, 16, 4], F32)
                decs = T(ph2, "decs", [128, 64], F32)
                sm = [T(ph2, "sm%d" % i, [128, 64], F32) for i in range(6)]
                zsb = [T(ph2, "zsb%d" % i, [128, 64], BF16) for i in range(2)]
                lam = T(ph2, "lam", [128, 2, 16], F32)

                k.op("pool", lambda E: E.iota(iot[:], pattern=[[1, CH]], base=0, channel_multiplier=0,
                                              allow_small_or_imprecise_dtypes=True), (), ["iot"])
                MS("dve", m0[:], 1.0, ["m0"])
                MS("dve", m0[:, :, 0:1], 0.0, ["m0"])

                import os as _os
                QSTOP = int(_os.environ.get("QSTOP", "99"))
                for q in range(32):
                    if q == QSTOP:
                        ckp(0.65)
                    dti = q // 4
                    qq = q % 4
                    pi = q % 2
                    cs, sn = csb[pi], snb[pi]
                    csn, snn = "cs%d" % pi, "sn%d" % pi
                    fq = sp6[:, FQ, q:q + 1]
                    magq = sp6[:, MAG, q:q + 1]
                    TS("pool", sn[:], iot[:], fq, ALU.mult, ["iot", "sp6"], [snn])
                    TS("pool", x2[:], sn[:], MAGIC, ALU.add, [snn], ["x2"], s2=MAGIC, op1=ALU.subtract)
                    TT("pool", sn[:], sn[:], x2[:], ALU.subtract, [snn, "x2"], [snn])
                    ACT(sn[:], sn[:], AF.Sin, [snn], [snn], scale=SC2PI)
                    TS("pool", cs[:], iot[:], fq, ALU.mult, ["iot", "sp6"], [csn], s2=0.25, op1=ALU.add)
                    TS("pool", x2[:], cs[:], MAGIC, ALU.add, [csn], ["x2"], s2=MAGIC, op1=ALU.subtract)
                    TT("pool", cs[:], cs[:], x2[:], ALU.subtract, [csn, "x2"], [csn])
                    ACT(cs[:], cs[:], AF.Sin, [csn], [csn], scale=SC2PI)

                    ckp(0.5)
                    for c in range(NCH):
                        bi = c % 2
                        cols = slice(c * CH, (c + 1) * CH)
                        MM(pA[:, bi * 512:bi * 512 + CH], LT_re[:, q, :], hT[:, dti, cols], True, True,
                           ["LT_re", hTn[dti]], ["pA%d" % bi])
                        MM(pB[:, bi * 512:bi * 512 + CH], LT_im[:, q, :], hT[:, dti, cols], True, True,
                           ["LT_im", hTn[dti]], ["pB%d" % bi])
                        par = pA[:, bi * 512:bi * 512 + CH]
                        pai = pB[:, bi * 512:bi * 512 + CH]
                        TT("dve", d1[:], par, cs[:], ALU.mult, ["pA%d" % bi, csn], ["d1"])
                        TT("dve", d2[:], pai, sn[:], ALU.mult, ["pB%d" % bi, snn], ["d2"])
                        TT("dve", bpr[bi][:], d1[:], d2[:], ALU.add, ["d1", "d2"], ["bpr%d" % bi])
                        TT("dve", d1[:], pai, cs[:], ALU.mult, ["pB%d" % bi, csn], ["d1"])
                        TT("dve", d2[:], par, sn[:], ALU.mult, ["pA%d" % bi, snn], ["d2"])
                        TT("dve", bpi[bi][:], d1[:], d2[:], ALU.subtract, ["d1", "d2"], ["bpi%d" % bi])
                        magb = magq.to_broadcast([128, CH])
                        if c == 0:
                            ir, ii = 0.0, 0.0
                            rd = []
                        else:
                            ir, ii = cinit[:, 0, 0:1], cinit[:, 0, 1:2]
                            rd = ["cinit"]
                        k.op("dve", lambda E: E.tensor_tensor_scan(out=vr[bi][:], data0=magb, data1=bpr[bi][:], initial=ir,
                                                                   op0=ALU.mult, op1=ALU.add),
                             ["sp6", "bpr%d" % bi] + rd, ["vr%d" % bi])
                        k.op("dve", lambda E: E.tensor_tensor_scan(out=vi[bi][:], data0=magb, data1=bpi[bi][:], initial=ii,
                                                                   op0=ALU.mult, op1=ALU.add),
                             ["sp6", "bpi%d" % bi] + rd, ["vi%d" % bi])
                        TT("pool", u1[:], vr[bi][:], cs[:], ALU.mult, ["vr%d" % bi, csn], ["u1"])
                        TT("pool", u2[:], vi[bi][:], sn[:], ALU.mult, ["vi%d" % bi, snn], ["u2"])
                        TT("pool", zr[bi][:], u1[:], u2[:], ALU.subtract, ["u1", "u2"], ["zr%d" % bi])
                        TT("pool", u1[:], vr[bi][:], sn[:], ALU.mult, ["vr%d" % bi, snn], ["u1"])
                        TT("pool", u2[:], vi[bi][:], cs[:], ALU.mult, ["vi%d" % bi, csn], ["u2"])
                        TT("pool", zi[bi][:], u1[:], u2[:], ALU.add, ["u1", "u2"], ["zi%d" % bi])
                        L = slice(CH - 1, CH)
                        TS("dve", tiny[:, 0:1], vi[bi][:, L], sn[:, L], ALU.mult, ["vi%d" % bi, snn], ["tiny"])
                        STT("dve", tiny[:, 1:2], vr[bi][:, L], cs[:, L], tiny[:, 0:1], ALU.mult, ALU.subtract,
                            ["vr%d" % bi, csn, "tiny"], ["tiny"])
                        TS("dve", tiny[:, 2:3], vi[bi][:, L], cs[:, L], ALU.mult, ["vi%d" % bi, csn], ["tiny"])
                        STT("dve", tiny[:, 3:4], vr[bi][:, L], sn[:, L], tiny[:, 2:3], ALU.mult, ALU.add,
                            ["vr%d" % bi, snn, "tiny"], ["tiny"])
                        if c < NCH - 1:
                            TS("dve", tiny[:, 4:5], tiny[:, 3:4], sn[:, 1:2], ALU.mult, ["tiny", snn], ["tiny"])
                            STT("dve", cinit[:, 0, 0:1], tiny[:, 1:2], cs[:, 1:2], tiny[:, 4:5], ALU.mult, ALU.subtract,
                                ["tiny", csn], ["cinit"])
                            TS("dve", tiny[:, 5:6], tiny[:, 3:4], cs[:, 1:2], ALU.mult, ["tiny", csn], ["tiny"])
                            STT("dve", cinit[:, 0, 1:2], tiny[:, 1:2], sn[:, 1:2], tiny[:, 5:6], ALU.mult, ALU.add,
                                ["tiny", snn], ["cinit"])
                        else:
                            CP("dve", fin_re[:, q:q + 1], tiny[:, 1:2], ["tiny"], ["fin_re"])
                            CP("dve", fin_im[:, q:q + 1], tiny[:, 3:4], ["tiny"], ["fin_im"])
                        yo = pY[32 * qq:32 * qq + 32, bi * 512:bi * 512 + CH]
                        if qq % 2 == 0:
                            ymm, csl = yo, slice(32, 64)
                        else:
                            ymm, csl = pY[32 * (qq - 1):32 * (qq + 1), bi * 512:bi * 512 + CH], slice(0, 64)
                        MM(ymm, Cre[:, q, csl], zr[bi][:], True, False, ["Cre", "zr%d" % bi], ["pY%d" % bi])
                        MM(ymm, Cimn[:, q, csl], zi[bi][:], False, True, ["Cimn", "zi%d" % bi], ["pY%d" % bi])
                        rows = slice(32 * qq, 32 * qq + 32)
                        STT("dve", ypre[rows, :], hT[rows, dti, cols], dsk[rows, dti:dti + 1], yo, ALU.mult, ALU.add,
                            [hTn[dti], "dsk", "pY%d" % bi], ["ypre"])
                        ACT(hT[rows, dti, cols], ypre[rows, :], AF.Gelu, ["ypre"], [hTn[dti]])

                    ckp(0.55)
                    scol = slice(2048, 2112)
                    MM(pA[:, 0:64], LT_re[:, q, :], hT[:, dti, scol], True, True, ["LT_re", hTn[dti]], ["pA0"])
                    MM(pB[:, 0:64], LT_im[:, q, :], hT[:, dti, scol], True, True, ["LT_im", hTn[dti]], ["pB0"])
                    v3 = lambda ap: ap.rearrange("p (s t) -> p s t", t=4)
                    cs4 = cs[:, 0:4].unsqueeze(1).to_broadcast([128, 16, 4])
                    sn4 = sn[:, 0:4].unsqueeze(1).to_broadcast([128, 16, 4])
                    par = v3(pA[:, 0:64])
                    pai = v3(pB[:, 0:64])
                    s0, s1, s2_, s3, s4, s5 = [v3(x[:]) for x in sm]
                    TT("dve", s0, par, cs4, ALU.mult, ["pA0", csn], ["sm0"])
                    TT("dve", s1, pai, sn4, ALU.mult, ["pB0", snn], ["sm1"])
                    TT("dve", s2_, s0, s1, ALU.add, ["sm0", "sm1"], ["sm2"])
                    TT("dve", s0, pai, cs4, ALU.mult, ["pB0", csn], ["sm0"])
                    TT("dve", s1, par, sn4, ALU.mult, ["pA0", snn], ["sm1"])
                    TT("dve", s3, s0, s1, ALU.subtract, ["sm0", "sm1"], ["sm3"])
                    lbr = sp6[:, LBRE, q:q + 1]
                    lbi = sp6[:, LBIM, q:q + 1]
                    TS("dve", lam[:, 0, :], s0T_im[:, q, :], lbi, ALU.mult, ["s0T_im", "sp6"], ["lam"])
                    STT("dve", lam[:, 0, :], s0T_re[:, q, :], lbr, lam[:, 0, :], ALU.mult, ALU.subtract,
                        ["s0T_re", "sp6", "lam"], ["lam"])
                    TS("dve", lam[:, 1, :], s0T_re[:, q, :], lbi, ALU.mult, ["s0T_re", "sp6"], ["lam"])
                    STT("dve", lam[:, 1, :], s0T_im[:, q, :], lbr, lam[:, 1, :], ALU.mult, ALU.add,
                        ["s0T_im", "sp6", "lam"], ["lam"])
                    TT("dve", s2_[:, :, 0:1], s2_[:, :, 0:1], lam[:, 0, :].unsqueeze(2), ALU.add, ["sm2", "lam"], ["sm2"])
                    TT("dve", s3[:, :, 0:1], s3[:, :, 0:1], lam[:, 1, :].unsqueeze(2), ALU.add, ["sm3", "lam"], ["sm3"])
                    TS("dve", decs[:], m0[:].rearrange("p s t -> p (s t)"), magq, ALU.mult, ["m0", "sp6"], ["decs"])
                    k.op("dve", lambda E: E.tensor_tensor_scan(out=sm[4][:], data0=decs[:], data1=sm[2][:], initial=0.0,
                                                               op0=ALU.mult, op1=ALU.add), ["decs", "sm2"], ["sm4"])
                    k.op("dve", lambda E: E.tensor_tensor_scan(out=sm[5][:], data0=decs[:], data1=sm[3][:], initial=0.0,
                                                               op0=ALU.mult, op1=ALU.add), ["decs", "sm3"], ["sm5"])
                    TT("dve", s0, s4, cs4, ALU.mult, ["sm4", csn], ["sm0"])
                    TT("dve", s1, s5, sn4, ALU.mult, ["sm5", snn], ["sm1"])
                    TT("dve", s2_, s0, s1, ALU.subtract, ["sm0", "sm1"], ["sm2"])
                    TT("dve", s0, s4, sn4, ALU.mult, ["sm4", snn], ["sm0"])
                    TT("dve", s1, s5, cs4, ALU.mult, ["sm5", csn], ["sm1"])
                    TT("dve", s3, s0, s1, ALU.add, ["sm0", "sm1"], ["sm3"])
                    CP("dve", zsb[0][:], sm[2][:], ["sm2"], ["zsb0"])
                    CP("dve", zsb[1][:], sm[3][:], ["sm3"], ["zsb1"])
                    CP("dve", fins_re[:, q, :].unsqueeze(2), s2_[:, :, 3:4], ["sm2"], ["fins_re"])
                    CP("dve", fins_im[:, q, :].unsqueeze(2), s3[:, :, 3:4], ["sm3"], ["fins_im"])
                    yo = pY[32 * qq:32 * qq + 32, 0:64]
                    if qq % 2 == 0:
                        ymm, csl = yo, slice(32, 64)
                    else:
                        ymm, csl = pY[32 * (qq - 1):32 * (qq + 1), 0:64], slice(0, 64)
                    MM(ymm, Cre[:, q, csl], zsb[0][:], True, False, ["Cre", "zsb0"], ["pY0"])
                    MM(ymm, Cimn[:, q, csl], zsb[1][:], False, True, ["Cimn", "zsb1"], ["pY0"])
                    rows = slice(32 * qq, 32 * qq + 32)
                    STT("dve", ypre[rows, 0:64], hT[rows, dti, scol], dsk[rows, dti:dti + 1], yo, ALU.mult, ALU.add,
                        [hTn[dti], "dsk", "pY0"], ["ypre"])
                    ACT(hT[rows, dti, scol], ypre[rows, 0:64], AF.Gelu, ["ypre"], [hTn[dti]])

                    ckp(0.6)
                ckp(0.7)
                stg = T(ph2, "stg", [32, 128], F32)
                stg2 = T(ph2, "stg2", [16, 1024], F32)
                for (fin, dst, nm) in ((fin_re, o_sre_p, "fin_re"), (fin_im, o_sim_p, "fin_im")):
                    TR(pA[0:32, 0:128], fin[:], identf[:], [nm, "identf"], ["pA0"])
                    CP("act", stg[:], pA[0:32, 0:128], ["pA0"], ["stg"])
                    LD(dst, stg[:], [], ["stg"])
                for (fin, dst, nm) in ((fins_re, o_sre_s, "fins_re"), (fins_im, o_sim_s, "fins_im")):
                    for b8 in range(4):
                        for j in range(8):
                            q = b8 * 8 + j
                            TR(pB[0:16, j * 128:(j + 1) * 128], fin[:, q, :], identf[:], [nm, "identf"], ["pB0"])
                        CP("act", stg2[:], pB[0:16, :], ["pB0"], ["stg2"])
                        LD(dst[:, b8 * 1024:(b8 + 1) * 1024], stg2[:], [], ["stg2"])
                k.barrier()

            ckp(0.8)
            with ExitStack() as ph2:
                wglu = T(ph2, "wglu", [128, 8, 2048], BF16)
                sig = T(ph2, "sig", [128, 1024], F32)
                gtmp = T(ph2, "gtmp", [128, 1024], F32)
                pg = [PS(ph2, "pg%d" % i, [128, 2048], F32) for i in range(2)]
                wv_ = w_glu.rearrange("(dt p) f -> p dt f", p=128)
                for dt in range(8):
                    LDC(wglu[:, dt, :], wv_[:, dt, :], ["wglu"])
                for t in range(NT):
                    pi = t % 2
                    for fc in range(4):
                        for dt in range(8):
                            MM(pg[pi][:, fc * 512:(fc + 1) * 512], hT[:, dt, t * 128:(t + 1) * 128],
                               wglu[:, dt, fc * 512:(fc + 1) * 512], dt == 0, dt == 7,
                               [hTn[dt], "wglu"], ["pg%d" % pi])
                    ACT(sig[:], pg[pi][:, 1024:2048], AF.Sigmoid, ["pg%d" % pi], ["sig"])
                    TT("dve", gtmp[:], pg[pi][:, 0:1024], sig[:], ALU.mult, ["pg%d" % pi, "sig"], ["gtmp"])
                    TT("dve", X[:, t, :], X[:, t, :], gtmp[:], ALU.add, ["X%d" % t, "gtmp"], ["X%d" % t])
                k.barrier()

        except _Stop:
            pass
        k.halt = False

        def attention(l):
            with ExitStack() as ph:
                psn = PS(ph, "psn", [128, 1024], BF16)
                pA = PS(ph, "pA", [128, 1024], F32)
                pB = PS(ph, "pB", [128, 2048], F32)
                psx = PS(ph, "psx", [128, 1024], BF16)
                KT = T(ph, "KT", [128, 8, 256], BF16)
                Vb = T(ph, "Vb", [128, 2, 1024], BF16)
                load_gain(norm_mem[l:l + 1, :])
                with ExitStack() as ph2:
                    wk = T(ph2, "wk", [128, 8, 1024], BF16)
                    wv = T(ph2, "wv", [128, 8, 1024], BF16)
                    kst = [T(ph2, "kst%d" % i, [128, 1024], F32) for i in range(2)]
                    wkv = mem_w_k[l].rearrange("(dt p) f -> p dt f", p=128)
                    wvv = mem_w_v[l].rearrange("(dt p) f -> p dt f", p=128)
                    for dt in range(8):
                        LDC(wk[:, dt, :], wkv[:, dt, :], ["wk"])
                        LDC(wv[:, dt, :], wvv[:, dt, :], ["wv"])
                    if l == 0:
                        memb = T(ph2, "memb", [128, 2, 1024], BF16)
                        for mt in range(2):
                            LDC(memb[:, mt, :], memp[mt * 128:(mt + 1) * 128, :], ["memb"])
                        for mt in range(2):
                            for dt in range(8):
                                TR(psx[:, dt * 128:(dt + 1) * 128], memb[:, mt, dt * 128:(dt + 1) * 128], identb[:],
                                   ["memb", "identb"], ["psx"])
                            CP("act", memT[:, :, mt * 128:(mt + 1) * 128], psx[:].rearrange("p (a b) -> p a b", a=8),
                               ["psx"], ["memT"])
                    ckp(1.25)
                    n = 0
                    for (w_, wn, dst, isv) in ((wk, "wk", o_mk, False), (wv, "wv", o_mv, True)):
                        if isv and _os2.environ.get("NOV"):
                            continue
                        for mt in range(2):
                            ks = kst[n % 2]
                            ksn = "kst%d" % (n % 2)
                            n += 1
                            for fc in range(2):
                                for dt in range(8):
                                    MM(pA[:, fc * 512:(fc + 1) * 512], memT[:, dt, mt * 128:(mt + 1) * 128],
                                       w_[:, dt, fc * 512:(fc + 1) * 512], dt == 0, dt == 7, ["memT", wn], ["pA"])
                            CP("act", ks[:], pA[:], ["pA"], [ksn])
                            if isv:
                                CP("dve", Vb[:, mt, :], ks[:], [ksn], ["Vb"])
                            if NOOUT is None:
                                LD(dst[l * 256 + mt * 128:l * 256 + (mt + 1) * 128, :], ks[:], [], [ksn])
                    ckp(1.3)
                    for ft in range(8):
                        for dt in range(8):
                            MM(pB[:, (ft % 4) * 512:(ft % 4) * 512 + 256], wk[:, dt, ft * 128:(ft + 1) * 128], memT[:, dt, :],
                               dt == 0, dt == 7, ["wk", "memT"], ["pB"])
                        CP("act", KT[:, ft, :], pB[:, (ft % 4) * 512:(ft % 4) * 512 + 256], ["pB"], ["KT"])
                    k.barrier()
                ckp(1.4)

                with ExitStack() as ph2:
                    wq = T(ph2, "wq", [128, 8, 1024], BF16)
                    wo = T(ph2, "wo", [128, 8, 1024], BF16)
                    hTg = T(ph2, "hTg", [128, 8, 512], BF16)
                    qT = T(ph2, "qT", [128, 8, 512], BF16)
                    Pf = T(ph2, "Pf", [128, 4, 256], F32)
                    Pn = T(ph2, "Pn", [128, 4, 256], BF16)
                    PT = T(ph2, "PT", [128, 8, 128], BF16)
                    oT = T(ph2, "oT", [128, 8, 128], BF16)
                    stat = T(ph2, "stat", [128, 16], F32)
                    Kc = T(ph2, "Kc", [128, 2, 1024], BF16)
                    Vc = [T(ph2, "Vc%d" % i, [128, 2, 1024], BF16) for i in range(2)]
                    KTc = [T(ph2, "KTc%d" % i, [128, 8, 256], BF16) for i in range(2)]
                    qTm = [T(ph2, "qTm%d" % i, [128, 8, 128], BF16) for i in range(2)]
                    wqv = mem_w_q[l].rearrange("(dt p) f -> p dt f", p=128)
                    wov = mem_w_o[l].rearrange("(dt p) f -> p dt f", p=128)
                    for dt in range(8):
                        LDC(wq[:, dt, :], wqv[:, dt, :], ["wq"])
                        LDC(wo[:, dt, :], wov[:, dt, :], ["wo"])

                    def softmax_pv_o(t, nheadbanks):
                        hs = 512 if nheadbanks == 4 else 256
                        sview = pB[:, 0:4 * hs].rearrange("p (h m) -> p h m", h=4)[:, :, 0:256]
                        k.op("dve", lambda E: E.tensor_reduce(out=stat[:, 0:4], in_=sview, axis=AX.X, op=ALU.max),
                             ["pB"], ["stat"])
                        TS("dve", stat[:, 4:8], stat[:, 0:4], -1.0, ALU.mult, ["stat"], ["stat"])
                        for h in range(4):
                            ACT(Pf[:, h, :], pB[:, h * hs:h * hs + 256], AF.Exp, ["pB", "stat"], ["Pf", "stat"],
                                bias=stat[:, 4 + h:5 + h], scale=1.0, accum_out=stat[:, 8 + h:9 + h])
                        k.op("dve", lambda E: E.reciprocal(out=stat[:, 12:16], in_=stat[:, 8:12]), ["stat"], ["stat"])
                        TT("dve", Pn[:], Pf[:], stat[:, 12:16].unsqueeze(2).to_broadcast([128, 4, 256]), ALU.mult,
                           ["Pf", "stat"], ["Pn"])
                        for h in range(4):
                            for mt in range(2):
                                TR(psx[:, (h * 2 + mt) * 128:(h * 2 + mt + 1) * 128], Pn[:, h, mt * 128:(mt + 1) * 128],
                                   identb[:], ["Pn", "identb"], ["psx"])
                        CP("act", PT[:], psx[:].rearrange("p (a b) -> p a b", a=8), ["psx"], ["PT"])

                    def oproj(t):
                        CP("act", oT[:], pA[:].rearrange("p (a b) -> p a b", a=8), ["pA"], ["oT"])
                        for fc in range(2):
                            for ft in range(8):
                                MM(pA[:, fc * 512:(fc + 1) * 512], oT[:, ft, :], wo[:, ft, fc * 512:(fc + 1) * 512],
                                   ft == 0, ft == 7, ["oT", "wo"], ["pA"])
                        TT("dve", X[:, t, :], X[:, t, :], pA[:], ALU.add, ["X%d" % t, "pA"], ["X%d" % t])

                    for g in range(4):
                        for j in range(4):
                            norm_tile(4 * g + j, psn, hTg, ["hTg"], j * 128)
                        for ft in range(8):
                            bo = 1024 + (ft % 2) * 512
                            for dt in range(8):
                                MM(pB[:, bo:bo + 512], wq[:, dt, ft * 128:(ft + 1) * 128], hTg[:, dt, :], dt == 0, dt == 7,
                                   ["wq", "hTg"], ["pBq%d" % (ft % 2)])
                            ACT(qT[:, ft, :], pB[:, bo:bo + 512], AF.Copy, ["pBq%d" % (ft % 2)], ["qT"], scale=1.0 / 16.0)
                        for j in range(4):
                            t = 4 * g + j
                            tc_ = slice(j * 128, (j + 1) * 128)
                            for h in range(4):
                                for jj in range(2):
                                    MM(pB[:, h * 256:(h + 1) * 256], qT[:, 2 * h + jj, tc_], KT[:, 2 * h + jj, :],
                                       jj == 0, jj == 1, ["qT", "KT"], ["pB"])
                            if j == 0 and g == 0:
                                ckp(1.45)
                            softmax_pv_o(t, 2)
                            if j == 0 and g == 0:
                                ckp(1.5)
                            for ft in range(8):
                                h = ft // 2
                                for mt in range(2):
                                    MM(pA[:, ft * 128:(ft + 1) * 128], Vb[:, mt, ft * 128:(ft + 1) * 128], PT[:, h * 2 + mt, :],
                                       mt == 0, mt == 1, ["Vb", "PT"], ["pA"])
                            oproj(t)
                            if j == 0 and g == 0:
                                ckp(1.6)
                    ckp(1.7)
                    t = 16
                    norm_tile(t, psn, hTg, ["hTg"], 0)
                    for ft in range(8):
                        for dt in range(8):
                            MM(pA[:, ft * 128:(ft + 1) * 128], wq[:, dt, ft * 128:(ft + 1) * 128], hTg[:, dt, 0:128],
                               dt == 0, dt == 7, ["wq", "hTg"], ["pA"])
                    ACT(qT[:, :, 0:128], pA[:].rearrange("p (a b) -> p a b", a=8), AF.Copy, ["pA"], ["qT"], scale=1.0 / 16.0)
                    for i in range(16):
                        pi = i % 2
                        for mt in range(2):
                            LDC(Kc[:, mt, :], ck[l, i, mt * 128:(mt + 1) * 128, :], ["Kc"])
                        for mt in range(2):
                            for ft in range(8):
                                TR(psx[:, ft * 128:(ft + 1) * 128], Kc[:, mt, ft * 128:(ft + 1) * 128], identb[:],
                                   ["Kc", "identb"], ["psx"])
                            CP("act", KTc[pi][:, :, mt * 128:(mt + 1) * 128], psx[:].rearrange("p (a b) -> p a b", a=8),
                               ["psx"], ["KTc%d" % pi])
                        MS("pool", qTm[pi][:], 0.0, ["qTm%d" % pi])
                        CP("pool", qTm[pi][:, :, 4 * i:4 * i + 4], qT[:, :, 4 * i:4 * i + 4], ["qT"], ["qTm%d" % pi])
                        for h in range(4):
                            for jj in range(2):
                                MM(pB[:, h * 512:h * 512 + 256], qTm[pi][:, 2 * h + jj, :], KTc[pi][:, 2 * h + jj, :],
                                   (i == 0 and jj == 0), (i == 15 and jj == 1), ["qTm%d" % pi, "KTc%d" % pi], ["pB"])
                    ckp(1.8)
                    softmax_pv_o(t, 4)
                    MS("dve", oT[:], 0.0, ["oT"])
                    for i in range(16):
                        pi = i % 2
                        for mt in range(2):
                            LDC(Vc[pi][:, mt, :], cv[l, i, mt * 128:(mt + 1) * 128, :], ["Vc%d" % pi])
                        for ft in range(8):
                            h = ft // 2
                            for mt in range(2):
                                MM(pA[:, ft * 128 + 4 * i:ft * 128 + 4 * i + 4], Vc[pi][:, mt, ft * 128:(ft + 1) * 128],
                                   PT[:, h * 2 + mt, 4 * i:4 * i + 4], mt == 0, mt == 1, ["Vc%d" % pi, "PT"], ["pA"])
                    ckp(1.9)
                    pAv = pA[:].rearrange("p (a b) -> p a b", a=8)
                    CP("act", oT[:, :, 0:64], pAv[:, :, 0:64], ["pA"], ["oT"])
                    for fc in range(2):
                        for ft in range(8):
                            MM(pA[:, fc * 512:(fc + 1) * 512], oT[:, ft, :], wo[:, ft, fc * 512:(fc + 1) * 512],
                               ft == 0, ft == 7, ["oT", "wo"], ["pA"])
                    TT("dve", X[0:64, t, :], X[0:64, t, :], pA[0:64, :], ALU.add, ["X16", "pA"], ["X16"])
                    k.barrier()

        def peer(l):
            NU = 6
            NV = 5
            with ExitStack() as ph:
                psn = PS(ph, "psn", [128, 1024], BF16)
                pQ = PS(ph, "pQ", [128, 2048], F32)
                pO = PS(ph, "pO", [128, 1024], F32)
                wqp = T(ph, "wqp", [128, 8, 2048], BF16)
                keyT = T(ph, "keyT", [128, 2, 128], BF16)
                kld = T(ph, "kld", [128, 128], F32)
                hTt = T(ph, "hTt", [128, 8, 128], BF16)
                qpT = T(ph, "qpT", [128, 16, 128], BF16)
                V12 = T(ph, "V12", [128, 16, 16], F32)
                I12u = T(ph, "I12u", [128, 16, 16], U32)
                I12f = T(ph, "I12f", [128, 16, 16], F32)
                work = T(ph, "work", [128, 128], F32)
                cand = T(ph, "cand", [128, 8, 256], F32)
                work2 = T(ph, "work2", [128, 256], F32)
                scv = T(ph, "scv", [128, 8, 16], F32)
                ciu = T(ph, "ciu", [128, 8, 16], U32)
                cff = T(ph, "cff", [128, 128], F32)
                caf = T(ph, "caf", [128, 128], F32)
                cbf = T(ph, "cbf", [128, 128], F32)
                io16 = T(ph, "io16", [128, 16], F32)
                eq = cand[:].rearrange("p h (k a) -> p (h k) a", a=16)
                isel = T(ph, "isel", [128, 2, 128], F32)
                ef = T(ph, "ef", [128, 128], F32)
                eu = T(ph, "eu", [128, 128], U32)
                gsm = T(ph, "gsm", [128, 8, 16], F32)
                gst = T(ph, "gst", [128, 16], F32)
                actv = T(ph, "actv", [128, 128], F32)
                wgt = T(ph, "wgt", [128, 128], F32)
                Ub = [T(ph, "Ub%d" % i, [128, 1024], F32) for i in range(NU)]
                Vg = [T(ph, "Vg%d" % i, [128, 1024], F32) for i in range(NV)]
                Vh = [T(ph, "Vh%d" % i, [128, 1024], BF16) for i in range(2)]
                dg = [T(ph, "dg%d" % i, [128, 128], BF16) for i in range(4)]
                load_gain(norm_ffn[l:l + 1, :])
                wv_ = peer_wq[l].rearrange("(dt p) f -> p dt f", p=128)
                for dt in range(8):
                    LDC(wqp[:, dt, :], wv_[:, dt, :], ["wqp"])
                for j, ksrc in enumerate((peer_k1, peer_k2)):
                    LD(kld[:], ksrc[l], ["kld"])
                    TR(pO[:, 0:128], kld[:], identf[:], ["kld", "identf"], ["pO"])
                    CP("act", keyT[:, j, :], pO[:, 0:128], ["pO"], ["keyT"])
                k.op("pool", lambda E: E.iota(io16[:], pattern=[[1, 16]], base=0, channel_multiplier=0,
                                              allow_small_or_imprecise_dtypes=True), (), ["io16"])
                eu2 = [T(ph, "eu2_%d" % i, [128, 128], U32) for i in range(2)]

                def partA(t):
                        norm_tile(t, psn, hTt, ["hTt"], 0)
                        for ft in range(16):
                            for dt in range(8):
                                MM(pQ[:, ft * 128:(ft + 1) * 128], wqp[:, dt, ft * 128:(ft + 1) * 128], hTt[:, dt, :],
                                   dt == 0, dt == 7, ["wqp", "hTt"], ["pQ"])
                        CP("act", qpT[:], pQ[:].rearrange("p (a b) -> p a b", a=16), ["pQ"], ["qpT"])
                        for blk in range(16):
                            MM(pQ[:, blk * 128:(blk + 1) * 128], qpT[:, blk, :], keyT[:, blk % 2, :], True, True,
                               ["qpT", "keyT"], ["pQ"])
                        for blk in range(16):
                            sb = pQ[:, blk * 128:(blk + 1) * 128]
                            k.op("dve", lambda E: E.max(out=V12[:, blk, 0:8], in_=sb), ["pQ"], ["V12"])
                            k.op("dve", lambda E: E.max_index(out=I12u[:, blk, 0:8], in_max=V12[:, blk, 0:8], in_values=sb),
                                 ["pQ", "V12"], ["I12u"])
                            k.op("dve", lambda E: E.match_replace(out=work[:], in_to_replace=V12[:, blk, 0:8], in_values=sb,
                                                                  imm_value=-1e30), ["pQ", "V12"], ["work"])
                            k.op("dve", lambda E: E.max(out=V12[:, blk, 8:16], in_=work[:]), ["work"], ["V12"])
                            k.op("dve", lambda E: E.max_index(out=I12u[:, blk, 8:16], in_max=V12[:, blk, 8:16], in_values=work[:]),
                                 ["work", "V12"], ["I12u"])
                        CP("dve", I12f[:], I12u[:], ["I12u"], ["I12f"])
                        V4 = V12[:].rearrange("p (h two) a -> p h two a", two=2)
                        I4 = I12f[:].rearrange("p (h two) a -> p h two a", two=2)
                        TT("dve", cand[:].rearrange("p h (a b) -> p h a b", a=16),
                           V4[:, :, 0, :].unsqueeze(3).to_broadcast([128, 8, 16, 16]),
                           V4[:, :, 1, :].unsqueeze(2).to_broadcast([128, 8, 16, 16]), ALU.add, ["V12"], ["cand"])
                        for h in range(8):
                            k.op("dve", lambda E: E.max(out=scv[:, h, 0:8], in_=cand[:, h, :]), ["cand"], ["scv"])
                            k.op("dve", lambda E: E.max_index(out=ciu[:, h, 0:8], in_max=scv[:, h, 0:8], in_values=cand[:, h, :]),
                                 ["cand", "scv"], ["ciu"])
                            k.op("dve", lambda E: E.match_replace(out=work2[:], in_to_replace=scv[:, h, 0:8], in_values=cand[:, h, :],
                                                                  imm_value=-1e30), ["cand", "scv"], ["work2"])
                            k.op("dve", lambda E: E.max(out=scv[:, h, 8:16], in_=work2[:]), ["work2"], ["scv"])
                            k.op("dve", lambda E: E.max_index(out=ciu[:, h, 8:16], in_max=scv[:, h, 8:16], in_values=work2[:]),
                                 ["work2", "scv"], ["ciu"])
                        civ = ciu[:].rearrange("p h k -> p (h k)")
                        CP("dve", cff[:], civ, ["ciu"], ["cff"])
                        TS("dve", caf[:], cff[:], 1.0 / 16.0, ALU.mult, ["cff"], ["caf"], s2=-0.46875, op1=ALU.add)
                        TS("dve", caf[:], caf[:], MAGIC, ALU.add, ["caf"], ["caf"], s2=MAGIC, op1=ALU.subtract)
                        STT("dve", cbf[:], caf[:], -16.0, cff[:], ALU.mult, ALU.add, ["caf", "cff"], ["cbf"])
                        io_b = io16[:, :].unsqueeze(1).to_broadcast([128, 128, 16])
                        for half, cf, cfn in ((0, caf, "caf"), (1, cbf, "cbf")):
                            TT("dve", eq, io_b, cf[:, :].unsqueeze(2).to_broadcast([128, 128, 16]), ALU.is_equal,
                               ["io16", cfn], ["cand"])
                            eq4 = cand[:].rearrange("p h (k a) -> p h k a", a=16)
                            TT("dve", eq4, eq4, I4[:, :, half, :].unsqueeze(2).to_broadcast([128, 8, 16, 16]), ALU.mult,
                               ["cand", "I12f"], ["cand"])
                            k.op("dve", lambda E: E.tensor_reduce(out=isel[:, half, :], in_=eq, axis=AX.X, op=ALU.add),
                                 ["cand"], ["isel"])
                        STT("dve", ef[:], isel[:, 0, :], 128.0, isel[:, 1, :], ALU.mult, ALU.add, ["isel"], ["ef"])
                        TS("dve", ef[:], ef[:], float(16384 * l), ALU.add, ["ef"], ["ef"])
                        CP("dve", eu2[t % 2][:], ef[:], ["ef"], ["eu%d" % (t % 2)])

                def partB(t):
                        TT("dve", gsm[:], scv[:], scv[:, :, 0:1].to_broadcast([128, 8, 16]), ALU.subtract, ["scv"], ["gsm"])
                        ACT(gsm[:], gsm[:], AF.Exp, ["gsm"], ["gsm"])
                        k.op("dve", lambda E: E.tensor_reduce(out=gst[:, 0:8], in_=gsm[:], axis=AX.X, op=ALU.add), ["gsm"], ["gst"])
                        k.op("dve", lambda E: E.reciprocal(out=gst[:, 8:16], in_=gst[:, 0:8]), ["gst"], ["gst"])
                        TT("dve", gsm[:], gsm[:], gst[:, 8:16].unsqueeze(2).to_broadcast([128, 8, 16]), ALU.mult,
                           ["gsm", "gst"], ["gsm"])

                def partU(t):
                        ht = htok[t % 2]
                        htn = "htok%d" % (t % 2)
                        eu = eu2[t % 2]
                        eun = "eu%d" % (t % 2)
                        for s in range(128):
                            ub = Ub[s % NU]
                            ubn = "Ub%d" % (s % NU)
                            k.dma("pool", lambda E: E.indirect_dma_start(
                                out=ub[:], out_offset=None, in_=peer_u,
                                in_offset=bass.IndirectOffsetOnAxis(ap=eu[:, s:s + 1], axis=0)), [eun], [ubn])
                            k.op("dve", lambda E: E.scalar_tensor_tensor(
                                out=junk[:], in0=ub[:], scalar=1.0, in1=ht[:], op0=ALU.mult, op1=ALU.mult,
                                accum_out=actv[:, s:s + 1]), [ubn, htn], ["junk", "actv"])
                        ACT(actv[:], actv[:], AF.Gelu, ["actv"], ["actv"])
                        TT("dve", wgt[:], actv[:], gsm[:].rearrange("p h k -> p (h k)"), ALU.mult, ["actv", "gsm"], ["wgt"])

                def partV(t):
                        eu = eu2[t % 2]
                        eun = "eu%d" % (t % 2)
                        for s in range(128):
                            vg = Vg[s % NV]
                            vgn = "Vg%d" % (s % NV)
                            dgi = dg[s % 4]
                            dgn = "dg%d" % (s % 4)
                            k.dma("pool", lambda E: E.indirect_dma_start(
                                out=vg[:], out_offset=None, in_=peer_v,
                                in_offset=bass.IndirectOffsetOnAxis(ap=eu[:, s:s + 1], axis=0)), [eun], [vgn])
                            ACT(dgi[:], identf[:], AF.Copy, ["identf", "wgt"], [dgn], scale=wgt[:, s:s + 1])
                            vh = Vh[s % 2]
                            vhn = "Vh%d" % (s % 2)
                            CP("act", vh[:], vg[:], [vgn], [vhn])
                            for fc in range(2):
                                MM(pO[:, fc * 512:(fc + 1) * 512], dgi[:], vh[:, fc * 512:(fc + 1) * 512], s == 0, s == 127,
                                   [dgn, vhn], ["pO"])
                        TT("dve", X[:, t, :], X[:, t, :], pO[:], ALU.add, ["X%d" % t, "pO"], ["X%d" % t])

                partA(0)
                partB(0)
                for t in range(NT):
                    partU(t)
                    if t + 1 < NT:
                        partA(t + 1)
                    partV(t)
                    if t + 1 < NT:
                        partB(t + 1)
                k.barrier()

        def conv_mixer():
            GW = 256
            with ExitStack() as ph:
                psn = PS(ph, "psn", [128, 1024], BF16)
                pC = [PS(ph, "pC%d" % i, [128, 512], F32) for i in range(3)]
                pX = PS(ph, "pX", [128, 1024], F32)
                win = T(ph, "win", [128, 8, 3072], BF16)
                wout = T(ph, "wout", [128, 8, 1024], BF16)
                cw24 = T(ph, "cw24", [24, 128], F32)
                cw = T(ph, "cw", [128, 24], F32)
                hTg = T(ph, "hTg", [128, 8, GW], BF16)
                uT = T(ph, "uT", [128, 8, GW], BF16)
                vp = [T(ph, "vp%d" % i, [128, GW + 2], F32) for i in range(2)]
                hvs = T(ph, "hvs", [128, GW], F32)
                acc = T(ph, "acc", [128, GW], F32)
                cbuf = T(ph, "cbuf", [128, 8, 2], F32)
                scb = T(ph, "scb", [128, 8, 32], F32)
                cso = T(ph, "cso", [128, 8, 32], F32)
                vps = T(ph, "vps", [128, 16, 6], F32)
                accs = T(ph, "accs", [128, 16, 4], F32)
                ostg = T(ph, "ostg", [32, 1024], F32)
                load_gain(norm_mix[1:2, :])
                wiv = conv_w_in.rearrange("(dt p) f -> p dt f", p=128)
                wov = conv_w_out.rearrange("(dt p) f -> p dt f", p=128)
                for dt in range(8):
                    LDC(win[:, dt, :], wiv[:, dt, :], ["win"])
                    LDC(wout[:, dt, :], wov[:, dt, :], ["wout"])
                LD(cw24[:], conv_w, ["cw24"])
                TR(pX[:, 0:24], cw24[:], identf[0:24, 0:24], ["cw24", "identf"], ["pX"])
                CP("act", cw[:], pX[:, 0:24], ["pX"], ["cw"])
                LD(ostg[:], sconv, ["ostg"])
                for ft in range(8):
                    TR(pX[:, 32 * ft:32 * ft + 32], ostg[:, ft * 128:(ft + 1) * 128], identf[0:32, 0:32],
                       ["ostg", "identf"], ["pX"])
                CP("act", scb[:].rearrange("p a b -> p (a b)"), pX[:, 0:256], ["pX"], ["scb"])
                MS("dve", cbuf[:], 0.0, ["cbuf"])

                def wout_tile(t, c0):
                    for fc in range(2):
                        for ft in range(8):
                            MM(pX[:, fc * 512:(fc + 1) * 512], uT[:, ft, c0:c0 + 128], wout[:, ft, fc * 512:(fc + 1) * 512],
                               ft == 0, ft == 7, ["uT", "wout"], ["pX"])
                    TT("dve", X[:, t, :], X[:, t, :], pX[:], ALU.add, ["X%d" % t, "pX"], ["X%d" % t])

                NG = 2048 // GW
                TPG = GW // 128
                for g in range(NG + 1):
                    sample = (g == NG)
                    ncol = 128 if sample else GW
                    if sample:
                        norm_tile(16, psn, hTg, ["hTg"], 0)
                    else:
                        for j in range(TPG):
                            norm_tile(TPG * g + j, psn, hTg, ["hTg"], j * 128)
                    for ft in range(8):
                        for j3, off in enumerate((0, 1024, 2048)):
                            for dt in range(8):
                                MM(pC[j3][:, 0:ncol], win[:, dt, off + ft * 128:off + (ft + 1) * 128], hTg[:, dt, 0:ncol],
                                   dt == 0, dt == 7, ["win", "hTg"], ["pC%d" % j3])
                        w0 = cw[:, 0 * 8 + ft:0 * 8 + ft + 1]
                        w1 = cw[:, 1 * 8 + ft:1 * 8 + ft + 1]
                        w2 = cw[:, 2 * 8 + ft:2 * 8 + ft + 1]
                        if not sample:
                            vpp = vp[ft % 2]
                            vpn = "vp%d" % (ft % 2)
                            CP("act", hvs[:], pC[2][:, 0:GW], ["pC2"], ["hvs"])
                            CP("dve", vpp[:, 0:2], cbuf[:, ft, :], ["cbuf"], [vpn])
                            TT("dve", vpp[:, 2:GW + 2], pC[1][:, 0:GW], hvs[:], ALU.mult, ["pC1", "hvs"], [vpn])
                            CP("dve", cbuf[:, ft, :], vpp[:, GW:GW + 2], [vpn], ["cbuf"])
                            TS("dve", acc[:], vpp[:, 0:GW], w0, ALU.mult, [vpn, "cw"], ["acc"])
                            STT("dve", acc[:], vpp[:, 1:GW + 1], w1, acc[:], ALU.mult, ALU.add, [vpn, "cw", "acc"], ["acc"])
                            STT("dve", acc[:], vpp[:, 2:GW + 2], w2, acc[:], ALU.mult, ALU.add, [vpn, "cw", "acc"], ["acc"])
                            TT("dve", uT[:, ft, :], pC[0][:, 0:GW], acc[:], ALU.mult, ["pC0", "acc"], ["uT"])
                        else:
                            CP("act", hvs[:, 0:64], pC[2][:, 0:64], ["pC2"], ["hvs"])
                            CP("dve", vps[:, :, 0:2], scb[:, ft, :].rearrange("p (s r) -> p s r", r=2), ["scb"], ["vps"])
                            TT("dve", vps[:, :, 2:6], pC[1][:, 0:64].rearrange("p (s t) -> p s t", t=4),
                               hvs[:, 0:64].rearrange("p (s t) -> p s t", t=4), ALU.mult, ["pC1", "hvs"], ["vps"])
                            CP("dve", cso[:, ft, :].rearrange("p (s r) -> p s r", r=2), vps[:, :, 4:6], ["vps"], ["cso"])
                            TS("dve", accs[:], vps[:, :, 0:4], w0, ALU.mult, ["vps", "cw"], ["accs"])
                            STT("dve", accs[:], vps[:, :, 1:5], w1, accs[:], ALU.mult, ALU.add, ["vps", "cw", "accs"], ["accs"])
                            STT("dve", accs[:], vps[:, :, 2:6], w2, accs[:], ALU.mult, ALU.add, ["vps", "cw", "accs"], ["accs"])
                            MS("dve", uT[:, ft, 0:128], 0.0, ["uT"])
                            TT("dve", uT[:, ft, 0:64].rearrange("p (s t) -> p s t", t=4),
                               pC[0][:, 0:64].rearrange("p (s t) -> p s t", t=4), accs[:], ALU.mult, ["pC0", "accs"], ["uT"])
                    if not sample:
                        for j in range(TPG):
                            wout_tile(TPG * g + j, j * 128)
                    else:
                        wout_tile(16, 0)
                    if g == NG - 1:
                        for ft in range(8):
                            TR(pX[0:2, ft * 128:(ft + 1) * 128], cbuf[:, ft, :], identf[:], ["cbuf", "identf"], ["pX"])
                        CP("act", ostg[0:2, :], pX[0:2, :], ["pX"], ["ostg"])
                        LD(o_conv_p, ostg[0:2, :], [], ["ostg"])
                for ft in range(8):
                    TR(pX[0:32, ft * 128:(ft + 1) * 128], cso[:, ft, :], identf[:], ["cso", "identf"], ["pX"])
                CP("act", ostg[:], pX[0:32, :], ["pX"], ["ostg"])
                LD(o_conv_s, ostg[:], [], ["ostg"])
                k.barrier()

        if stop >= 1.2:
            attention(0)
            k.halt = False
        if stop >= 3:
            peer(0)
        if stop >= 4:
            conv_mixer()
        if stop >= 5:
            attention(1)
        if stop >= 6:
            peer(1)

        with ExitStack() as ph:
            yo = [T(ph, "yo%d" % i, [128, 1024], F32) for i in range(2)]
            load_gain(norm_final[0:1, :])
            for t in range(NT):
                i = t % 2
                ACT(junk[:], X[:, t, :], AF.Square, ["X%d" % t], ["junk", "nss%d" % i], accum_out=nsm[:, 2 * i:2 * i + 1])
                ACT(nsm[:, 2 * i + 1:2 * i + 2], nsm[:, 2 * i:2 * i + 1], AF.Sqrt, ["nss%d" % i], ["nrs%d" % i],
                    scale=1.0 / 1024.0, bias=epsT[:, 0:1])
                k.op("dve", lambda E: E.reciprocal(out=nsm[:, 4 + i:5 + i], in_=nsm[:, 2 * i + 1:2 * i + 2]),
                     ["nrs%d" % i], ["nri%d" % i])
                STT("dve", yo[i][:], X[:, t, :], nsm[:, 4 + i:5 + i], gbc[:], ALU.mult, ALU.mult,
                    ["X%d" % t, "nri%d" % i, "gbc"], ["yo%d" % i])
                if t < 16:
                    LD(o_yp[t * 128:(t + 1) * 128, :], yo[i][:], [], ["yo%d" % i])
                else:
                    LD(o_ys, yo[i][0:64, :], [], ["yo%d" % i])
            k.barrier()
        print("ninstr", k.ninstr)
    return nc


_NC = None
_STOP = 99


def kernel(**inp):
    global _NC
    if _NC is None:
        _NC = build(_STOP)
    nc = _NC
    f = lambda a: np.ascontiguousarray(np.asarray(a, dtype=np.float32))
    common = {
        "norm_mix": f(inp["norm_mix"]), "norm_mem": f(inp["norm_mem"]), "norm_ffn": f(inp["norm_ffn"]),
        "norm_final": f(inp["norm_final"]).reshape(1, 1024),
        "ssm_a_re": f(inp["ssm_a_re"])[0], "ssm_a_im": f(inp["ssm_a_im"])[0], "ssm_log_dt": f(inp["ssm_log_dt"])[0],
        "ssm_b_re": f(inp["ssm_b_re"])[0], "ssm_b_im": f(inp["ssm_b_im"])[0],
        "ssm_c_re": f(inp["ssm_c_re"])[0], "ssm_c_im": f(inp["ssm_c_im"])[0],
        "ssm_d": f(inp["ssm_d"]).reshape(8, 128), "ssm_w_glu": f(inp["ssm_w_glu"])[0],
        "conv_w_in": f(inp["conv_w_in"])[0], "conv_w": f(inp["conv_w"]).reshape(24, 128),
        "conv_w_out": f(inp["conv_w_out"])[0],
        "mem_w_q": f(inp["mem_w_q"]), "mem_w_k": f(inp["mem_w_k"]), "mem_w_v": f(inp["mem_w_v"]),
        "mem_w_o": f(inp["mem_w_o"]), "peer_w_query": f(inp["peer_w_query"]),
        "peer_key1": f(inp["peer_key1"]), "peer_key2": f(inp["peer_key2"]),
        "peer_u": f(inp["peer_u"]).reshape(32768, 1024)[:(32768 if _STOP >= 3 else 128)],
        "peer_v": f(inp["peer_v"]).reshape(32768, 1024)[:(32768 if _STOP >= 3 else 128)],
    }
    x_prompt = f(inp["x_prompt"]); x_sample = f(inp["x_sample"]); mem_prompt = f(inp["mem_prompt"])
    sre = f(inp["state_ssm_re"]); sim = f(inp["state_ssm_im"]); sconv = f(inp["state_conv"])
    ckk = f(inp["cache_mem_k"]); cvv = f(inp["cache_mem_v"])
    in_maps = []
    for c in range(NCORES):
        s = slice(16 * c, 16 * c + 16)
        m = dict(common)
        m["xp"] = x_prompt[c]
        m["xs"] = np.ascontiguousarray(x_sample[s].reshape(64, 1024))
        m["memp"] = mem_prompt[c]
        m["s0re"] = np.ascontiguousarray(sre[0, s].reshape(16, 4096))
        m["s0im"] = np.ascontiguousarray(sim[0, s].reshape(16, 4096))
        m["sconv"] = np.ascontiguousarray(sconv[0, s].reshape(32, 1024))
        m["ck"] = np.ascontiguousarray(ckk[:, s].reshape(2, 16, 256, 1024))
        m["cv"] = np.ascontiguousarray(cvv[:, s].reshape(2, 16, 256, 1024))
        in_maps.append(m)
    res = run_bass_kernel_spmd(nc, in_maps, core_ids=list(range(NCORES)))
    R = res.results
    y_prompt = np.stack([R[c]["o_yp"] for c in range(NCORES)]).reshape(8, 2048, 1024)
    y_sample = np.concatenate([R[c]["o_ys"].reshape(16, 4, 1024) for c in range(NCORES)], axis=0)
    ssm_re_p = np.stack([R[c]["o_sre_p"].reshape(64, 64) for c in range(NCORES)])[None]
    ssm_im_p = np.stack([R[c]["o_sim_p"].reshape(64, 64) for c in range(NCORES)])[None]
    conv_p = np.stack([R[c]["o_conv_p"] for c in range(NCORES)])[None]
    mk = np.stack([R[c]["o_mk"].reshape(2, 256, 1024) for c in range(NCORES)], axis=1).reshape(2, 8, 256, 4, 256)
    mv = np.stack([R[c]["o_mv"].reshape(2, 256, 1024) for c in range(NCORES)], axis=1).reshape(2, 8, 256, 4, 256)
    ssm_re_s = np.concatenate([R[c]["o_sre_s"].reshape(16, 64, 64) for c in range(NCORES)], axis=0)[None]
    ssm_im_s = np.concatenate([R[c]["o_sim_s"].reshape(16, 64, 64) for c in range(NCORES)], axis=0)[None]
    conv_s = np.concatenate([R[c]["o_conv_s"].reshape(16, 2, 1024) for c in range(NCORES)], axis=0)[None]
    out = (y_prompt, y_sample, ssm_re_p, ssm_im_p, conv_p, mk, mv, ssm_re_s, ssm_im_s, conv_s)
    return tuple(np.ascontiguousarray(o, dtype=np.float32) for o in out)
```
